# Optimizing a Trainium2 kernel written in Bass

```python
import math
import jax, jax.numpy as jnp
from jax import lax
import numpy as np

D_MODEL = 2048
BATCH = 16
SEQ = 256
DEPTH = 2
DEC_BATCH = 8
DEC_SEQ = 1024
PAST_LEN = 512

GRID_W = 64
HEAD_DIM = 128
ATT_WIDTH = D_MODEL // 2
N_ATT_HEADS = ATT_WIDTH // HEAD_DIM
N_KV_HEADS = N_ATT_HEADS // 4
GQA_GROUP = N_ATT_HEADS // N_KV_HEADS
KV_WIDTH = N_KV_HEADS * HEAD_DIM
RET_WIDTH = D_MODEL // 4
N_RET_HEADS = RET_WIDTH // HEAD_DIM
HYENA_WIDTH = D_MODEL // 4
HYENA_ORDER = 2
MIX_WIDTH = ATT_WIDTH + RET_WIDTH + HYENA_WIDTH
SPLIT_SIZES = (ATT_WIDTH, KV_WIDTH, KV_WIDTH, RET_WIDTH, RET_WIDTH, RET_WIDTH, RET_WIDTH,
               (HYENA_ORDER + 1) * HYENA_WIDTH)
IN_WIDTH = ATT_WIDTH + 2 * KV_WIDTH + 4 * RET_WIDTH + (HYENA_ORDER + 1) * HYENA_WIDTH
D_FF = 5632
Q_BLOCK = 128
RET_CHUNK = 128
ROPE_THETA = 10000.0
FILTER_BANDS = 16
FILTER_EMB = 1 + 2 * FILTER_BANDS
FILTER_HID = 64
HYENA_DECAY_TARGET = 1e-2
HYENA_DECAY_PCT_MIN = 0.3
HYENA_DECAY_PCT_MAX = 1.5
DEEPNORM_ALPHA = (2 * DEPTH) ** 0.25
DEEPNORM_BETA = (8 * DEPTH) ** -0.25
EPS = 1e-6

kernel_name = 'hybrid_dit_attn_retention_hyena_step'


def _split_points(sizes):
    pts, acc = [], 0
    for s in sizes[:-1]:
        acc += s
        pts.append(acc)
    return pts


def layer_norm(x, w, b):
    xf = x.astype(jnp.float32)
    xc = xf - jnp.mean(xf, -1, keepdims=True)
    var = jnp.mean(xc * xc, -1, keepdims=True)
    return (xc * lax.rsqrt(var + EPS) * w.astype(jnp.float32) + b.astype(jnp.float32)).astype(x.dtype)


def head_norm(x):
    xf = x.astype(jnp.float32)
    xc = xf - jnp.mean(xf, -1, keepdims=True)
    var = jnp.mean(xc * xc, -1, keepdims=True)
    return (xc * lax.rsqrt(var + EPS)).astype(x.dtype)


def rms_norm(x, w):
    xf = x.astype(jnp.float32)
    return (xf * lax.rsqrt(jnp.mean(xf * xf, -1, keepdims=True) + EPS) * w.astype(jnp.float32)).astype(x.dtype)


def axial_rope_tables(n_tokens):
    rows = n_tokens // GRID_W
    row = jnp.repeat(jnp.arange(rows, dtype=jnp.float32), GRID_W)
    col = jnp.tile(jnp.arange(GRID_W, dtype=jnp.float32), rows)
    n_freq = HEAD_DIM // 4
    inv_freq = ROPE_THETA ** (-jnp.arange(n_freq, dtype=jnp.float32) / n_freq)
    ang = jnp.concatenate([row[:, None] * inv_freq[None], col[:, None] * inv_freq[None]], -1)
    return jnp.cos(ang), jnp.sin(ang)


def apply_rope(x, cos, sin):
    x1, x2 = jnp.split(x, 2, axis=-1)
    c = cos.astype(x.dtype)
    s = sin.astype(x.dtype)
    return jnp.concatenate([x1 * c - x2 * s, x1 * s + x2 * c], -1)


def centred_dwconv3(x, w):
    xp = jnp.pad(x, ((0, 0), (1, 1), (0, 0)))
    return xp[:, :-2] * w[0] + xp[:, 1:-1] * w[1] + xp[:, 2:] * w[2]


def block_attention(q, k, v):
    B, KV, G, Lq, hd = q.shape
    nb = Lq // Q_BLOCK
    qb = q.reshape(B, KV, G, nb, Q_BLOCK, hd).transpose(3, 0, 1, 2, 4, 5)
    scale = hd ** -0.5

    def one_block(qi):
        s = jnp.einsum('bkgqd,bksd->bkgqs', qi, k).astype(jnp.float32) * scale
        p = jax.nn.softmax(s, axis=-1).astype(v.dtype)
        return jnp.einsum('bkgqs,bksd->bkgqd', p, v)

    out = lax.map(one_block, qb)
    return out.transpose(1, 2, 3, 0, 4, 5).reshape(B, KV, G, Lq, hd)


def retention_scan(q, k, v, log_gamma, s0):
    B, H, L, d = q.shape
    C = RET_CHUNK
    n = L // C
    idx = jnp.arange(C, dtype=jnp.float32)
    lg = log_gamma[:, None]
    decay_q = jnp.exp(lg * (idx + 1.0))
    decay_k = jnp.exp(lg * (C - 1.0 - idx))
    diff = idx[:, None] - idx[None, :]
    dmat = jnp.where(diff >= 0, jnp.exp(lg[:, :, None] * jnp.maximum(diff, 0.0)), 0.0)
    chunk_decay = jnp.exp(log_gamma * C)
    qc = q.reshape(B, H, n, C, d)
    kc = k.reshape(B, H, n, C, d)
    vc = v.reshape(B, H, n, C, d)
    inner = jnp.einsum('bhnid,bhnjd->bhnij', qc, kc) * dmat[None, :, None].astype(q.dtype)
    inner_out = jnp.einsum('bhnij,bhnje->bhnie', inner, vc)
    kv = jnp.einsum('bhnjd,bhnje->bhnde', kc * decay_k[None, :, None, :, None].astype(q.dtype), vc)
    kv = kv.astype(jnp.float32).transpose(2, 0, 1, 3, 4)

    def step(s, kv_n):
        return chunk_decay[None, :, None, None] * s + kv_n, s

    s_final, s_prev = lax.scan(step, s0.astype(jnp.float32), kv)
    s_prev = s_prev.transpose(1, 2, 0, 3, 4)
    cross = jnp.einsum('bhnid,bhnde->bhnie', qc.astype(jnp.float32) * decay_q[None, :, None, :, None], s_prev)
    out = (inner_out.astype(jnp.float32) + cross).reshape(B, H, L, d)
    return out.astype(q.dtype), s_final.astype(q.dtype)


def hyena_filters(L, w1, b1, freq, w2, b2, w3):
    pos = jnp.arange(L, dtype=jnp.float32)
    t = pos / max(L - 1, 1)
    w = 2.0 * math.pi * pos / L
    f = jnp.linspace(1e-4, FILTER_BANDS - 1, FILTER_BANDS, dtype=jnp.float32)
    ang = w[:, None] * f[None, :]
    z = jnp.concatenate([t[:, None], jnp.cos(ang), -jnp.sin(ang)], -1).astype(w1.dtype)
    hdn = jnp.sin(freq * (z @ w1 + b1))
    hdn = jnp.sin(freq * (hdn @ w2 + b2))
    h = hdn @ w3
    deltas = jnp.linspace(math.log(HYENA_DECAY_TARGET) / HYENA_DECAY_PCT_MIN,
                          math.log(HYENA_DECAY_TARGET) / HYENA_DECAY_PCT_MAX, HYENA_WIDTH, dtype=jnp.float32)
    decay = jnp.exp(-t[:, None] * jnp.abs(deltas)[None, :]).astype(h.dtype)
    return h.reshape(L, HYENA_ORDER, 2, HYENA_WIDTH) * decay[:, None, None, :]


def centred_long_conv(u, h_fwd, h_bwd):
    L = u.shape[1]
    taps = jnp.concatenate([h_fwd, h_bwd[::-1]], 0)
    U = jnp.fft.rfft(u.astype(jnp.float32), n=2 * L, axis=1)
    K = jnp.fft.rfft(taps.astype(jnp.float32), n=2 * L, axis=0)
    y = jnp.fft.irfft(U * K[None], n=2 * L, axis=1)[:, :L]
    return y.astype(u.dtype)


def mixing(h, lp, rope, ctx_k, ctx_v, ret_init):
    B, L, _ = h.shape
    proj = h @ lp['w_in']
    q, k, v, rq, rk, rv, rg, hy = jnp.split(proj, _split_points(SPLIT_SIZES), axis=-1)

    q = rms_norm(q.reshape(B, L, N_KV_HEADS, GQA_GROUP, HEAD_DIM).transpose(0, 2, 3, 1, 4), lp['q_norm'])
    k = rms_norm(k.reshape(B, L, N_KV_HEADS, HEAD_DIM).transpose(0, 2, 1, 3), lp['k_norm'])
    v = v.reshape(B, L, N_KV_HEADS, HEAD_DIM).transpose(0, 2, 1, 3)
    own_k, own_v = k, v
    if rope is not None:
        cos, sin = rope
        q = apply_rope(q, cos, sin)
        k = apply_rope(k, cos, sin)
    if ctx_k is not None:
        k = jnp.concatenate([ctx_k, k], axis=2)
        v = jnp.concatenate([ctx_v, v], axis=2)
    att = block_attention(q, k, v).transpose(0, 3, 1, 2, 4).reshape(B, L, ATT_WIDTH)

    def heads(t):
        return t.reshape(B, L, N_RET_HEADS, HEAD_DIM).transpose(0, 2, 1, 3)
    rq_, rk_, rv_ = heads(rq), heads(rk) * (HEAD_DIM ** -0.5), heads(rv)
    log_gamma = jax.nn.log_sigmoid(lp['ret_decay'].astype(jnp.float32))
    o_f, s_f = retention_scan(rq_, rk_, rv_, log_gamma[0], ret_init[:, 0])
    o_b, s_b = retention_scan(rq_[:, :, ::-1], rk_[:, :, ::-1], rv_[:, :, ::-1], log_gamma[1], ret_init[:, 1])
    ret = head_norm(o_f + o_b[:, :, ::-1]).transpose(0, 2, 1, 3).reshape(B, L, RET_WIDTH)
    ret = jax.nn.silu(rg) * ret

    hy = centred_dwconv3(hy, lp['hy_conv'])
    parts = jnp.split(hy, HYENA_ORDER + 1, axis=-1)
    z = parts[0]
    filt = hyena_filters(L, lp['hf_w1'], lp['hf_b1'], lp['hf_freq'], lp['hf_w2'], lp['hf_b2'], lp['hf_w3'])
    for o in range(HYENA_ORDER):
        z = parts[o + 1] * (centred_long_conv(z, filt[:, o, 0], filt[:, o, 1]) + z * lp['hy_bias'][o])

    out = jnp.concatenate([att, ret, z], -1) @ lp['w_out']
    return out, (own_k, own_v, jnp.stack([s_f, s_b], axis=1))


def conv_ffn(h, lp):
    up = centred_dwconv3(h @ lp['w_up'], lp['ffn_conv'])
    g, u = jnp.split(up, 2, axis=-1)
    return (jax.nn.silu(g) * u) @ lp['w_down']


def trunk_layer(x, mod, lp, rope, ctx_k, ctx_v, ret_init):
    sh1, sc1, g1, sh2, sc2, g2 = jnp.split(mod, 6, axis=-1)
    mix, ctx_state = mixing(x * (1 + sc1) + sh1, lp, rope, ctx_k, ctx_v, ret_init)
    x = layer_norm(DEEPNORM_ALPHA * x + g1 * mix, lp['ln1_w'], lp['ln1_b'])
    ffn = conv_ffn(x * (1 + sc2) + sh2, lp)
    x = layer_norm(DEEPNORM_ALPHA * x + g2 * ffn, lp['ln2_w'], lp['ln2_b'])
    return x, ctx_state


def setup_inputs(seed: int = 0) -> dict:
    key = jax.random.key(seed)
    ks = iter(jax.random.split(key, 40))

    def nrm(shape, std):
        return std * jax.random.normal(next(ks), shape, jnp.float32)

    D = D_MODEL
    x_prompt = nrm((BATCH, SEQ, D), 1.0)
    x_sample = nrm((DEC_BATCH, DEC_SEQ, D), 1.0)
    cache_k = nrm((DEC_BATCH, DEPTH, N_KV_HEADS, PAST_LEN, HEAD_DIM), 1.0)
    cache_v = nrm((DEC_BATCH, DEPTH, N_KV_HEADS, PAST_LEN, HEAD_DIM), 1.0)
    state_ret = nrm((DEC_BATCH, DEPTH, 2, N_RET_HEADS, HEAD_DIM, HEAD_DIM), 1.0)
    c = nrm((DEC_BATCH, D), 1.0)
    c_ctx = nrm((D,), 1.0)
    w_ada = nrm((DEPTH, D, 6 * D), 0.5 * D ** -0.5)
    b_ada = nrm((DEPTH, 6 * D), 0.01)
    w_in = nrm((DEPTH, D, IN_WIDTH), D ** -0.5)
    q_norm = 1.0 + nrm((DEPTH, HEAD_DIM), 0.02)
    k_norm = 1.0 + nrm((DEPTH, HEAD_DIM), 0.02)
    ret_base = jnp.log(2.0 ** (5.0 + jnp.arange(N_RET_HEADS, dtype=jnp.float32)) - 1.0)
    ret_decay = ret_base[None, None, :] + nrm((DEPTH, 2, N_RET_HEADS), 0.1)
    hy_conv = nrm((DEPTH, 3, (HYENA_ORDER + 1) * HYENA_WIDTH), 0.5)
    hf_w1 = nrm((DEPTH, FILTER_EMB, FILTER_HID), FILTER_EMB ** -0.5)
    hf_b1 = nrm((DEPTH, FILTER_HID), 0.1)
    hf_freq = 1.0 + nrm((DEPTH, FILTER_HID), 0.02)
    hf_w2 = nrm((DEPTH, FILTER_HID, FILTER_HID), FILTER_HID ** -0.5)
    hf_b2 = nrm((DEPTH, FILTER_HID), 0.1)
    hf_w3 = nrm((DEPTH, FILTER_HID, HYENA_ORDER * 2 * HYENA_WIDTH), 0.1 * FILTER_HID ** -0.5)
    hy_bias = nrm((DEPTH, HYENA_ORDER, HYENA_WIDTH), 0.5)
    w_out = nrm((DEPTH, MIX_WIDTH, D), DEEPNORM_BETA * MIX_WIDTH ** -0.5)
    ln1_w = 1.0 + nrm((DEPTH, D), 0.02)
    ln1_b = nrm((DEPTH, D), 0.01)
    w_up = nrm((DEPTH, D, 2 * D_FF), D ** -0.5)
    ffn_conv = nrm((DEPTH, 3, 2 * D_FF), 0.5)
    w_down = nrm((DEPTH, D_FF, D), DEEPNORM_BETA * D_FF ** -0.5)
    ln2_w = 1.0 + nrm((DEPTH, D), 0.02)
    ln2_b = nrm((DEPTH, D), 0.01)
    return {'x_prompt': x_prompt, 'x_sample': x_sample, 'cache_k': cache_k, 'cache_v': cache_v,
            'state_ret': state_ret, 'c': c, 'c_ctx': c_ctx, 'w_ada': w_ada, 'b_ada': b_ada,
            'w_in': w_in, 'q_norm': q_norm, 'k_norm': k_norm, 'ret_decay': ret_decay,
            'hy_conv': hy_conv, 'hf_w1': hf_w1, 'hf_b1': hf_b1, 'hf_freq': hf_freq, 'hf_w2': hf_w2,
            'hf_b2': hf_b2, 'hf_w3': hf_w3, 'hy_bias': hy_bias, 'w_out': w_out, 'ln1_w': ln1_w,
            'ln1_b': ln1_b, 'w_up': w_up, 'ffn_conv': ffn_conv, 'w_down': w_down,
            'ln2_w': ln2_w, 'ln2_b': ln2_b}


def reference(x_prompt, x_sample, cache_k, cache_v, state_ret, c, c_ctx, w_ada, b_ada, w_in,
              q_norm, k_norm, ret_decay, hy_conv, hf_w1, hf_b1, hf_freq, hf_w2, hf_b2, hf_w3,
              hy_bias, w_out, ln1_w, ln1_b, w_up, ffn_conv, w_down, ln2_w, ln2_b):
    rope = axial_rope_tables(x_sample.shape[1])
    ret_zero = jnp.zeros((x_prompt.shape[0], 2, N_RET_HEADS, HEAD_DIM, HEAD_DIM), x_prompt.dtype)
    y_p, y_s = x_prompt, x_sample
    ks_out, vs_out, ss_out = [], [], []
    for l in range(DEPTH):
        lp = {'w_in': w_in[l], 'q_norm': q_norm[l], 'k_norm': k_norm[l], 'ret_decay': ret_decay[l],
              'hy_conv': hy_conv[l], 'hf_w1': hf_w1[l], 'hf_b1': hf_b1[l], 'hf_freq': hf_freq[l],
              'hf_w2': hf_w2[l], 'hf_b2': hf_b2[l], 'hf_w3': hf_w3[l], 'hy_bias': hy_bias[l],
              'w_out': w_out[l], 'ln1_w': ln1_w[l], 'ln1_b': ln1_b[l], 'w_up': w_up[l],
              'ffn_conv': ffn_conv[l], 'w_down': w_down[l], 'ln2_w': ln2_w[l], 'ln2_b': ln2_b[l]}
        mod_ctx = (jax.nn.silu(c_ctx) @ w_ada[l] + b_ada[l])[None, None, :]
        mod_lat = (jax.nn.silu(c) @ w_ada[l] + b_ada[l])[:, None, :]
        y_p, (k_l, v_l, s_l) = trunk_layer(y_p, mod_ctx, lp, None, None, None, ret_zero)
        ks_out.append(k_l)
        vs_out.append(v_l)
        ss_out.append(s_l)
        y_s, _ = trunk_layer(y_s, mod_lat, lp, rope, cache_k[:, l], cache_v[:, l], state_ret[:, l])
    new_cache_k = jnp.stack(ks_out, axis=1)
    new_cache_v = jnp.stack(vs_out, axis=1)
    new_state_ret = jnp.stack(ss_out, axis=1)
    return (y_p, y_s, new_cache_k, new_cache_v, new_state_ret)
```

```python
import math
from contextlib import ExitStack
import numpy as np
import ml_dtypes
import concourse.bass as bass
import concourse.mybir as mybir
from concourse.bass_utils import run_bass_kernel_spmd

F32 = mybir.dt.float32
BF16 = mybir.dt.bfloat16
AF = mybir.ActivationFunctionType
ALU = mybir.AluOpType

NCORES = 8
D = 2048
KC = 16
DEPTH = 2
LAT = 1024
CTX = 256
T = 1536
NT = 12
SEGS = ((0, 1024, 0), (1024, 1280, 1), (1280, 1536, 1))
HD = 128
INW = 5120
DFF = 5632
FC = 44
PAST = 512
EPS = 1e-6
ALPHA = (2 * DEPTH) ** 0.25
GRID_W = 64
PI = math.pi


class Buf:
    __slots__ = ("name", "w", "r", "excl")

    def __init__(self, name, excl=False):
        self.name = name
        self.w = None
        self.r = {}
        self.excl = excl


class Eng:
    def __init__(self, name, h, sem, is_pe=False):
        self.name = name
        self.h = h
        self.sem = sem
        self.key = name
        self.count = 0
        self.seen = {}
        self.pending = False
        self.is_pe = is_pe
        self.slots = []
        self.slot_i = 0
        self.st_slots = []
        self.st_i = 0


class MK:
    def __init__(self, nc, es):
        self.nc = nc
        self.sems = {}
        self.engs = {}
        for name, h in (("pe", nc.tensor), ("act", nc.scalar), ("dve", nc.vector),
                        ("pool", nc.gpsimd), ("sp", nc.sync)):
            sem = es.enter_context(nc.semaphore("s_" + name))
            e = Eng(name, h, sem, is_pe=(name == "pe"))
            self.engs[name] = e
            self.sems[name] = sem
        for qn, nslots in (("sp", 16), ("pool", 8), ("act", 2)):
            e = self.engs[qn]
            for i in range(nslots):
                key = "d_%s%d" % (qn, i)
                sem = es.enter_context(nc.semaphore(key))
                self.sems[key] = sem
                e.slots.append([key, sem, 0])
            for i in range(nslots if qn == "sp" else 0):
                key = "s_%s%d" % (qn, i)
                sem = es.enter_context(nc.semaphore(key))
                self.sems[key] = sem
                e.st_slots.append([key, sem, 0])
        self.pe = self.engs["pe"]
        self.act = self.engs["act"]
        self.dve = self.engs["dve"]
        self.pool = self.engs["pool"]
        self.sp = self.engs["sp"]
        self.n_instr = 0

    def _wait(self, E, key, val):
        if E.seen.get(key, 0) >= val:
            return
        E.h.wait_ge(self.sems[key], val)
        E.seen[key] = val

    def _sync(self, E, reads, writes):
        need = {}
        for b in reads:
            if b.w is not None:
                k, v = b.w
                if v > need.get(k, 0):
                    need[k] = v
            if b.excl:
                for k, v in b.r.items():
                    if k != E.key and v > need.get(k, 0):
                        need[k] = v
        for b in writes:
            if b.w is not None:
                k, v = b.w
                if not (k == E.key):
                    if v > need.get(k, 0):
                        need[k] = v
            for k, v in b.r.items():
                if k == E.key:
                    continue
                if v > need.get(k, 0):
                    need[k] = v
        for k, v in need.items():
            if k == E.key and E.is_pe:
                continue
            self._wait(E, k, v)

    def op(self, E, fn, reads=(), writes=(), inc=True):
        self._sync(E, reads, writes)
        ins = fn(E.h)
        val = E.count + 1
        if inc:
            ins.then_inc(E.sem, 1)
            E.count = val
            E.pending = False
        else:
            E.pending = True
        for b in reads:
            if val > b.r.get(E.key, 0):
                b.r[E.key] = val
        for b in writes:
            b.w = (E.key, val)
            b.r = {}
        self.n_instr += 1
        return ins

    def dma(self, E, out, in_, reads=(), writes=(), **kw):
        is_store = str(out.space).upper().find("DRAM") >= 0
        if is_store and E.st_slots:
            slot = E.st_slots[E.st_i % len(E.st_slots)]
            E.st_i += 1
        else:
            slot = E.slots[E.slot_i % len(E.slots)]
            E.slot_i += 1
        self._wait(E, slot[0], slot[2])
        self._sync(E, reads, writes)
        ins = E.h.dma_start(out=out, in_=in_, **kw)
        ins.then_inc(slot[1], 16)
        slot[2] += 16
        val = slot[2]
        for b in reads:
            if val > b.r.get(slot[0], 0):
                b.r[slot[0]] = val
        for b in writes:
            b.w = (slot[0], val)
            b.r = {}
        self.n_instr += 1
        return ins

    def barrier(self):
        targets = {}
        for e in self.engs.values():
            assert not e.pending, "unsignalled instruction pending on " + e.name
            targets[e.key] = e.count
            for s in e.slots + e.st_slots:
                targets[s[0]] = s[2]
        for e in self.engs.values():
            for k, v in targets.items():
                if v > 0:
                    self._wait(e, k, v)

    def finish(self):
        self.barrier()


class Prog:
    def __init__(self, debug=False, stop_after=None):
        self.debug = debug
        self.stop_after = stop_after
        self.nc = bass.Bass("TRN2", target_bir_lowering=False)
        self.bg_hook = None
        self.in_names = []
        self.out_names = []
        self.dbg_names = []

    def din(self, name, shape, dt=F32):
        self.in_names.append(name)
        return self.nc.dram_tensor(name, list(shape), dt, kind="ExternalInput").ap()

    def dout(self, name, shape, dt=F32):
        self.out_names.append(name)
        return self.nc.dram_tensor(name, list(shape), dt, kind="ExternalOutput").ap()

    def dscr(self, name, shape, dt=F32):
        if self.debug:
            self.dbg_names.append(name)
            return self.nc.dram_tensor(name, list(shape), dt, kind="ExternalOutput").ap()
        return self.nc.dram_tensor(name, list(shape), dt, kind="Internal").ap()

    def build(self):
        nc = self.nc
        with ExitStack() as es:
            self.es = es
            mk = self.mk = MK(nc, es)
            self.declare()
            self.ps = []
            self.psb = []
            for i in range(8):
                t = es.enter_context(nc.psum_tensor("ps%d" % i, [128, 512], F32))
                self.ps.append(t)
                self.psb.append(Buf("ps%d" % i, excl=True))
            self.load_consts()
            self.phase_ada()
            done = self.stop_after == "ada"
            if done:
                self.ada_flush()
                mk.dma(mk.sp, self.mod_d, self.modT[:], writes=[self.bufs["mod_d"]])
            for l in range(DEPTH):
                if done:
                    break
                self.phase_in(l)
                if self.stop_after == "in%d" % l:
                    break
                self.phase_att(l)
                if self.stop_after == "att%d" % l:
                    break
                self.phase_ret(l)
                if self.stop_after == "ret%d" % l:
                    break
                self.phase_hy(l)
                if self.stop_after == "hy%d" % l:
                    break
                self.phase_out(l)
                self.phase_ln(l, 0)
                if self.stop_after == "ln1_%d" % l:
                    break
                self.phase_ffn(l)
                self.phase_ln(l, 1)
            mk.finish()
        return nc

    def declare(self):
        self.xT0 = self.din("xT0", [D, T])
        self.cT = self.din("cT", [128, KC, 2])
        self.w_ada = self.din("w_ada", [DEPTH, D, 6 * D])
        self.w_in = self.din("w_in", [DEPTH, D, INW])
        self.w_out = self.din("w_out", [DEPTH, D, D])
        self.w_up = self.din("w_up", [DEPTH, D, 2 * DFF])
        self.w_down = self.din("w_down", [DEPTH, DFF, D])
        self.vecs = self.din("vecs", [128, NVEC])
        self.cmats = self.din("cmats", [128, NCM * 128])
        self.rope = self.din("rope", [128, 2, LAT])
        self.cache_kT = self.din("cache_kT", [DEPTH, 2, 128, PAST])
        self.cache_v = self.din("cache_v", [DEPTH, 2, PAST, 128])
        self.state_ret = self.din("state_ret", [DEPTH, 2, 4, 128, 128])
        self.yT = self.dout("yT", [D, T])
        self.nck = self.dout("nck", [2, DEPTH, 2, CTX, HD])
        self.ncv = self.dout("ncv", [2, DEPTH, 2, CTX, HD])
        self.nsr = self.dout("nsr", [2, DEPTH, 2, 4, HD, HD])
        self.mod_d = self.dscr("mod_d", [128, DEPTH, 96, 2])
        self.xT_d = self.dscr("xT_d", [D, T])
        self.hT_d = self.dscr("hT_d", [D, T], BF16)
        self.qT_d = self.dscr("qT_d", [8, 128, T], BF16)
        self.kT_d = self.dscr("kT_d", [2, 128, T], BF16)
        self.v_d = self.dscr("v_d", [T, 256], BF16)
        self.rqT_d = self.dscr("rqT_d", [4, 128, T], BF16)
        self.rkT_d = self.dscr("rkT_d", [4, 128, T], BF16)
        self.rk_d = self.dscr("rk_d", [T, 512], BF16)
        self.rv_d = self.dscr("rv_d", [T, 512], BF16)
        self.rg_d = self.dscr("rg_d", [T, 512])
        self.hyT_d = self.dscr("hyT_d", [1536, T])
        self.catT_d = self.dscr("catT_d", [D, T], BF16)
        self.rT_d = self.dscr("rT_d", [D, T])
        self.actT_d = self.dscr("actT_d", [DFF, T], BF16)
        self.bufs = {}
        self.declare_hy()
        for n in ("mod_d", "xT_d", "hT_d", "qT_d", "kT_d", "v_d", "rqT_d", "rkT_d", "rk_d", "rv_d",
                  "rg_d", "hyT_d", "catT_d", "rT_d", "actT_d", "yT", "nck", "ncv", "nsr"):
            self.bufs[n] = Buf(n)

    def sb(self, es, name, shape, dt=F32):
        self._uid = getattr(self, "_uid", 0) + 1
        name = "%s_u%d" % (name, self._uid)
        t = es.enter_context(self.nc.sbuf_tensor(name, list(shape), dt))
        return t, Buf(name)

    def load_consts(self):
        mk = self.mk
        es = self.es
        self.vec_sb, self.vec_b = self.sb(es, "vec_sb", [128, NVEC])
        self.cm_sb, self.cm_b = self.sb(es, "cm_sb", [128, NCM * 128])
        self.modT, self.modT_b = self.sb(es, "modT", [128, DEPTH, 96, 2])
        self.ones_bf, self.ones_bf_b = self.sb(es, "ones_bf", [128, 128], BF16)
        mk.dma(mk.sp, self.vec_sb[:], self.vecs, writes=[self.vec_b])
        mk.dma(mk.sp, self.cm_sb[:], self.cmats, writes=[self.cm_b])
        mk.op(mk.dve, lambda e: e.memset(self.ones_bf[:], 1.0), writes=[self.ones_bf_b])
        self.oneshd_bf, self.oneshd_bf_b = self.sb(es, "oneshd_bf", [128, 128], BF16)
        mk.op(mk.dve, lambda e: e.memset(self.oneshd_bf[:], 1.0 / HD), writes=[self.oneshd_bf_b])
        self.onesd_bf, self.onesd_bf_b = self.sb(es, "onesd_bf", [128, 128], BF16)
        mk.op(mk.dve, lambda e: e.memset(self.onesd_bf[:], 1.0 / D), writes=[self.onesd_bf_b])

    def load_kq(self, dst, nk, src, src_b, nq=4):
        mk = self.mk
        bufs = []
        per = nk // nq
        for q in range(nq):
            b = Buf("kq%d" % q)
            mk.dma(mk.sp, dst[:, q * per:(q + 1) * per, :],
                   src[q * per * 128:(q + 1) * per * 128, :].rearrange("(kc p) t -> p kc t", p=128),
                   reads=[src_b], writes=[b])
            bufs.extend([b] * per)
        return bufs

    def vec(self, off, n=1):
        return self.vec_sb[:, off:off + n]

    def cmat(self, i):
        return self.cm_sb[:, i * 128:(i + 1) * 128]


V_QKN = 0
V_HYC = V_QKN + 4
V_FCV = V_HYC + 72
V_LNP = V_FCV + 528
V_HYB = V_LNP + 128
V_RETD = V_HYB + 16
V_HF = V_RETD + 16
V_JJ = V_HF + 6
V_BADA = V_JJ + 2
V_EPS = V_BADA + 192
NVEC = V_EPS + 4

CM_ID, CM_ONES_HD, CM_PSW, CM_ONES_D, CM_DPOS, CM_MF, CM_DNEG, CM_MB, CM_IP1, CM_CMI = range(10)
NCM = 10


def _ada_tile(self, l, nt, w, w_b, row, row_b):
    mk = self.mk
    pe, act, dve, pool = mk.pe, mk.act, mk.dve, mk.pool
    ident = self.cmat(CM_ID)
    ps7, ps7_b = self.ps[7], self.psb[7]
    sT, sT_b = self.siluT, self.siluT_b
    src = self.w_ada[l].rearrange("(kc p) n -> p kc n", p=128)[:, :, nt * 512:(nt + 1) * 512]
    mk.dma(pool, w[:], src, writes=[w_b])
    for j in range(4):
        for k in range(KC):
            mk.op(pe, lambda e, k=k, j=j: e.matmul(ps7[:, j * 2:(j + 1) * 2], w[:, k, j * 128:(j + 1) * 128], sT[:, k, :],
                                                   start=(k == 0), stop=(k == KC - 1)),
                  reads=[sT_b, w_b], writes=[ps7_b], inc=(k == KC - 1 and j == 3))
    which = nt // 4
    c0 = nt * 4
    bada = self.vec(V_BADA + l * 96 + c0, 4)
    plus = 1.0 if which in (1, 4) else 0.0
    mk.op(dve, lambda e: e.scalar_tensor_tensor(
        out=self.modT[:, l, c0:c0 + 4, :], in0=ps7[:, 0:8].rearrange("p (c j) -> p c j", j=2), scalar=plus,
        in1=bada.unsqueeze(2).broadcast_to([128, 4, 2]), op0=ALU.add, op1=ALU.add),
        reads=[ps7_b, self.vec_b], writes=[self.modb(l, which)])


Prog.ada_tile = _ada_tile


def _ada_bufs(self, es, n=2):
    self._ada_w = [self.sb(es, "adaw%d" % i, [128, KC, 512], BF16) for i in range(n)]
    self._ada_row = [self.sb(es, "modrow%d" % i, [2, 512]) for i in range(n)]


def _ada_bg(self, n=1):
    for _ in range(n):
        if not self.ada_pending:
            return
        l, nt = self.ada_pending.pop(0)
        i = self._ada_i = getattr(self, "_ada_i", 0) + 1
        w, w_b = self._ada_w[i % len(self._ada_w)]
        row, row_b = self._ada_row[i % len(self._ada_row)]
        self.ada_tile(l, nt, w, w_b, row, row_b)


def _ada_flush(self):
    if not self.ada_pending:
        return
    with ExitStack() as es:
        self.ada_bufs(es)
        self.ada_bg(len(self.ada_pending))
        self.mk.barrier()


Prog.ada_bufs = _ada_bufs
Prog.ada_bg = _ada_bg
Prog.ada_flush = _ada_flush


def _phase_ada(self):
    mk = self.mk
    act, sp = mk.act, mk.sp
    self.siluT, self.siluT_b = self.sb(self.es, "siluT", [128, KC, 2], BF16)
    self.ada_pending = [(0, nt) for nt in range(24)] + [(1, nt) for nt in range(24)]
    for l in range(DEPTH):
        self.ret_tables(l)
    with ExitStack() as es:
        cT_sb, cT_b = self.sb(es, "cT_sb", [128, KC, 2])
        self.ada_bufs(es, n=1)
        m0_load, m0_compute = self.phase_mod0(es)
        mk.dma(sp, cT_sb[:], self.cT, writes=[cT_b])
        mk.op(act, lambda e: e.activation(out=self.siluT[:], in_=cT_sb[:], func=AF.Silu),
              reads=[cT_b], writes=[self.siluT_b])
        left = {"n": 8, "k": 0, "x": 8}

        def hook():
            if left["n"] > 0:
                left["n"] -= 1
                self.ada_bg(1)
                if left["n"] == 0:
                    m0_load(0)
                    m0_load(1)
            elif left["k"] < KC:
                k = left["k"]
                left["k"] += 1
                m0_load(k + 2)
                m0_compute(k)
            elif left["x"] > 0:
                left["x"] -= 1
                self.ada_bg(1)

        hook()
        hook()
        self.bg_hook = hook
        self.hy_filters()
        self.bg_hook = None
        while left["n"] > 0 or left["k"] < KC:
            hook()
        mk.barrier()


Prog.phase_ada = _phase_ada


def _modb(self, l, which):
    if not hasattr(self, "_modbufs"):
        self._modbufs = {}
    key = (l, which)
    if key not in self._modbufs:
        self._modbufs[key] = Buf("mod_%d_%d" % key)
    return self._modbufs[key]


Prog.modb = _modb


def _mod(self, l, which, k, cond):
    return self.modT[:, l, which * 16 + k, cond:cond + 1]


Prog.mod = _mod


def _phase_mod0(self, es):
    mk = self.mk
    act, sp = mk.act, mk.sp
    xin = [self.sb(es, "m0x%d" % i, [128, T]) for i in range(3)]
    ho = [self.sb(es, "m0h%d" % i, [128, T], BF16) for i in range(2)]

    def load(k):
        if k < KC:
            x, x_b = xin[k % 3]
            mk.dma(sp, x[:], self.xT0[k * 128:(k + 1) * 128, :], writes=[x_b])

    def compute(k):
        x, x_b = xin[k % 3]
        h, h_b = ho[k % 2]
        for (s, e_, cond) in ((0, 1024, 0), (1024, 1536, 1)):
            mk.op(act, lambda e, s=s, e_=e_, cond=cond: e.activation(
                out=h[:, s:e_], in_=x[:, s:e_], func=AF.Identity,
                scale=self.mod(0, 1, k, cond), bias=self.mod(0, 0, k, cond)),
                reads=[x_b, self.modb(0, 0), self.modb(0, 1)], writes=[h_b])
        mk.dma(sp, self.hT_d[k * 128:(k + 1) * 128, :], h[:], reads=[h_b], writes=[self.bufs["hT_d"]])
    return load, compute


Prog.phase_mod0 = _phase_mod0


def _phase_in(self, l):
    mk = self.mk
    pe, act, dve, pool, sp = mk.pe, mk.act, mk.dve, mk.pool, mk.sp
    B = self.bufs
    ps, psb = self.ps, self.psb
    with ExitStack() as es:
        hT, hT_b = self.sb(es, "hT", [128, KC, T], BF16)
        wb = [self.sb(es, "inw%d" % i, [128, KC, 512], BF16) for i in range(2)]
        ropeT, rope_b = self.sb(es, "ropeT", [128, 2, LAT])
        hob = [self.sb(es, "hob%d" % i, [128, T], BF16) for i in range(2)]
        sq3 = [self.sb(es, "sq%d" % i, [128, 512], BF16) for i in range(3)]
        rstd3 = [self.sb(es, "rstd%d" % i, [128, 512]) for i in range(3)]
        qn3 = [self.sb(es, "qn%d" % i, [128, 512]) for i in range(6)]
        t1 = [self.sb(es, "t1_%d" % i, [128, 512]) for i in range(2)]
        t2 = [self.sb(es, "t2_%d" % i, [128, 512]) for i in range(2)]
        hyraw = [self.sb(es, "hyraw%d" % i, [128, T]) for i in range(2)]
        hyo = [self.sb(es, "hyo%d" % i, [128, T]) for i in range(2)]
        tmb = [self.sb(es, "tmb%d" % i, [128, 512], BF16) for i in range(2)]
        tmf = [self.sb(es, "tmf%d" % i, [128, 512]) for i in range(2)]
        kct = [self.sb(es, "kct%d" % i, [128, 128]) for i in range(2)]
        cnt = {"e": 0, "tm": 0, "hy": 0, "set": 0}

        hT_kb = self.load_kq(hT, KC, self.hT_d, B["hT_d"])
        mk.dma(sp, ropeT[:], self.rope, writes=[rope_b])

        def load_w(i):
            w, w_b = wb[i % 2]
            src = self.w_in[l].rearrange("(kc p) n -> p kc n", p=128)[:, :, i * 512:(i + 1) * 512]
            for q in range(4):
                mk.dma(pool, w[:, q * 4:(q + 1) * 4, :], src[:, q * 4:(q + 1) * 4, :], writes=[w_b])

        def fm_chunk(w, w_b, col0):
            banks = (0, 1, 2) if cnt["set"] % 2 == 0 else (3, 4, 5)
            cnt["set"] += 1
            for k in range(KC):
                for tt in range(3):
                    mk.op(pe, lambda e, k=k, tt=tt: e.matmul(
                        ps[banks[tt]][:, :], w[:, k, col0:col0 + 128], hT[:, k, tt * 512:(tt + 1) * 512],
                        start=(k == 0), stop=(k == KC - 1)),
                        reads=[w_b, hT_kb[k]], writes=[psb[banks[tt]]], inc=(k == KC - 1))
            return banks

        def tm_tile(w, w_b, col0, ncols, t):
            bank = 6 + (cnt["tm"] % 2)
            cnt["tm"] += 1
            for k in range(KC):
                mk.op(pe, lambda e, k=k: e.matmul(
                    ps[bank][:, 0:ncols], hT[:, k, t * 128:(t + 1) * 128], w[:, k, col0:col0 + ncols],
                    start=(k == 0), stop=(k == KC - 1)),
                    reads=[w_b, hT_kb[k]], writes=[psb[bank]], inc=(k == KC - 1))
            return bank

        pend = {"s1": None, "s2": None}

        def qk_stage1(st):
            banks, wvec = st["banks"], st["wvec"]
            st["qn"] = []
            sqs = []
            for tt in range(3):
                bk = banks[tt]
                sq_, sq_b = self.sb(es, "sqp", [128, 512]) if False else sq3[(st["idx"] * 3 + tt) % len(sq3)]
                mk.op(act, lambda e, bk=bk, sq_=sq_: e.activation(out=sq_[:], in_=ps[bk][:, :], func=AF.Square),
                      reads=[psb[bk]], writes=[sq_b])
                sqs.append((sq_, sq_b))
            for tt in range(3):
                bk = banks[tt]
                sq_, sq_b = sqs[tt]
                r_, r_b = rstd3[(st["idx"] * 3 + tt) % len(rstd3)]
                q_, q_b = qn3[(st["idx"] * 3 + tt) % len(qn3)]
                mk.op(pe, lambda e, sq_=sq_: e.matmul(ps[6][:, :], self.oneshd_bf[:], sq_[:], start=True, stop=True),
                      reads=[self.oneshd_bf_b, sq_b], writes=[psb[6]])
                mk.op(act, lambda e, r_=r_: e.activation(out=r_[:], in_=ps[6][:, :], func=AF.Sqrt,
                                                         bias=self.vec(V_EPS), scale=1.0),
                      reads=[psb[6], self.vec_b], writes=[r_b])
                mk.op(dve, lambda e, r_=r_: e.reciprocal(r_[:], r_[:]), reads=[r_b], writes=[r_b])
                mk.op(dve, lambda e, bk=bk, r_=r_, q_=q_: e.scalar_tensor_tensor(
                    out=q_[:], in0=ps[bk][:, :], scalar=wvec, in1=r_[:], op0=ALU.mult, op1=ALU.mult),
                    reads=[psb[bk], r_b, self.vec_b], writes=[q_b])
                st["qn"].append((q_, q_b))

        def qk_stage2(st):
            ho, ho_b = hob[st["idx"] % 2]
            is_k, kvh = st["is_k"], st["kvh"]
            for tt in range(3):
                q_, q_b = st["qn"][tt]
                i = cnt["e"] % 2
                cnt["e"] += 1
                if tt < 2:
                    mk.op(pe, lambda e, q_=q_: e.matmul(ps[7][:, :], self.cmat(CM_PSW), q_[:], start=True, stop=True),
                          reads=[self.cm_b, q_b], writes=[psb[7]])
                    mk.op(dve, lambda e, i=i, tt=tt, q_=q_: e.tensor_tensor(
                        out=t1[i][0][:], in0=q_[:], in1=ropeT[:, 0, tt * 512:(tt + 1) * 512], op=ALU.mult),
                        reads=[q_b, rope_b], writes=[t1[i][1]])
                    mk.op(dve, lambda e, i=i, tt=tt: e.tensor_tensor(
                        out=t2[i][0][:], in0=ps[7][:, :], in1=ropeT[:, 1, tt * 512:(tt + 1) * 512], op=ALU.mult),
                        reads=[psb[7], rope_b], writes=[t2[i][1]])
                    mk.op(dve, lambda e, i=i, tt=tt: e.tensor_tensor(
                        out=ho[:, tt * 512:(tt + 1) * 512], in0=t1[i][0][:], in1=t2[i][0][:], op=ALU.add),
                        reads=[t1[i][1], t2[i][1]], writes=[ho_b])
                else:
                    mk.op(act, lambda e, q_=q_: e.activation(out=ho[:, 1024:1536], in_=q_[:], func=AF.Identity),
                          reads=[q_b], writes=[ho_b])
                    if is_k:
                        for j in range(4):
                            c, c_b = kct[j % 2]
                            mk.op(pe, lambda e, j=j, q_=q_: e.matmul(
                                ps[7][:, 0:128], q_[:, j * 128:(j + 1) * 128], self.cmat(CM_ID),
                                start=True, stop=True),
                                reads=[q_b, self.cm_b], writes=[psb[7]])
                            mk.op(act, lambda e, c=c: e.activation(out=c[:], in_=ps[7][:, 0:128], func=AF.Identity),
                                  reads=[psb[7]], writes=[c_b])
                            mk.dma(sp, self.nck[j // 2, l, kvh, (j % 2) * 128:(j % 2 + 1) * 128, :], c[:],
                                   reads=[c_b], writes=[B["nck"]])
            mk.dma(sp, st["dst"], ho[:], reads=[ho_b], writes=[st["dst_b"]])

        def qk_advance(new_st):
            if pend["s1"] is not None:
                qk_stage1(pend["s1"])
            if pend["s2"] is not None:
                qk_stage2(pend["s2"])
            pend["s2"] = pend["s1"]
            pend["s1"] = new_st

        def qk_drain():
            qk_advance(None)
            qk_advance(None)

        def qk_head(w, w_b, col0, wvec, dst, dst_b, is_k, kvh):
            banks = fm_chunk(w, w_b, col0)
            cnt["qk"] = cnt.get("qk", 0) + 1
            qk_advance({"banks": banks, "wvec": wvec, "dst": dst, "dst_b": dst_b, "is_k": is_k, "kvh": kvh,
                        "idx": cnt["qk"]})

        def copy_head(w, w_b, col0, scale, dst, dst_b):
            banks = fm_chunk(w, w_b, col0)
            ho, ho_b = hob[cnt["e"] % 2]
            cnt["e"] += 1
            for tt in range(3):
                bk = banks[tt]
                mk.op(act, lambda e, bk=bk, tt=tt: e.activation(
                    out=ho[:, tt * 512:(tt + 1) * 512], in_=ps[bk][:, :], func=AF.Identity, scale=scale),
                    reads=[psb[bk]], writes=[ho_b])
            mk.dma(sp, dst, ho[:], reads=[ho_b], writes=[dst_b])

        def hy_chunk(w, w_b, col0, ch):
            banks = fm_chunk(w, w_b, col0)
            i = cnt["hy"] % 2
            cnt["hy"] += 1
            raw, raw_b = hyraw[i]
            o, o_b = hyo[i]
            for tt in range(3):
                bk = banks[tt]
                mk.op(act, lambda e, bk=bk, tt=tt: e.activation(
                    out=raw[:, tt * 512:(tt + 1) * 512], in_=ps[bk][:, :], func=AF.Identity),
                    reads=[psb[bk]], writes=[raw_b])
            self.conv3(raw, raw_b, o, o_b, lambda tap: self.vec(V_HYC + (l * 3 + tap) * 12 + ch))
            mk.dma(sp, self.hyT_d[ch * 128:(ch + 1) * 128, :], o[:], reads=[o_b], writes=[B["hyT_d"]])

        def tm_group(w, w_b, col0, ncols, kind):
            for t in range(NT):
                bank = tm_tile(w, w_b, col0, ncols, t)
                i = cnt["e"] % 2
                cnt["e"] += 1
                rows = slice(t * 128, (t + 1) * 128)
                if kind == "v":
                    ob, ob_b = tmb[i]
                    mk.op(act, lambda e: e.activation(out=ob[:, 0:256], in_=ps[bank][:, 0:256], func=AF.Identity),
                          reads=[psb[bank]], writes=[ob_b])
                    mk.dma(sp, self.v_d[rows, :], ob[:, 0:256], reads=[ob_b], writes=[B["v_d"]])
                    if t >= 8:
                        of, of_b = tmf[i]
                        mk.op(dve, lambda e: e.tensor_copy(of[:, 0:256], ps[bank][:, 0:256]),
                              reads=[psb[bank]], writes=[of_b])
                        seq, half = (t - 8) // 2, (t - 8) % 2
                        mk.dma(sp, self.ncv[seq, l, :, half * 128:(half + 1) * 128, :].rearrange("kv p d -> p kv d"),
                               of[:, 0:256].rearrange("p (kv d) -> p kv d", kv=2),
                               reads=[of_b], writes=[B["ncv"]])
                elif kind == "rk":
                    ob, ob_b = tmb[i]
                    mk.op(act, lambda e: e.activation(out=ob[:], in_=ps[bank][:, :], func=AF.Identity, scale=HD ** -0.5),
                          reads=[psb[bank]], writes=[ob_b])
                    mk.dma(sp, self.rk_d[rows, :], ob[:], reads=[ob_b], writes=[B["rk_d"]])
                elif kind == "rv":
                    ob, ob_b = tmb[i]
                    mk.op(act, lambda e: e.activation(out=ob[:], in_=ps[bank][:, :], func=AF.Identity),
                          reads=[psb[bank]], writes=[ob_b])
                    mk.dma(sp, self.rv_d[rows, :], ob[:], reads=[ob_b], writes=[B["rv_d"]])
                elif kind == "rg":
                    of, of_b = tmf[i]
                    mk.op(act, lambda e: e.activation(out=of[:], in_=ps[bank][:, :], func=AF.Silu),
                          reads=[psb[bank]], writes=[of_b])
                    mk.dma(sp, self.rg_d[rows, :], of[:], reads=[of_b], writes=[B["rg_d"]])

        load_w(0)
        for i in range(10):
            if i + 1 < 10:
                load_w(i + 1)
            w, w_b = wb[i % 2]
            if i < 2:
                for hh in range(4):
                    h = i * 4 + hh
                    qk_head(w, w_b, hh * 128, self.vec(V_QKN + l * 2 + 0), self.qT_d[h], B["qT_d"], False, 0)
            elif i == 2:
                for hh in range(2):
                    qk_head(w, w_b, hh * 128, self.vec(V_QKN + l * 2 + 1), self.kT_d[hh], B["kT_d"], True, hh)
                qk_drain()
                tm_group(w, w_b, 256, 256, "v")
            elif i == 3:
                for hh in range(4):
                    copy_head(w, w_b, hh * 128, 1.0, self.rqT_d[hh], B["rqT_d"])
            elif i == 4:
                for hh in range(4):
                    copy_head(w, w_b, hh * 128, HD ** -0.5, self.rkT_d[hh], B["rkT_d"])
                tm_group(w, w_b, 0, 512, "rk")
            elif i == 5:
                tm_group(w, w_b, 0, 512, "rv")
            elif i == 6:
                tm_group(w, w_b, 0, 512, "rg")
            else:
                for hh in range(4):
                    hy_chunk(w, w_b, hh * 128, (i - 7) * 4 + hh)
        mk.barrier()


Prog.phase_in = _phase_in


def _conv3(self, raw, raw_b, o, o_b, tapvec):
    mk = self.mk
    mk.op(mk.act, lambda e: e.activation(out=o[:, :], in_=raw[:, :], func=AF.Identity, scale=tapvec(1)),
          reads=[raw_b, self.vec_b], writes=[o_b])
    for (s, e_, _c) in SEGS:
        mk.op(mk.dve, lambda e, s=s, e_=e_: e.scalar_tensor_tensor(
            out=o[:, s + 1:e_], in0=raw[:, s:e_ - 1], scalar=tapvec(0), in1=o[:, s + 1:e_],
            op0=ALU.mult, op1=ALU.add), reads=[raw_b, o_b, self.vec_b], writes=[o_b])
        mk.op(mk.dve, lambda e, s=s, e_=e_: e.scalar_tensor_tensor(
            out=o[:, s:e_ - 1], in0=raw[:, s + 1:e_], scalar=tapvec(2), in1=o[:, s:e_ - 1],
            op0=ALU.mult, op1=ALU.add), reads=[raw_b, o_b, self.vec_b], writes=[o_b])


Prog.conv3 = _conv3


def _fm(v):
    return np.ascontiguousarray(v.reshape(-1, 128).T)


def _const_tables():
    f32 = np.float32
    cm = np.zeros((NCM, 128, 128), f32)
    cm[CM_ID] = np.eye(128, dtype=f32)
    cm[CM_ONES_HD] = 1.0 / 128.0
    m = np.arange(128)
    cm[CM_PSW][(m + 64) % 128, m] = 1.0
    cm[CM_ONES_D] = 1.0 / D
    j = np.arange(128)[:, None].astype(f32)
    i = np.arange(128)[None, :].astype(f32)
    cm[CM_DPOS] = np.maximum(i - j, 0)
    cm[CM_MF] = (i >= j)
    cm[CM_DNEG] = np.maximum(j - i, 0)
    cm[CM_MB] = (j >= i)
    cm[CM_IP1] = np.broadcast_to(i + 1.0, (128, 128))
    cm[CM_CMI] = np.broadcast_to(128.0 - i, (128, 128))
    cmats = np.ascontiguousarray(cm.transpose(1, 0, 2).reshape(128, NCM * 128))
    t = np.arange(LAT)
    row = (t // GRID_W).astype(f32)
    col = (t % GRID_W).astype(f32)
    n_freq = HD // 4
    inv_freq = (np.float32(10000.0) ** (-np.arange(n_freq, dtype=f32) / n_freq)).astype(f32)
    ang = np.concatenate([row[:, None] * inv_freq[None], col[:, None] * inv_freq[None]], -1)
    cos, sin = np.cos(ang).astype(f32), np.sin(ang).astype(f32)
    rope = np.zeros((128, 2, LAT), f32)
    rope[:64, 0] = cos.T
    rope[64:, 0] = cos.T
    rope[:64, 1] = -sin.T
    rope[64:, 1] = sin.T
    return cmats, rope


def _prep(inp):
    f32 = np.float32
    g = {k: np.asarray(v) for k, v in inp.items()}
    cmats, rope = _const_tables()
    vec = np.zeros((128, NVEC), f32)
    for l in range(DEPTH):
        vec[:, V_QKN + l * 2 + 0] = g["q_norm"][l]
        vec[:, V_QKN + l * 2 + 1] = g["k_norm"][l]
        for tap in range(3):
            vec[:, V_HYC + (l * 3 + tap) * 12:V_HYC + (l * 3 + tap + 1) * 12] = _fm(g["hy_conv"][l, tap])
            vec[:, V_FCV + (l * 3 + tap) * 88:V_FCV + (l * 3 + tap + 1) * 88] = _fm(g["ffn_conv"][l, tap])
        for j, nm in enumerate(("ln1_w", "ln1_b", "ln2_w", "ln2_b")):
            vec[:, V_LNP + (l * 4 + j) * 16:V_LNP + (l * 4 + j + 1) * 16] = _fm(g[nm][l])
        for o in range(2):
            vec[:, V_HYB + (l * 2 + o) * 4:V_HYB + (l * 2 + o + 1) * 4] = _fm(g["hy_bias"][l, o])
        vec[:, V_RETD + l * 8:V_RETD + (l + 1) * 8] = g["ret_decay"][l].reshape(1, 8)
        vec[:64, V_HF + l * 3 + 0] = g["hf_b1"][l]
        vec[:64, V_HF + l * 3 + 1] = g["hf_freq"][l]
        vec[:64, V_HF + l * 3 + 2] = g["hf_b2"][l]
        vec[:, V_BADA + l * 96:V_BADA + (l + 1) * 96] = _fm(g["b_ada"][l])
    vec[:, V_JJ] = np.arange(128)
    vec[:, V_EPS] = EPS
    vec[:, V_EPS + 1] = -PI
    vec[:, V_EPS + 2] = 1.0
    vec[:, V_JJ + 1] = 127 - np.arange(128)
    shared = {
        "w_ada": g["w_ada"], "w_in": g["w_in"], "w_out": g["w_out"], "w_up": g["w_up"],
        "w_down": g["w_down"], "vecs": vec, "cmats": cmats, "rope": rope,
        "hf_w1": g["hf_w1"], "hf_w2": g["hf_w2"], "hf_w3": g["hf_w3"],
    }
    for L_, tb in _hy_consts().items():
        for nm, arr in tb.items():
            shared["hy_%s_%d" % (nm, L_)] = arr
    maps = []
    for i in range(NCORES):
        xT0 = np.concatenate([g["x_sample"][i].T, g["x_prompt"][2 * i].T, g["x_prompt"][2 * i + 1].T], axis=1)
        cpair = np.stack([g["c"][i], g["c_ctx"]], 0)
        cT = np.ascontiguousarray(cpair.reshape(2, KC, 128).transpose(2, 1, 0))
        m = dict(shared)
        m["xT0"] = np.ascontiguousarray(xT0)
        m["cT"] = cT
        m["cache_kT"] = np.ascontiguousarray(g["cache_k"][i].transpose(0, 1, 3, 2))
        m["cache_v"] = np.ascontiguousarray(g["cache_v"][i])
        m["state_ret"] = np.ascontiguousarray(g["state_ret"][i])
        maps.append(m)
    return maps


_PROG_CACHE = {}


def _get_prog(debug=False, stop_after=None):
    key = (debug, stop_after)
    if key not in _PROG_CACHE:
        p = Prog(debug=debug, stop_after=stop_after)
        p.build()
        _PROG_CACHE[key] = p
    return _PROG_CACHE[key]


def run(inputs, debug=False, stop_after=None, cores=None):
    p = _get_prog(debug, stop_after)
    maps = _prep(inputs)
    maps = [{k: m[k] for k in p.in_names} for m in maps]
    if cores is not None:
        maps = [maps[c] for c in cores]
    res = run_bass_kernel_spmd(p.nc, maps, core_ids=list(range(len(maps))))
    return p, res


def kernel(**inputs):
    p, res = run(inputs)
    R = res.results
    y_s = np.stack([R[i]["yT"][:, 0:LAT].T for i in range(NCORES)], 0)
    y_p = np.stack([R[i]["yT"][:, LAT + s * CTX:LAT + (s + 1) * CTX].T for i in range(NCORES) for s in range(2)], 0)
    nck = np.concatenate([R[i]["nck"] for i in range(NCORES)], 0)
    ncv = np.concatenate([R[i]["ncv"] for i in range(NCORES)], 0)
    nsr = np.concatenate([R[i]["nsr"] for i in range(NCORES)], 0)
    return (np.ascontiguousarray(y_p, dtype=np.float32), np.ascontiguousarray(y_s, dtype=np.float32),
            nck.astype(np.float32), ncv.astype(np.float32), nsr.astype(np.float32))


def _phase_att(self, l):
    mk = self.mk
    pe, act, dve, pool, sp = mk.pe, mk.act, mk.dve, mk.pool, mk.sp
    B = self.bufs
    ps, psb = self.ps, self.psb
    scale = HD ** -0.5
    with ExitStack() as es:
        qT, qT_b = self.sb(es, "a_qT", [128, 8, T], BF16)
        kT, kT_b = self.sb(es, "a_kT", [128, 2, PAST + T], BF16)
        vA, vA_b = self.sb(es, "a_v", [128, 16, 256], BF16)
        E = [self.sb(es, "a_E%d" % i, [128, 512], BF16) for i in range(3)]
        rec = [self.sb(es, "a_rec%d" % i, [128, 512]) for i in range(2)]
        aT = [self.sb(es, "a_o%d" % i, [128, 512], BF16) for i in range(2)]
        if self.ada_pending:
            self.ada_bufs(es)
        mk.dma(sp, kT[:, :, PAST:PAST + T], self.kT_d.rearrange("h p t -> p h t"), reads=[B["kT_d"]], writes=[kT_b])
        mk.dma(pool, kT[:, :, 0:PAST], self.cache_kT[l].rearrange("h p t -> p h t"), writes=[kT_b])
        qT_hb = [Buf("qTh%d" % h) for h in range(8)]
        mk.dma(sp, qT[:, 0, :], self.qT_d[0], reads=[B["qT_d"]], writes=[qT_hb[0]])
        mk.dma(sp, vA[:, 4:16, :], self.v_d.rearrange("(c p) n -> p c n", p=128), reads=[B["v_d"]], writes=[vA_b])
        for h_ in range(1, 8):
            mk.dma(sp, qT[:, h_, :], self.qT_d[h_], reads=[B["qT_d"]], writes=[qT_hb[h_]])
        for kv_ in range(2):
            mk.dma(pool, vA[:, 0:4, kv_ * 128:(kv_ + 1) * 128],
                   self.cache_v[l, kv_].rearrange("(c p) d -> p c d", p=128), writes=[vA_b])
        cnt = {"s": 0, "g": 0}

        def attend(h, q0, q1, chunks):
            kv = h // 4
            n = q1 - q0
            g = cnt["g"] % 2
            cnt["g"] += 1
            ob, db = 3 + g, 5 + g
            nchunks = len(chunks)
            sis = []
            LA = 2
            for ci in range(nchunks + LA):
                if ci < nchunks:
                    koff, vc = chunks[ci]
                    si = cnt["s"] % 3
                    cnt["s"] += 1
                    sis.append(si)
                    mk.op(pe, lambda e, si=si, koff=koff: e.matmul(ps[si][:, 0:n], kT[:, kv, koff:koff + 128], qT[:, h, q0:q1],
                                                                 start=True, stop=True),
                          reads=[kT_b, qT_hb[h]], writes=[psb[si]])
                    Et, Et_b = E[si]
                    mk.op(act, lambda e, si=si, Et=Et: e.activation(out=Et[:, 0:n], in_=ps[si][:, 0:n], func=AF.Exp, scale=scale),
                          reads=[psb[si]], writes=[Et_b])
                if ci >= LA:
                    cj = ci - LA
                    koff, vc = chunks[cj]
                    Et, Et_b = E[sis[cj]]
                    last = cj == nchunks - 1
                    mk.op(pe, lambda e, vc=vc, Et=Et, cj=cj, last=last: e.matmul(
                        ps[ob][:, 0:n], vA[:, vc, kv * 128:(kv + 1) * 128], Et[:, 0:n], start=(cj == 0), stop=last),
                        reads=[vA_b, Et_b], writes=[psb[ob]], inc=last)
                    mk.op(pe, lambda e, Et=Et, cj=cj, last=last: e.matmul(
                        ps[db][:, 0:n], self.ones_bf[:], Et[:, 0:n], start=(cj == 0), stop=last),
                        reads=[self.ones_bf_b, Et_b], writes=[psb[db]], inc=True)
            r, r_b = rec[g]
            o, o_b = aT[g]
            mk.op(dve, lambda e: e.reciprocal(r[:, 0:n], ps[db][:, 0:n]), reads=[psb[db]], writes=[r_b])
            mk.op(dve, lambda e: e.tensor_tensor(out=o[:, 0:n], in0=ps[ob][:, 0:n], in1=r[:, 0:n], op=ALU.mult),
                  reads=[psb[ob], r_b], writes=[o_b])
            mk.dma(sp, self.catT_d[h * 128:(h + 1) * 128, q0:q1], o[:, 0:n], reads=[o_b], writes=[B["catT_d"]])
            self.ada_bg(1)

        lat_chunks = [(c * 128, c) for c in range(4)] + [(PAST + c * 128, 4 + c) for c in range(8)]
        for h in range(8):
            for qt in range(2):
                attend(h, qt * 512, (qt + 1) * 512, lat_chunks)
            for s in range(2):
                t0 = 8 + 2 * s
                ch = [(PAST + (t0 + c) * 128, 4 + t0 + c) for c in range(2)]
                attend(h, LAT + s * CTX, LAT + (s + 1) * CTX, ch)
        mk.barrier()


Prog.phase_att = _phase_att


def _ret_tables(self, l):
    mk = self.mk
    act, dve = mk.act, mk.dve
    es = self.es
    M, M_b = self.sb(es, "rt_M", [128, 4, 128])
    dq, _ = self.sb(es, "rt_dq", [128, 4, 2, 128])
    dk, _ = self.sb(es, "rt_dk", [128, 4, 2])
    cd, _ = self.sb(es, "rt_cd", [128, 4, 2])
    cdf, _ = self.sb(es, "rt_cdf", [128, 2, 4, 128])
    dq_b = dk_b = cd_b = M_b
    if True:
        lg, lg_b = self.sb(es, "r_lg", [128, 8])
        tmpm, tmpm_b = self.sb(es, "r_tmpm", [128, 128])
        retd = self.vec(V_RETD + l * 8, 8)
        mk.op(act, lambda e: e.activation(out=lg[:], in_=retd, func=AF.Exp, scale=-1.0),
              reads=[self.vec_b], writes=[lg_b])
        mk.op(act, lambda e: e.activation(out=lg[:], in_=lg[:], func=AF.Ln, bias=self.vec(V_EPS + 2), scale=1.0),
              reads=[lg_b, self.vec_b], writes=[lg_b])
        mk.op(dve, lambda e: e.tensor_scalar_mul(lg[:], lg[:], -1.0), reads=[lg_b], writes=[lg_b])
        for h in range(4):
            lf = lg[:, h:h + 1]
            lb = lg[:, 4 + h:5 + h]
            mk.op(act, lambda e: e.activation(out=tmpm[:], in_=self.cmat(CM_DPOS), func=AF.Exp, scale=lf),
                  reads=[self.cm_b, lg_b], writes=[tmpm_b])
            mk.op(dve, lambda e: e.tensor_tensor(out=M[:, h, :], in0=tmpm[:], in1=self.cmat(CM_MF), op=ALU.mult),
                  reads=[tmpm_b, self.cm_b], writes=[M_b])
            mk.op(act, lambda e: e.activation(out=tmpm[:], in_=self.cmat(CM_DNEG), func=AF.Exp, scale=lb),
                  reads=[self.cm_b, lg_b], writes=[tmpm_b])
            mk.op(dve, lambda e: e.tensor_tensor(out=tmpm[:], in0=tmpm[:], in1=self.cmat(CM_MB), op=ALU.mult),
                  reads=[tmpm_b, self.cm_b], writes=[tmpm_b])
            mk.op(dve, lambda e: e.tensor_tensor(out=M[:, h, :], in0=M[:, h, :], in1=tmpm[:], op=ALU.add),
                  reads=[tmpm_b, M_b], writes=[M_b])
            mk.op(act, lambda e: e.activation(out=dq[:, h, 0, :], in_=self.cmat(CM_IP1), func=AF.Exp, scale=lf),
                  reads=[self.cm_b, lg_b], writes=[dq_b])
            mk.op(act, lambda e: e.activation(out=dq[:, h, 1, :], in_=self.cmat(CM_CMI), func=AF.Exp, scale=lb),
                  reads=[self.cm_b, lg_b], writes=[dq_b])
            mk.op(act, lambda e: e.activation(out=dk[:, h, 0:1], in_=self.vec(V_JJ + 1), func=AF.Exp, scale=lf),
                  reads=[self.vec_b, lg_b], writes=[dk_b])
            mk.op(act, lambda e: e.activation(out=dk[:, h, 1:2], in_=self.vec(V_JJ), func=AF.Exp, scale=lb),
                  reads=[self.vec_b, lg_b], writes=[dk_b])
            mk.op(act, lambda e: e.activation(out=cd[:, h, 0:1], in_=lf, func=AF.Exp, scale=128.0),
                  reads=[lg_b], writes=[cd_b])
            mk.op(act, lambda e: e.activation(out=cd[:, h, 1:2], in_=lb, func=AF.Exp, scale=128.0),
                  reads=[lg_b], writes=[cd_b])
            for d_ in range(2):
                mk.op(act, lambda e, d_=d_: e.activation(out=cdf[:, d_, h, :], in_=self.cmat(CM_MF), func=AF.Identity,
                                                        scale=0.0, bias=cd[:, h, d_:d_ + 1]),
                      reads=[cd_b, self.cm_b], writes=[cd_b])

    if not hasattr(self, "rt"):
        self.rt = {}
    self.rt[l] = {"M": M, "dq": dq, "dk": dk, "cd": cd, "cdf": cdf, "b": M_b}


Prog.ret_tables = _ret_tables


def _phase_ret(self, l):
    mk = self.mk
    pe, act, dve, pool, sp = mk.pe, mk.act, mk.dve, mk.pool, mk.sp
    B = self.bufs
    ps, psb = self.ps, self.psb
    ident = self.cmat(CM_ID)
    with ExitStack() as es:
        rqT, rqT_b = self.sb(es, "r_qT", [128, 4, T], BF16)
        rkT, rkT_b = self.sb(es, "r_kT", [128, 4, T], BF16)
        rk, rk_b = self.sb(es, "r_k", [128, NT, 512], BF16)
        rv, rv_b = self.sb(es, "r_v", [128, NT, 512], BF16)
        rg, rg_b = self.sb(es, "r_g", [128, NT, 512])
        rkF, _ = self.sb(es, "r_kF", [128, NT, 512], BF16)
        rkB, _ = self.sb(es, "r_kB", [128, NT, 512], BF16)
        Sf4, Sf4_b = self.sb(es, "r_Sf4", [128, 4, 128])
        Sb4, Sb4_b = self.sb(es, "r_Sb4", [128, 4, 128])
        Sfb4, Sfb4_b = self.sb(es, "r_Sfb4", [128, 4, 128], BF16)
        cdf = self.rt[l]["cdf"]
        SbAll, SbAll_b = self.sb(es, "r_SbAll", [128, NT, 4, 128], BF16)
        msk4 = [self.sb(es, "r_msk%d" % i, [128, 4, 128], BF16) for i in range(2)]
        qf4 = [self.sb(es, "r_qf%d" % i, [128, 4, 128], BF16) for i in range(2)]
        qb4 = [self.sb(es, "r_qb%d" % i, [128, 4, 128], BF16) for i in range(2)]
        st6, st6_b = self.sb(es, "r_st6", [128, 4, 6])
        mv, mv_b = self.sb(es, "r_mv", [128, 4, 2])
        rs, rs_b = self.sb(es, "r_rs", [128, 4])
        rn = [self.sb(es, "r_rn%d" % i, [128, 512]) for i in range(2)]
        retT, retT_b = self.sb(es, "r_retT", [128, 4, T], BF16)
        if self.ada_pending:
            self.ada_bufs(es)

        mk.dma(sp, rk[:], self.rk_d.rearrange("(c p) n -> p c n", p=128), reads=[B["rk_d"]], writes=[rk_b])
        mk.dma(sp, rv[:], self.rv_d.rearrange("(c p) n -> p c n", p=128), reads=[B["rv_d"]], writes=[rv_b])
        mk.dma(sp, rqT[:], self.rqT_d.rearrange("h p t -> p h t"), reads=[B["rqT_d"]], writes=[rqT_b])
        mk.dma(sp, rkT[:], self.rkT_d.rearrange("h p t -> p h t"), reads=[B["rkT_d"]], writes=[rkT_b])
        mk.dma(sp, rg[:], self.rg_d.rearrange("(c p) n -> p c n", p=128), reads=[B["rg_d"]], writes=[rg_b])

        M, dq, dk, cd = self.rt[l]["M"], self.rt[l]["dq"], self.rt[l]["dk"], self.rt[l]["cd"]
        M_b = dq_b = dk_b = cd_b = self.rt[l]["b"]
        rkF_nb = [Buf("rkF%d" % n) for n in range(NT)]
        rkB_nb = [Buf("rkB%d" % n) for n in range(NT)]

        def scale_keys(n, fwd):
            for h in range(4):
                hc = slice(h * 128, (h + 1) * 128)
                if fwd:
                    mk.op(act, lambda e, n=n, h=h, hc=hc: e.activation(out=rkF[:, n, hc], in_=rk[:, n, hc], func=AF.Identity,
                                                                     scale=dk[:, h, 0:1]),
                          reads=[rk_b, dk_b], writes=[rkF_nb[n]])
                else:
                    mk.op(dve, lambda e, n=n, h=h, hc=hc: e.tensor_scalar_mul(rkB[:, n, hc], rk[:, n, hc], dk[:, h, 1:2]),
                          reads=[rk_b, dk_b], writes=[rkB_nb[n]])

        cnt = {"a": 0, "kv": 0, "t": 0}
        segs = ((0, 8, True, None), (8, 2, False, 0), (10, 2, False, 1))
        for (c0, nch, has_init, seq) in segs:
            for d_, (S4, S4_b) in enumerate(((Sf4, Sf4_b), (Sb4, Sb4_b))):
                if has_init:
                    mk.dma(sp, S4[:], self.state_ret[l, d_].rearrange("h p e -> p h e"), writes=[S4_b])
                else:
                    mk.op(dve, lambda e, S4=S4: e.memset(S4[:], 0.0), writes=[S4_b])
            for n in range(c0 + nch - 1, c0 - 1, -1):
                scale_keys(n, False)
                mk.op(act, lambda e, n=n: e.activation(out=SbAll[:, n, :, :], in_=Sb4[:], func=AF.Identity),
                      reads=[Sb4_b], writes=[SbAll_b])
                bk = 4 + cnt["kv"] % 2
                cnt["kv"] += 1
                for h in range(4):
                    hc = slice(h * 128, (h + 1) * 128)
                    mk.op(pe, lambda e, n=n, hc=hc, bk=bk: e.matmul(ps[bk][:, hc], rkB[:, n, hc], rv[:, n, hc],
                                                                  start=True, stop=True),
                          reads=[rkB_nb[n], rv_b], writes=[psb[bk]], inc=(h == 3))
                mk.op(dve, lambda e: e.tensor_tensor(out=Sb4[:], in0=Sb4[:], in1=cdf[:, 1, :, :], op=ALU.mult),
                      reads=[Sb4_b, cd_b], writes=[Sb4_b])
                mk.op(dve, lambda e, bk=bk: e.tensor_tensor(out=Sb4[:], in0=Sb4[:],
                                                            in1=ps[bk][:, :].rearrange("p (h e) -> p h e", h=4), op=ALU.add),
                      reads=[Sb4_b, psb[bk]], writes=[Sb4_b])
            if seq is not None:
                mk.dma(sp, self.nsr[seq, l, 1].rearrange("h p e -> p h e"), Sb4[:], reads=[Sb4_b], writes=[B["nsr"]])
            mk.op(act, lambda e: e.activation(out=Sfb4[:], in_=Sf4[:], func=AF.Identity), reads=[Sf4_b], writes=[Sfb4_b])

            def P1(n):
                tk = slice(n * 128, (n + 1) * 128)
                ab = n % 2
                for h in range(4):
                    hc = slice(h * 128, (h + 1) * 128)
                    mk.op(pe, lambda e, h=h, hc=hc: e.matmul(ps[ab][:, hc], rkT[:, h, tk], rqT[:, h, tk], start=True, stop=True),
                          reads=[rkT_b, rqT_b], writes=[psb[ab]], inc=(h == 3))
                m_, m_b = msk4[n % 2]
                f_, f_b = qf4[n % 2]
                b_, b_b = qb4[n % 2]
                for h in range(4):
                    hc = slice(h * 128, (h + 1) * 128)
                    mk.op(dve, lambda e, h=h, hc=hc: e.tensor_tensor(out=m_[:, h, :], in0=ps[ab][:, hc], in1=M[:, h, :], op=ALU.mult),
                          reads=[psb[ab], M_b], writes=[m_b])
                    mk.op(dve, lambda e, h=h: e.tensor_tensor(out=f_[:, h, :], in0=rqT[:, h, tk], in1=dq[:, h, 0, :], op=ALU.mult),
                          reads=[rqT_b, dq_b], writes=[f_b])
                    mk.op(dve, lambda e, h=h: e.tensor_tensor(out=b_[:, h, :], in0=rqT[:, h, tk], in1=dq[:, h, 1, :], op=ALU.mult),
                          reads=[rqT_b, dq_b], writes=[b_b])

            def P2(n):
                ob = 2 + (n % 2)
                kb = 4 + (n % 2)
                m_, m_b = msk4[n % 2]
                f_, f_b = qf4[n % 2]
                b_, b_b = qb4[n % 2]
                for h in range(4):
                    hc = slice(h * 128, (h + 1) * 128)
                    mk.op(pe, lambda e, h=h, hc=hc: e.matmul(ps[ob][:, hc], m_[:, h, :], rv[:, n, hc], start=True, stop=False),
                          reads=[m_b, rv_b], writes=[psb[ob]], inc=False)
                    mk.op(pe, lambda e, h=h, hc=hc: e.matmul(ps[ob][:, hc], f_[:, h, :], Sfb4[:, h, :], start=False, stop=False),
                          reads=[f_b, Sfb4_b], writes=[psb[ob]], inc=False)
                    mk.op(pe, lambda e, h=h, hc=hc: e.matmul(ps[ob][:, hc], b_[:, h, :], SbAll[:, n, h, :], start=False, stop=True),
                          reads=[b_b, SbAll_b], writes=[psb[ob]])
                for h in range(4):
                    hc = slice(h * 128, (h + 1) * 128)
                    mk.op(pe, lambda e, hc=hc: e.matmul(ps[kb][:, hc], rkF[:, n, hc], rv[:, n, hc], start=True, stop=True),
                          reads=[rkF_nb[n], rv_b], writes=[psb[kb]], inc=(h == 3))
                mk.op(dve, lambda e: e.tensor_tensor(out=Sf4[:], in0=Sf4[:], in1=cdf[:, 0, :, :], op=ALU.mult),
                      reads=[Sf4_b, cd_b], writes=[Sf4_b])
                mk.op(dve, lambda e: e.tensor_tensor(out=Sf4[:], in0=Sf4[:],
                                                     in1=ps[kb][:, :].rearrange("p (h e) -> p h e", h=4), op=ALU.add),
                      reads=[Sf4_b, psb[kb]], writes=[Sf4_b])
                mk.op(act, lambda e: e.activation(out=Sfb4[:], in_=Sf4[:], func=AF.Identity), reads=[Sf4_b], writes=[Sfb4_b])

            def P3(n):
                tk = slice(n * 128, (n + 1) * 128)
                ob = 2 + (n % 2)
                for h in range(4):
                    hc = slice(h * 128, (h + 1) * 128)
                    mk.op(dve, lambda e, h=h, hc=hc: e.bn_stats(st6[:, h, :], ps[ob][:, hc]), reads=[psb[ob]], writes=[st6_b])
                    mk.op(dve, lambda e, h=h: e.bn_aggr(mv[:, h, :], st6[:, h, :]), reads=[st6_b], writes=[mv_b])
                mk.op(act, lambda e: e.activation(out=rs[:], in_=mv[:, :, 1], func=AF.Sqrt, bias=self.vec(V_EPS), scale=1.0),
                      reads=[mv_b, self.vec_b], writes=[rs_b])
                mk.op(dve, lambda e: e.reciprocal(rs[:], rs[:]), reads=[rs_b], writes=[rs_b])
                r_, r_b = rn[n % 2]
                for h in range(4):
                    hc = slice(h * 128, (h + 1) * 128)
                    mk.op(dve, lambda e, h=h, hc=hc: e.tensor_scalar(r_[:, hc], ps[ob][:, hc], mv[:, h, 0:1], rs[:, h:h + 1],
                                                                  ALU.subtract, ALU.mult),
                          reads=[psb[ob], mv_b, rs_b], writes=[r_b])
                mk.op(dve, lambda e: e.tensor_tensor(out=r_[:], in0=r_[:], in1=rg[:, n, :], op=ALU.mult),
                      reads=[r_b, rg_b], writes=[r_b])
                for h in range(4):
                    hc = slice(h * 128, (h + 1) * 128)
                    tb = 6 + cnt["t"] % 2
                    cnt["t"] += 1
                    mk.op(pe, lambda e, tb=tb, hc=hc: e.matmul(ps[tb][:, 0:128], r_[:, hc], ident, start=True, stop=True),
                          reads=[r_b, self.cm_b], writes=[psb[tb]])
                    mk.op(act, lambda e, tb=tb, h=h: e.activation(out=retT[:, h, tk], in_=ps[tb][:, 0:128], func=AF.Identity),
                          reads=[psb[tb]], writes=[retT_b])

            scale_keys(c0, True)
            P1(c0)
            for n in range(c0, c0 + nch):
                self.ada_bg(1)
                if n + 1 < c0 + nch:
                    scale_keys(n + 1, True)
                    P1(n + 1)
                P2(n)
                P3(n)
            if seq is not None:
                mk.dma(sp, self.nsr[seq, l, 0].rearrange("h p e -> p h e"), Sf4[:], reads=[Sf4_b], writes=[B["nsr"]])
        mk.dma(sp, self.catT_d[1024:1536, :].rearrange("(h p) t -> p h t", p=128), retT[:], reads=[retT_b],
               writes=[B["catT_d"]])
        mk.barrier()


Prog.phase_ret = _phase_ret


def _hy_tables(L):
    f32 = np.float32
    f = np.arange(L, dtype=np.int64)[None, :]
    j = np.arange(2 * L, dtype=np.int64)[:, None]
    m = ((2 * f + 1) * j) % (4 * L)
    th = np.pi * m.astype(np.float64) / (2 * L)
    cf2 = np.cos(th)
    sf2 = np.sin(th)
    ci = (cf2[:L].T / L)
    si = (sf2[:L].T / L)
    bf = ml_dtypes.bfloat16
    pos = np.arange(L, dtype=f32)
    t = (pos / f32(max(L - 1, 1))).astype(f32)
    w = (f32(2.0 * math.pi) * pos / f32(L)).astype(f32)
    fb = np.linspace(1e-4, 15, 16, dtype=f32)
    ang = (w[:, None] * fb[None, :]).astype(f32)
    z = np.concatenate([t[:, None], np.cos(ang), -np.sin(ang)], -1).astype(f32)
    deltas = np.linspace(math.log(1e-2) / 0.3, math.log(1e-2) / 1.5, 512, dtype=f32)
    dec = np.exp(-t[:, None] * np.abs(deltas)[None, :]).astype(f32)
    return {
        "cf2": cf2.astype(bf), "sf2": sf2.astype(bf), "ci": ci.astype(bf), "si": si.astype(bf),
        "zt": np.ascontiguousarray(np.stack([z.T, z[::-1].T], 0)),
        "dec": np.ascontiguousarray(np.stack([dec, dec[::-1]], 0)),
    }


_HYT = {}


def _hy_consts():
    if not _HYT:
        for L in (LAT, CTX):
            _HYT[L] = _hy_tables(L)
    return _HYT


def _declare_hy(self):
    self.hyc_in = {}
    for L in (LAT, CTX):
        d = {}
        d["cf2"] = self.din("hy_cf2_%d" % L, [2 * L, L], BF16)
        d["sf2"] = self.din("hy_sf2_%d" % L, [2 * L, L], BF16)
        d["ci"] = self.din("hy_ci_%d" % L, [L, L], BF16)
        d["si"] = self.din("hy_si_%d" % L, [L, L], BF16)
        d["zt"] = self.din("hy_zt_%d" % L, [2, 33, L])
        d["dec"] = self.din("hy_dec_%d" % L, [2, L, 512])
        self.hyc_in[L] = d
    self.hf_w1 = self.din("hf_w1", [DEPTH, 33, 64])
    self.hf_w2 = self.din("hf_w2", [DEPTH, 64, 64])
    self.hf_w3 = self.din("hf_w3", [DEPTH, 64, 2048])
    self.kspec_d = self.dscr("kspec_d", [DEPTH, 2, 2, 2, LAT, 512])
    self.bufs["kspec_d"] = Buf("kspec_d")


Prog.declare_hy = _declare_hy


def _hy_filters(self):
    mk = self.mk
    pe, act, dve, pool, sp = mk.pe, mk.act, mk.dve, mk.pool, mk.sp
    B = self.bufs
    ps, psb = self.ps, self.psb
    ident = self.cmat(CM_ID)
    TWO_PI = 2.0 * PI
    for li, L in enumerate((LAT, CTX)):
        C = self.hyc_in[L]
        nj = 2 * L // 128
        nf = L // 128
        ntile = max(L // 512, 1)
        N = min(L, 512)
        with ExitStack() as es:
            cf2, cf2_b = self.sb(es, "h_cf2", [128, nj, L], BF16)
            sf2, sf2_b = self.sb(es, "h_sf2", [128, nj, L], BF16)
            w1, w1_b = self.sb(es, "h_w1", [33, 64])
            w2, w2_b = self.sb(es, "h_w2", [64, 64])
            w3, w3_b = self.sb(es, "h_w3", [64, 2048], BF16)
            zt, zt_b = self.sb(es, "h_zt", [33, 2, L])
            fb, fb_b = self.sb(es, "h_fb", [64, 2])
            vv = [self.sb(es, "h_vv%d" % i, [64, 512]) for i in range(2)]
            mm_ = [self.sb(es, "h_mm%d" % i, [64, 512]) for i in range(2)]
            hd1 = [self.sb(es, "h_hd1%d" % i, [64, 512]) for i in range(2)]
            hd2, hd2_b = self.sb(es, "h_hd2", [64, 2, L], BF16)
            dec = [self.sb(es, "h_dec%d" % i, [128, 512]) for i in range(2)]
            gsb, gsb_b = self.sb(es, "h_g", [128, 2, nj, 512], BF16)
            ko = [self.sb(es, "h_ko%d" % i, [128, 512]) for i in range(2)]
            mk.dma(sp, cf2[:], C["cf2"].rearrange("(c p) f -> p c f", p=128), writes=[cf2_b])
            mk.dma(sp, sf2[:], C["sf2"].rearrange("(c p) f -> p c f", p=128), writes=[sf2_b])
            for l in range(DEPTH):
                mk.dma(sp, w1[:], self.hf_w1[l], writes=[w1_b])
                mk.dma(sp, w2[:], self.hf_w2[l], writes=[w2_b])
                mk.dma(pool, w3[:], self.hf_w3[l], writes=[w3_b])
                mk.dma(sp, zt[:], C["zt"].rearrange("d k t -> k d t"), writes=[zt_b])
                freq = self.vec_sb[0:64, V_HF + l * 3 + 1:V_HF + l * 3 + 2]
                for j_, boff in enumerate((0, 2)):
                    bvec = self.vec_sb[0:64, V_HF + l * 3 + boff:V_HF + l * 3 + boff + 1]
                    mk.op(dve, lambda e, j_=j_, bvec=bvec: e.tensor_scalar(fb[:, j_:j_ + 1], bvec, freq, None, ALU.mult),
                          reads=[self.vec_b], writes=[fb_b])
                negpi = self.vec_sb[0:64, V_EPS + 1:V_EPS + 2]
                it = 0

                def sin_layer(src_bank, n, fbcol, out_ap, out_b):
                    nonlocal it
                    i = it % 2
                    it += 1
                    v_, v_b = vv[i]
                    m_, m_b = mm_[i]
                    mk.op(dve, lambda e: e.tensor_scalar(v_[:, 0:n], ps[src_bank][0:64, 0:n], freq, fb[:, fbcol:fbcol + 1],
                                                         ALU.mult, ALU.add),
                          reads=[psb[src_bank], self.vec_b, fb_b], writes=[v_b])
                    mk.op(dve, lambda e: e.tensor_scalar(m_[:, 0:n], v_[:, 0:n], PI, -TWO_PI, ALU.is_gt, ALU.mult),
                          reads=[v_b], writes=[m_b])
                    mk.op(dve, lambda e: e.tensor_tensor(out=m_[:, 0:n], in0=m_[:, 0:n], in1=v_[:, 0:n], op=ALU.add),
                          reads=[v_b, m_b], writes=[m_b])
                    mk.op(dve, lambda e: e.tensor_scalar(v_[:, 0:n], v_[:, 0:n], -PI, TWO_PI, ALU.is_lt, ALU.mult),
                          reads=[v_b], writes=[v_b])
                    mk.op(dve, lambda e: e.tensor_tensor(out=m_[:, 0:n], in0=m_[:, 0:n], in1=v_[:, 0:n], op=ALU.add),
                          reads=[v_b, m_b], writes=[m_b])
                    mk.op(act, lambda e: e.activation(out=out_ap, in_=m_[:, 0:n], func=AF.Sin),
                          reads=[m_b], writes=[out_b])

                for d_ in range(2):
                    for tt in range(ntile):
                        ts_ = slice(tt * N, (tt + 1) * N)
                        mk.op(pe, lambda e: e.matmul(ps[0][0:64, 0:N], w1[:, :], zt[:, d_, ts_], start=True, stop=True),
                              reads=[w1_b, zt_b], writes=[psb[0]])
                        h1, h1_b = hd1[(d_ * ntile + tt) % 2]
                        sin_layer(0, N, 0, h1[:, 0:N], h1_b)
                        mk.op(pe, lambda e: e.matmul(ps[1][0:64, 0:N], w2[:, :], h1[:, 0:N], start=True, stop=True),
                              reads=[w2_b, h1_b], writes=[psb[1]])
                        sin_layer(1, N, 1, hd2[:, d_, ts_], hd2_b)
                it2 = 0
                for d_ in range(2):
                    for jc in range(nf):
                        dc, dc_b = dec[it2 % 2]
                        mk.dma(sp, dc[:], C["dec"][d_, jc * 128:(jc + 1) * 128, :], writes=[dc_b])
                        for o in range(2):
                            bk = 2 + it2 % 2
                            it2 += 1
                            col0 = o * 1024 + d_ * 512
                            mk.op(pe, lambda e, bk=bk, col0=col0: e.matmul(ps[bk][:, :], hd2[:, d_, jc * 128:(jc + 1) * 128],
                                                                         w3[:, col0:col0 + 512], start=True, stop=True),
                                  reads=[hd2_b, w3_b], writes=[psb[bk]])
                            mk.op(dve, lambda e, bk=bk, o=o: e.scalar_tensor_tensor(
                                out=gsb[:, o, d_ * nf + jc, :], in0=ps[bk][:, :], scalar=(1.0 if d_ == 0 else -1.0),
                                in1=dc[:], op0=ALU.mult, op1=ALU.mult),
                                reads=[psb[bk], dc_b], writes=[gsb_b])
                        it2 += 1
                it3 = 0
                for o in range(2):
                    for fc in range(nf):
                        for cs, (mat, mat_b) in enumerate(((cf2, cf2_b), (sf2, sf2_b))):
                            bk = 4 + it3 % 2
                            k_, k_b = ko[it3 % 2]
                            it3 += 1
                            for jc in range(nj):
                                mk.op(pe, lambda e, jc=jc, mat=mat, bk=bk: e.matmul(
                                    ps[bk][:, :], mat[:, jc, fc * 128:(fc + 1) * 128], gsb[:, o, jc, :],
                                    start=(jc == 0), stop=(jc == nj - 1)),
                                    reads=[mat_b, gsb_b], writes=[psb[bk]], inc=(jc == nj - 1))
                            mk.op(act, lambda e, bk=bk, k_=k_: e.activation(out=k_[:], in_=ps[bk][:, :], func=AF.Identity),
                                  reads=[psb[bk]], writes=[k_b])
                            mk.dma(sp, self.kspec_d[l, li, o, cs, fc * 128:(fc + 1) * 128, :], k_[:], reads=[k_b],
                                   writes=[B["kspec_d"]])
                            if self.bg_hook is not None:
                                self.bg_hook()
            mk.barrier()


Prog.hy_filters = _hy_filters


def _phase_hy(self, l):
    mk = self.mk
    pe, act, dve, pool, sp = mk.pe, mk.act, mk.dve, mk.pool, mk.sp
    B = self.bufs
    ps, psb = self.ps, self.psb
    ident = self.cmat(CM_ID)
    for (L, seqs) in ((LAT, ((0, 1024),)), (CTX, ((1024, 1280), (1280, 1536)))):
        li = 0 if L == LAT else 1
        C = self.hyc_in[L]
        nf = L // 128
        ntile = max(L // 512, 1)
        N = min(L, 512)
        with ExitStack() as es:
            cf, cf_b = self.sb(es, "c_cf", [128, nf, L], BF16)
            sf, sf_b = self.sb(es, "c_sf", [128, nf, L], BF16)
            ci, ci_b = self.sb(es, "c_ci", [128, nf, L], BF16)
            si, si_b = self.sb(es, "c_si", [128, nf, L], BF16)
            zT, _ = self.sb(es, "c_zT", [128, 4, L])
            zT_bb = [[Buf("zT%d_%d" % (cc_, tt_)) for tt_ in range(ntile)] for cc_ in range(4)]
            xT = [self.sb(es, "c_xT%d" % i, [128, 4, L]) for i in range(2)]
            ztok, ztok_b = self.sb(es, "c_ztok", [128, nf, 512], BF16)
            yre, yre_b = self.sb(es, "c_yre", [128, nf, 512], BF16)
            ysn, ysn_b = self.sb(es, "c_ysn", [128, nf, 512], BF16)
            kc = [self.sb(es, "c_kc%d" % i, [128, 512]) for i in range(2)]
            ks = [self.sb(es, "c_ks%d" % i, [128, 512]) for i in range(2)]
            tq = [self.sb(es, "c_tq%d" % i, [128, 512]) for i in range(4)]
            tz = [self.sb(es, "c_tz%d" % i, [128, 512]) for i in range(2)]
            outT, outT_b = self.sb(es, "c_outT", [128, 4, L], BF16)
            def load_z(a, b_):
                mk.dma(sp, zT[:], self.hyT_d[0:512, a:b_].rearrange("(c p) t -> p c t", p=128), reads=[B["hyT_d"]],
                       writes=[zb for row in zT_bb for zb in row])

            load_z(*seqs[0])
            mk.dma(sp, cf[:], C["cf2"][0:L, :].rearrange("(c p) f -> p c f", p=128), writes=[cf_b])
            mk.dma(sp, sf[:], C["sf2"][0:L, :].rearrange("(c p) f -> p c f", p=128), writes=[sf_b])
            mk.dma(sp, ci[:], C["ci"].rearrange("(c p) n -> p c n", p=128), writes=[ci_b])
            mk.dma(sp, si[:], C["si"].rearrange("(c p) n -> p c n", p=128), writes=[si_b])
            for si_, (a, b_) in enumerate(seqs):
                if si_ > 0:
                    load_z(a, b_)
                for o in range(2):
                    mk.dma(sp, xT[o][0][:], self.hyT_d[512 * (o + 1):512 * (o + 2), a:b_].rearrange("(c p) t -> p c t", p=128),
                           reads=[B["hyT_d"]], writes=[xT[o][1]])
                itk = 0
                for o in range(2):
                    for tc in range(nf):
                        bk = 6 + tc % 2
                        for cc in range(4):
                            mk.op(pe, lambda e, cc=cc, bk=bk: e.matmul(ps[bk][:, cc * 128:(cc + 1) * 128],
                                                                     zT[:, cc, tc * 128:(tc + 1) * 128], ident,
                                                                     start=True, stop=True),
                                  reads=[zT_bb[cc][(tc * 128) // N], self.cm_b], writes=[psb[bk]], inc=(cc == 3))
                        mk.op(act, lambda e, bk=bk: e.activation(out=ztok[:, tc, :], in_=ps[bk][:, :], func=AF.Identity),
                              reads=[psb[bk]], writes=[ztok_b])
                    for fc in range(nf):
                        i = itk % 2
                        itk += 1
                        kc_, kc_b = kc[i]
                        ks_, ks_b = ks[i]
                        mk.dma(sp, kc_[:], self.kspec_d[l, li, o, 0, fc * 128:(fc + 1) * 128, :], reads=[B["kspec_d"]], writes=[kc_b])
                        mk.dma(sp, ks_[:], self.kspec_d[l, li, o, 1, fc * 128:(fc + 1) * 128, :], reads=[B["kspec_d"]], writes=[ks_b])
                        bc, bs = (0, 1) if i == 0 else (2, 3)
                        for (mat, mat_b, bk) in ((cf, cf_b, bc), (sf, sf_b, bs)):
                            for tc in range(nf):
                                mk.op(pe, lambda e, mat=mat, bk=bk, tc=tc: e.matmul(
                                    ps[bk][:, :], mat[:, tc, fc * 128:(fc + 1) * 128], ztok[:, tc, :],
                                    start=(tc == 0), stop=(tc == nf - 1)),
                                    reads=[mat_b, ztok_b], writes=[psb[bk]], inc=(tc == nf - 1))
                        q0, q1, q2, q3 = tq
                        mk.op(dve, lambda e: e.tensor_tensor(out=q0[0][:], in0=ps[bc][:, :], in1=kc_[:], op=ALU.mult),
                              reads=[psb[bc], kc_b], writes=[q0[1]])
                        mk.op(dve, lambda e: e.tensor_tensor(out=q1[0][:], in0=ps[bs][:, :], in1=ks_[:], op=ALU.mult),
                              reads=[psb[bs], ks_b], writes=[q1[1]])
                        mk.op(pool, lambda e: e.tensor_tensor(out=yre[:, fc, :], in0=q0[0][:], in1=q1[0][:], op=ALU.subtract),
                              reads=[q0[1], q1[1]], writes=[yre_b])
                        mk.op(dve, lambda e: e.tensor_tensor(out=q2[0][:], in0=ps[bc][:, :], in1=ks_[:], op=ALU.mult),
                              reads=[psb[bc], ks_b], writes=[q2[1]])
                        mk.op(dve, lambda e: e.tensor_tensor(out=q3[0][:], in0=ps[bs][:, :], in1=kc_[:], op=ALU.mult),
                              reads=[psb[bs], kc_b], writes=[q3[1]])
                        mk.op(pool, lambda e: e.tensor_tensor(out=ysn[:, fc, :], in0=q2[0][:], in1=q3[0][:], op=ALU.add),
                              reads=[q2[1], q3[1]], writes=[ysn_b])
                    ity = 0
                    for cc in range(4):
                        bias = self.vec(V_HYB + (l * 2 + o) * 4 + cc)
                        for tt in range(ntile):
                            ts_ = slice(tt * N, (tt + 1) * N)
                            bk = 4 + ity % 2
                            t_, t_b = tz[ity % 2]
                            ity += 1
                            for fc in range(nf):
                                mk.op(pe, lambda e, fc=fc, bk=bk: e.matmul(ps[bk][:, 0:N], yre[:, fc, cc * 128:(cc + 1) * 128],
                                                                         ci[:, fc, ts_], start=(fc == 0), stop=False),
                                      reads=[yre_b, ci_b], writes=[psb[bk]], inc=False)
                                mk.op(pe, lambda e, fc=fc, bk=bk: e.matmul(ps[bk][:, 0:N], ysn[:, fc, cc * 128:(cc + 1) * 128],
                                                                         si[:, fc, ts_], start=False, stop=(fc == nf - 1)),
                                      reads=[ysn_b, si_b], writes=[psb[bk]], inc=(fc == nf - 1))
                            mk.op(dve, lambda e, bk=bk, t_=t_: e.scalar_tensor_tensor(
                                out=t_[:, 0:N], in0=zT[:, cc, ts_], scalar=bias, in1=ps[bk][:, 0:N], op0=ALU.mult, op1=ALU.add),
                                reads=[zT_bb[cc][tt], self.vec_b, psb[bk]], writes=[t_b])
                            if o == 0:
                                mk.op(pool, lambda e, t_=t_: e.tensor_tensor(out=zT[:, cc, ts_], in0=t_[:, 0:N],
                                                                            in1=xT[0][0][:, cc, ts_], op=ALU.mult),
                                      reads=[t_b, xT[0][1]], writes=[zT_bb[cc][tt]])
                            else:
                                mk.op(pool, lambda e, t_=t_: e.tensor_tensor(out=outT[:, cc, ts_], in0=t_[:, 0:N],
                                                                            in1=xT[1][0][:, cc, ts_], op=ALU.mult),
                                      reads=[t_b, xT[1][1]], writes=[outT_b])
                mk.dma(sp, self.catT_d[1536:2048, a:b_].rearrange("(c p) t -> p c t", p=128), outT[:], reads=[outT_b],
                       writes=[B["catT_d"]])
            mk.barrier()


Prog.phase_hy = _phase_hy


def _gemm_resid(self, l, wdram, nkc, actT, actT_b, gate_idx, xsrc, xsrc_b, es, wcols, wb=None, preloaded=False):
    mk = self.mk
    pe, act, dve, pool, sp = mk.pe, mk.act, mk.dve, mk.pool, mk.sp
    B = self.bufs
    ps, psb = self.ps, self.psb
    ntile = D // wcols
    per = wcols // 128
    if wb is None:
        wb = [self.sb(es, "gw%d" % i, [128, nkc, wcols], BF16) for i in range(2)]
    xb = [self.sb(es, "gx%d" % i, [128, T]) for i in range(2 if nkc <= 16 else 1)]
    rb = [self.sb(es, "gr%d" % i, [128, T]) for i in range(2 if nkc <= 16 else 1)]
    tb = [self.sb(es, "gt%d" % i, [128, 512]) for i in range(2)]

    def load_w(i):
        w, w_b = wb[i % 2]
        src = wdram.rearrange("(kc p) n -> p kc n", p=128)[:, :, i * wcols:(i + 1) * wcols]
        mk.dma(pool, w[:], src, writes=[w_b])

    if not preloaded:
        load_w(0)
    it = 0
    for i in range(ntile):
        if i + 1 < ntile:
            load_w(i + 1)
        w, w_b = wb[i % 2]
        for j in range(per):
            oc = i * per + j
            banks = (0, 1, 2) if oc % 2 == 0 else (3, 4, 5)
            x, x_b = xb[oc % len(xb)]
            r, r_b = rb[oc % len(rb)]
            mk.dma(sp, x[:], xsrc[oc * 128:(oc + 1) * 128, :], reads=[xsrc_b], writes=[x_b])
            for k in range(nkc):
                for tt in range(3):
                    mk.op(pe, lambda e, k=k, tt=tt: e.matmul(
                        ps[banks[tt]][:, :], w[:, k, j * 128:(j + 1) * 128], actT[:, k, tt * 512:(tt + 1) * 512],
                        start=(k == 0), stop=(k == nkc - 1)),
                        reads=[w_b, actT_b[k]], writes=[psb[banks[tt]]], inc=(k == nkc - 1))
            for tt in range(3):
                t_, t_b = tb[it % 2]
                it += 1
                cond = 0 if tt < 2 else 1
                ts_ = slice(tt * 512, (tt + 1) * 512)
                mk.op(act, lambda e, tt=tt, t_=t_, cond=cond: e.activation(
                    out=t_[:], in_=ps[banks[tt]][:, :], func=AF.Identity, scale=self.mod(l, gate_idx, oc, cond)),
                    reads=[psb[banks[tt]], self.modb(l, gate_idx)], writes=[t_b])
                mk.op(dve, lambda e, t_=t_, ts_=ts_: e.scalar_tensor_tensor(
                    out=r[:, ts_], in0=x[:, ts_], scalar=ALPHA, in1=t_[:], op0=ALU.mult, op1=ALU.add),
                    reads=[x_b, t_b], writes=[r_b])
            mk.dma(sp, self.rT_d[oc * 128:(oc + 1) * 128, :], r[:], reads=[r_b], writes=[B["rT_d"]])


Prog.gemm_resid = _gemm_resid


def _phase_out(self, l):
    mk = self.mk
    B = self.bufs
    self.ada_flush()
    with ExitStack() as es:
        catT, catT_b = self.sb(es, "o_catT", [128, KC, T], BF16)
        catT_b = self.load_kq(catT, KC, self.catT_d, B["catT_d"])
        xsrc, xsrc_b = (self.xT0, Buf("xT0")) if l == 0 else (self.xT_d, B["xT_d"])
        self.gemm_resid(l, self.w_out[l], KC, catT, catT_b, 2, xsrc, xsrc_b, es, 512)
        mk.barrier()


Prog.phase_out = _phase_out


def _phase_down(self, l, dwb=None):
    mk = self.mk
    B = self.bufs
    with ExitStack() as es:
        aT, aT_b = self.sb(es, "d_actT", [128, FC, T], BF16)
        aT_b = self.load_kq(aT, FC, self.actT_d, B["actT_d"], nq=11)
        self.gemm_resid(l, self.w_down[l], FC, aT, aT_b, 5, self.xT_d, B["xT_d"], es, 128,
                        wb=dwb, preloaded=dwb is not None)
        mk.barrier()


Prog.phase_down = _phase_down


def _phase_ln(self, l, which):
    mk = self.mk
    pe, act, dve, pool, sp = mk.pe, mk.act, mk.dve, mk.pool, mk.sp
    B = self.bufs
    ps, psb = self.ps, self.psb
    final = (which == 1 and l == DEPTH - 1)
    ones_d = self.onesd_bf
    with ExitStack() as es:
        rA, _ = self.sb(es, "n_r", [128, KC, T])
        rA_b = [Buf("n_r%d" % k) for k in range(KC)]
        r16 = [self.sb(es, "n_r16%d" % i, [128, T], BF16) for i in range(2)]
        sq = [self.sb(es, "n_sq%d" % i, [128, T], BF16) for i in range(2)]
        mean, mean_b = self.sb(es, "n_mean", [128, T])
        rstd, rstd_b = self.sb(es, "n_rstd", [128, T])
        AB, AB_b = self.sb(es, "n_AB", [128, 2, KC, 2])
        xo = [self.sb(es, "n_xo%d" % i, [128, T]) for i in range(2)]
        ho = [self.sb(es, "n_ho%d" % i, [128, T], BF16) for i in range(2)]
        for k in range(KC):
            mk.dma(sp, rA[:, k, :], self.rT_d[k * 128:(k + 1) * 128, :], reads=[B["rT_d"]], writes=[rA_b[k]])
        lw = lambda k: self.vec(V_LNP + (l * 4 + which * 2 + 0) * 16 + k)
        lb = lambda k: self.vec(V_LNP + (l * 4 + which * 2 + 1) * 16 + k)
        if not final:
            ml, sh_i, sc_i = (l, 3, 4) if which == 0 else (l + 1, 0, 1)
            lwv = self.vec(V_LNP + (l * 4 + which * 2 + 0) * 16, 16)
            lbv = self.vec(V_LNP + (l * 4 + which * 2 + 1) * 16, 16)
            for cond in range(2):
                sc = self.modT[:, ml, sc_i * 16:(sc_i + 1) * 16, cond]
                sh = self.modT[:, ml, sh_i * 16:(sh_i + 1) * 16, cond]
                mk.op(dve, lambda e, sc=sc, cond=cond: e.tensor_tensor(out=AB[:, 0, :, cond], in0=lwv, in1=sc, op=ALU.mult),
                      reads=[self.vec_b, self.modb(ml, sc_i)], writes=[AB_b])
                mk.op(dve, lambda e, sc=sc, cond=cond: e.tensor_tensor(out=AB[:, 1, :, cond], in0=lbv, in1=sc, op=ALU.mult),
                      reads=[self.vec_b, self.modb(ml, sc_i)], writes=[AB_b])
                mk.op(dve, lambda e, sh=sh, cond=cond: e.tensor_tensor(out=AB[:, 1, :, cond], in0=AB[:, 1, :, cond], in1=sh, op=ALU.add),
                      reads=[AB_b, self.modb(ml, sh_i)], writes=[AB_b])
        for k in range(KC):
            s_, s_b = sq[k % 2]
            c_, c_b = r16[k % 2]
            mk.op(act, lambda e, k=k, s_=s_: e.activation(out=s_[:], in_=rA[:, k, :], func=AF.Square),
                  reads=[rA_b[k]], writes=[s_b])
            mk.op(dve, lambda e, k=k, c_=c_: e.tensor_copy(c_[:], rA[:, k, :]), reads=[rA_b[k]], writes=[c_b])
            for tt in range(3):
                ts_ = slice(tt * 512, (tt + 1) * 512)
                mk.op(pe, lambda e, k=k, tt=tt, ts_=ts_, c_=c_: e.matmul(ps[tt][:, :], ones_d[:], c_[:, ts_],
                                                                       start=(k == 0), stop=(k == KC - 1)),
                      reads=[self.onesd_bf_b, c_b], writes=[psb[tt]], inc=(tt == 2 or k == KC - 1))
            for tt in range(3):
                ts_ = slice(tt * 512, (tt + 1) * 512)
                mk.op(pe, lambda e, k=k, tt=tt, ts_=ts_, s_=s_: e.matmul(ps[3 + tt][:, :], ones_d[:], s_[:, ts_],
                                                                       start=(k == 0), stop=(k == KC - 1)),
                      reads=[self.onesd_bf_b, s_b], writes=[psb[3 + tt]], inc=(tt == 2 or k == KC - 1))
        for tt in range(3):
            ts_ = slice(tt * 512, (tt + 1) * 512)
            mk.op(act, lambda e, tt=tt, ts_=ts_: e.activation(out=mean[:, ts_], in_=ps[tt][:, :], func=AF.Identity),
                  reads=[psb[tt]], writes=[mean_b])
            mk.op(dve, lambda e, ts_=ts_: e.tensor_tensor(out=rstd[:, ts_], in0=mean[:, ts_], in1=mean[:, ts_], op=ALU.mult),
                  reads=[mean_b], writes=[rstd_b])
            mk.op(dve, lambda e, tt=tt, ts_=ts_: e.tensor_tensor(out=rstd[:, ts_], in0=ps[3 + tt][:, :], in1=rstd[:, ts_],
                                                                op=ALU.subtract),
                  reads=[psb[3 + tt], rstd_b], writes=[rstd_b])
        mk.op(act, lambda e: e.activation(out=rstd[:], in_=rstd[:], func=AF.Sqrt, bias=self.vec(V_EPS), scale=1.0),
              reads=[rstd_b, self.vec_b], writes=[rstd_b])
        mk.op(dve, lambda e: e.reciprocal(rstd[:], rstd[:]), reads=[rstd_b], writes=[rstd_b])
        dst, dst_b = (self.yT, B["yT"]) if final else (self.xT_d, B["xT_d"])
        for k in range(KC):
            mk.op(dve, lambda e, k=k: e.tensor_tensor(out=rA[:, k, :], in0=rA[:, k, :], in1=mean[:], op=ALU.subtract),
                  reads=[rA_b[k], mean_b], writes=[rA_b[k]])
            mk.op(dve, lambda e, k=k: e.tensor_tensor(out=rA[:, k, :], in0=rA[:, k, :], in1=rstd[:], op=ALU.mult),
                  reads=[rA_b[k], rstd_b], writes=[rA_b[k]])
            x_, x_b = xo[k % 2]
            mk.op(act, lambda e, k=k, x_=x_: e.activation(out=x_[:], in_=rA[:, k, :], func=AF.Identity,
                                                         scale=lw(k), bias=lb(k)),
                  reads=[rA_b[k], self.vec_b], writes=[x_b])
            mk.dma(sp, dst[k * 128:(k + 1) * 128, :], x_[:], reads=[x_b], writes=[dst_b])
            if not final:
                h_, h_b = ho[k % 2]
                for (s, e_, cond) in ((0, 1024, 0), (1024, 1536, 1)):
                    mk.op(act, lambda e, k=k, s=s, e_=e_, cond=cond, h_=h_: e.activation(
                        out=h_[:, s:e_], in_=rA[:, k, s:e_], func=AF.Identity,
                        scale=AB[:, 0, k, cond:cond + 1], bias=AB[:, 1, k, cond:cond + 1]),
                        reads=[rA_b[k], AB_b], writes=[h_b])
                mk.dma(sp, self.hT_d[k * 128:(k + 1) * 128, :], h_[:], reads=[h_b], writes=[B["hT_d"]])
        mk.barrier()


Prog.phase_ln = _phase_ln


def _phase_up(self, l, after=None):
    mk = self.mk
    pe, act, dve, pool, sp = mk.pe, mk.act, mk.dve, mk.pool, mk.sp
    B = self.bufs
    ps, psb = self.ps, self.psb
    WC = 256
    with ExitStack() as es:
        hT, hT_b = self.sb(es, "u_hT", [128, KC, T], BF16)
        wg = [self.sb(es, "u_wg%d" % i, [128, KC, WC], BF16) for i in range(2)]
        wu = [self.sb(es, "u_wu%d" % i, [128, KC, WC], BF16) for i in range(2)]
        raw = [self.sb(es, "u_raw%d" % i, [128, T]) for i in range(2)]
        go = [self.sb(es, "u_go%d" % i, [128, T]) for i in range(2)]
        uo = [self.sb(es, "u_uo%d" % i, [128, T]) for i in range(2)]
        ab = [self.sb(es, "u_ab%d" % i, [128, T], BF16) for i in range(2)]
        hT_kb = self.load_kq(hT, KC, self.hT_d, B["hT_d"])
        wsrc = self.w_up[l].rearrange("(kc p) n -> p kc n", p=128)
        ntile = DFF // WC

        def load_w(i):
            mk.dma(pool, wg[i % 2][0][:], wsrc[:, :, i * WC:(i + 1) * WC], writes=[wg[i % 2][1]])
            mk.dma(pool, wu[i % 2][0][:], wsrc[:, :, DFF + i * WC:DFF + (i + 1) * WC], writes=[wu[i % 2][1]])

        cnt = {"set": 0, "raw": 0}

        def chunk(w, w_b, col0, tapchunk, o, o_b):
            banks = (0, 1, 2) if cnt["set"] % 2 == 0 else (3, 4, 5)
            cnt["set"] += 1
            for k in range(KC):
                for tt in range(3):
                    mk.op(pe, lambda e, k=k, tt=tt: e.matmul(
                        ps[banks[tt]][:, :], w[:, k, col0:col0 + 128], hT[:, k, tt * 512:(tt + 1) * 512],
                        start=(k == 0), stop=(k == KC - 1)),
                        reads=[w_b, hT_kb[k]], writes=[psb[banks[tt]]], inc=(k == KC - 1))
            r_, r_b = raw[cnt["raw"] % 2]
            cnt["raw"] += 1
            for tt in range(3):
                mk.op(act, lambda e, tt=tt: e.activation(out=r_[:, tt * 512:(tt + 1) * 512], in_=ps[banks[tt]][:, :],
                                                         func=AF.Identity),
                      reads=[psb[banks[tt]]], writes=[r_b])
            self.conv3(r_, r_b, o, o_b, lambda tap: self.vec(V_FCV + (l * 3 + tap) * 88 + tapchunk))

        load_w(0)
        for i in range(ntile):
            if i + 1 < ntile:
                load_w(i + 1)
            for j in range(WC // 128):
                fc = i * (WC // 128) + j
                g_, g_b = go[fc % 2]
                u_, u_b = uo[fc % 2]
                a_, a_b = ab[fc % 2]
                chunk(wg[i % 2][0], wg[i % 2][1], j * 128, fc, g_, g_b)
                chunk(wu[i % 2][0], wu[i % 2][1], j * 128, FC + fc, u_, u_b)
                mk.op(act, lambda e, g_=g_: e.activation(out=g_[:], in_=g_[:], func=AF.Silu), reads=[g_b], writes=[g_b])
                mk.op(dve, lambda e, g_=g_, u_=u_, a_=a_: e.tensor_tensor(out=a_[:], in0=g_[:], in1=u_[:], op=ALU.mult),
                      reads=[g_b, u_b], writes=[a_b])
                mk.dma(sp, self.actT_d[fc * 128:(fc + 1) * 128, :], a_[:], reads=[a_b], writes=[B["actT_d"]])
        if after is not None:
            after()
        mk.barrier()


Prog.phase_up = _phase_up


def _phase_ffn(self, l):
    mk = self.mk
    with ExitStack() as esw:
        dwb = [self.sb(esw, "dw%d" % i, [128, FC, 128], BF16) for i in range(2)]

        def prefetch():
            src = self.w_down[l].rearrange("(kc p) n -> p kc n", p=128)[:, :, 0:128]
            mk.dma(mk.pool, dwb[0][0][:], src, writes=[dwb[0][1]])

        self.phase_up(l, after=prefetch)
        self.phase_down(l, dwb)


Prog.phase_ffn = _phase_ffn
```

```python
import math
from contextlib import ExitStack
import numpy as np
import ml_dtypes
import concourse.bass as bass
import concourse.mybir as mybir
from concourse.bass_utils import run_bass_kernel_spmd

F32 = mybir.dt.float32
BF16 = mybir.dt.bfloat16
AF = mybir.ActivationFunctionType
ALU = mybir.AluOpType

NCORES = 8
D = 2048
KC = 16
DEPTH = 2
LAT = 1024
CTX = 256
T = 1536
NT = 12
SEGS = ((0, 1024, 0), (1024, 1280, 1), (1280, 1536, 1))
HD = 128
INW = 5120
DFF = 5632
FC = 44
PAST = 512
EPS = 1e-6
ALPHA = (2 * DEPTH) ** 0.25
GRID_W = 64
PI = math.pi


class Buf:
    __slots__ = ("name", "w", "r", "excl")

    def __init__(self, name, excl=False):
        self.name = name
        self.w = None
        self.r = {}
        self.excl = excl


class Eng:
    def __init__(self, name, h, sem, is_pe=False):
        self.name = name
        self.h = h
        self.sem = sem
        self.key = name
        self.count = 0
        self.seen = {}
        self.pending = False
        self.is_pe = is_pe
        self.slots = []
        self.slot_i = 0
        self.st_slots = []
        self.st_i = 0


class MK:
    def __init__(self, nc, es):
        self.nc = nc
        self.sems = {}
        self.engs = {}
        for name, h in (("pe", nc.tensor), ("act", nc.scalar), ("dve", nc.vector),
                        ("pool", nc.gpsimd), ("sp", nc.sync)):
            sem = es.enter_context(nc.semaphore("s_" + name))
            e = Eng(name, h, sem, is_pe=(name == "pe"))
            self.engs[name] = e
            self.sems[name] = sem
        for qn, nslots in (("sp", 16), ("pool", 8), ("act", 2)):
            e = self.engs[qn]
            for i in range(nslots):
                key = "d_%s%d" % (qn, i)
                sem = es.enter_context(nc.semaphore(key))
                self.sems[key] = sem
                e.slots.append([key, sem, 0])
            for i in range(nslots if qn == "sp" else 0):
                key = "s_%s%d" % (qn, i)
                sem = es.enter_context(nc.semaphore(key))
                self.sems[key] = sem
                e.st_slots.append([key, sem, 0])
        self.pe = self.engs["pe"]
        self.act = self.engs["act"]
        self.dve = self.engs["dve"]
        self.pool = self.engs["pool"]
        self.sp = self.engs["sp"]
        self.n_instr = 0

    def _wait(self, E, key, val):
        if E.seen.get(key, 0) >= val:
            return
        E.h.wait_ge(self.sems[key], val)
        E.seen[key] = val

    def _sync(self, E, reads, writes):
        need = {}
        for b in reads:
            if b.w is not None:
                k, v = b.w
                if v > need.get(k, 0):
                    need[k] = v
            if b.excl:
                for k, v in b.r.items():
                    if k != E.key and v > need.get(k, 0):
                        need[k] = v
        for b in writes:
            if b.w is not None:
                k, v = b.w
                if not (k == E.key):
                    if v > need.get(k, 0):
                        need[k] = v
            for k, v in b.r.items():
                if k == E.key:
                    continue
                if v > need.get(k, 0):
                    need[k] = v
        for k, v in need.items():
            if k == E.key and E.is_pe:
                continue
            self._wait(E, k, v)

    def op(self, E, fn, reads=(), writes=(), inc=True):
        self._sync(E, reads, writes)
        ins = fn(E.h)
        val = E.count + 1
        if inc:
            ins.then_inc(E.sem, 1)
            E.count = val
            E.pending = False
        else:
            E.pending = True
        for b in reads:
            if val > b.r.get(E.key, 0):
                b.r[E.key] = val
        for b in writes:
            b.w = (E.key, val)
            b.r = {}
        self.n_instr += 1
        return ins

    def dma(self, E, out, in_, reads=(), writes=(), **kw):
        is_store = str(out.space).upper().find("DRAM") >= 0
        if is_store and E.st_slots:
            slot = E.st_slots[E.st_i % len(E.st_slots)]
            E.st_i += 1
        else:
            slot = E.slots[E.slot_i % len(E.slots)]
            E.slot_i += 1
        self._wait(E, slot[0], slot[2])
        self._sync(E, reads, writes)
        ins = E.h.dma_start(out=out, in_=in_, **kw)
        ins.then_inc(slot[1], 16)
        slot[2] += 16
        val = slot[2]
        for b in reads:
            if val > b.r.get(slot[0], 0):
                b.r[slot[0]] = val
        for b in writes:
            b.w = (slot[0], val)
            b.r = {}
        self.n_instr += 1
        return ins

    def barrier(self):
        targets = {}
        for e in self.engs.values():
            assert not e.pending, "unsignalled instruction pending on " + e.name
            targets[e.key] = e.count
            for s in e.slots + e.st_slots:
                targets[s[0]] = s[2]
        for e in self.engs.values():
            for k, v in targets.items():
                if v > 0:
                    self._wait(e, k, v)

    def finish(self):
        self.barrier()


class Prog:
    def __init__(self, debug=False, stop_after=None):
        self.debug = debug
        self.stop_after = stop_after
        self.nc = bass.Bass("TRN2", target_bir_lowering=False)
        self.bg_hook = None
        self.in_names = []
        self.out_names = []
        self.dbg_names = []

    def din(self, name, shape, dt=F32):
        self.in_names.append(name)
        return self.nc.dram_tensor(name, list(shape), dt, kind="ExternalInput").ap()

    def dout(self, name, shape, dt=F32):
        self.out_names.append(name)
        return self.nc.dram_tensor(name, list(shape), dt, kind="ExternalOutput").ap()

    def dscr(self, name, shape, dt=F32):
        if self.debug:
            self.dbg_names.append(name)
            return self.nc.dram_tensor(name, list(shape), dt, kind="ExternalOutput").ap()
        return self.nc.dram_tensor(name, list(shape), dt, kind="Internal").ap()

    def build(self):
        nc = self.nc
        with ExitStack() as es:
            self.es = es
            mk = self.mk = MK(nc, es)
            self.declare()
            self.ps = []
            self.psb = []
            for i in range(8):
                t = es.enter_context(nc.psum_tensor("ps%d" % i, [128, 512], F32))
                self.ps.append(t)
                self.psb.append(Buf("ps%d" % i, excl=True))
            self.load_consts()
            self.phase_ada()
            done = self.stop_after == "ada"
            if done:
                self.ada_flush()
                mk.dma(mk.sp, self.mod_d, self.modT[:], writes=[self.bufs["mod_d"]])
            for l in range(DEPTH):
                if done:
                    break
                self.phase_in(l)
                if self.stop_after == "in%d" % l:
                    break
                self.phase_att(l)
                if self.stop_after == "att%d" % l:
                    break
                self.phase_ret(l)
                if self.stop_after == "ret%d" % l:
                    break
                self.phase_hy(l)
                if self.stop_after == "hy%d" % l:
                    break
                self.phase_out(l)
                self.phase_ln(l, 0)
                if self.stop_after == "ln1_%d" % l:
                    break
                self.phase_ffn(l)
                self.phase_ln(l, 1)
            mk.finish()
        return nc

    def declare(self):
        self.xT0 = self.din("xT0", [D, T])
        self.cT = self.din("cT", [128, KC, 2])
        self.w_ada = self.din("w_ada", [DEPTH, D, 6 * D])
        self.w_in = self.din("w_in", [DEPTH, D, INW])
        self.w_out = self.din("w_out", [DEPTH, D, D])
        self.w_up = self.din("w_up", [DEPTH, D, 2 * DFF])
        self.w_down = self.din("w_down", [DEPTH, DFF, D])
        self.vecs = self.din("vecs", [128, NVEC])
        self.cmats = self.din("cmats", [128, NCM * 128])
        self.rope = self.din("rope", [128, 2, LAT])
        self.cache_kT = self.din("cache_kT", [DEPTH, 2, 128, PAST])
        self.cache_v = self.din("cache_v", [DEPTH, 2, PAST, 128])
        self.state_ret = self.din("state_ret", [DEPTH, 2, 4, 128, 128])
        self.yT = self.dout("yT", [D, T])
        self.nck = self.dout("nck", [2, DEPTH, 2, CTX, HD])
        self.ncv = self.dout("ncv", [2, DEPTH, 2, CTX, HD])
        self.nsr = self.dout("nsr", [2, DEPTH, 2, 4, HD, HD])
        self.mod_d = self.dscr("mod_d", [128, DEPTH, 96, 2])
        self.xT_d = self.dscr("xT_d", [D, T])
        self.hT_d = self.dscr("hT_d", [D, T], BF16)
        self.qT_d = self.dscr("qT_d", [8, 128, T], BF16)
        self.kT_d = self.dscr("kT_d", [2, 128, T], BF16)
        self.v_d = self.dscr("v_d", [T, 256], BF16)
        self.rqT_d = self.dscr("rqT_d", [4, 128, T], BF16)
        self.rkT_d = self.dscr("rkT_d", [4, 128, T], BF16)
        self.rk_d = self.dscr("rk_d", [T, 512], BF16)
        self.rv_d = self.dscr("rv_d", [T, 512], BF16)
        self.rg_d = self.dscr("rg_d", [T, 512])
        self.hyT_d = self.dscr("hyT_d", [1536, T])
        self.catT_d = self.dscr("catT_d", [D, T], BF16)
        self.rT_d = self.dscr("rT_d", [D, T])
        self.actT_d = self.dscr("actT_d", [DFF, T], BF16)
        self.bufs = {}
        self.declare_hy()
        for n in ("mod_d", "xT_d", "hT_d", "qT_d", "kT_d", "v_d", "rqT_d", "rkT_d", "rk_d", "rv_d",
                  "rg_d", "hyT_d", "catT_d", "rT_d", "actT_d", "yT", "nck", "ncv", "nsr"):
            self.bufs[n] = Buf(n)

    def sb(self, es, name, shape, dt=F32):
        self._uid = getattr(self, "_uid", 0) + 1
        name = "%s_u%d" % (name, self._uid)
        t = es.enter_context(self.nc.sbuf_tensor(name, list(shape), dt))
        return t, Buf(name)

    def load_consts(self):
        mk = self.mk
        es = self.es
        self.vec_sb, self.vec_b = self.sb(es, "vec_sb", [128, NVEC])
        self.cm_sb, self.cm_b = self.sb(es, "cm_sb", [128, NCM * 128])
        self.modT, self.modT_b = self.sb(es, "modT", [128, DEPTH, 96, 2])
        self.ones_bf, self.ones_bf_b = self.sb(es, "ones_bf", [128, 128], BF16)
        mk.dma(mk.sp, self.vec_sb[:], self.vecs, writes=[self.vec_b])
        mk.dma(mk.sp, self.cm_sb[:], self.cmats, writes=[self.cm_b])
        mk.op(mk.dve, lambda e: e.memset(self.ones_bf[:], 1.0), writes=[self.ones_bf_b])
        self.oneshd_bf, self.oneshd_bf_b = self.sb(es, "oneshd_bf", [128, 128], BF16)
        mk.op(mk.dve, lambda e: e.memset(self.oneshd_bf[:], 1.0 / HD), writes=[self.oneshd_bf_b])
        self.onesd_bf, self.onesd_bf_b = self.sb(es, "onesd_bf", [128, 128], BF16)
        mk.op(mk.dve, lambda e: e.memset(self.onesd_bf[:], 1.0 / D), writes=[self.onesd_bf_b])

    def load_kq(self, dst, nk, src, src_b, nq=4):
        mk = self.mk
        bufs = []
        per = nk // nq
        for q in range(nq):
            b = Buf("kq%d" % q)
            mk.dma(mk.sp, dst[:, q * per:(q + 1) * per, :],
                   src[q * per * 128:(q + 1) * per * 128, :].rearrange("(kc p) t -> p kc t", p=128),
                   reads=[src_b], writes=[b])
            bufs.extend([b] * per)
        return bufs

    def vec(self, off, n=1):
        return self.vec_sb[:, off:off + n]

    def cmat(self, i):
        return self.cm_sb[:, i * 128:(i + 1) * 128]


V_QKN = 0
V_HYC = V_QKN + 4
V_FCV = V_HYC + 72
V_LNP = V_FCV + 528
V_HYB = V_LNP + 128
V_RETD = V_HYB + 16
V_HF = V_RETD + 16
V_JJ = V_HF + 6
V_BADA = V_JJ + 2
V_EPS = V_BADA + 192
NVEC = V_EPS + 4

CM_ID, CM_ONES_HD, CM_PSW, CM_ONES_D, CM_DPOS, CM_MF, CM_DNEG, CM_MB, CM_IP1, CM_CMI = range(10)
NCM = 10


def _ada_tile(self, l, nt, w, w_b, row, row_b):
    mk = self.mk
    pe, act, dve, pool = mk.pe, mk.act, mk.dve, mk.pool
    ident = self.cmat(CM_ID)
    ps7, ps7_b = self.ps[7], self.psb[7]
    sT, sT_b = self.siluT, self.siluT_b
    src = self.w_ada[l].rearrange("(kc p) n -> p kc n", p=128)[:, :, nt * 512:(nt + 1) * 512]
    mk.dma(pool, w[:], src, writes=[w_b])
    for j in range(4):
        for k in range(KC):
            mk.op(pe, lambda e, k=k, j=j: e.matmul(ps7[:, j * 2:(j + 1) * 2], w[:, k, j * 128:(j + 1) * 128], sT[:, k, :],
                                                   start=(k == 0), stop=(k == KC - 1)),
                  reads=[sT_b, w_b], writes=[ps7_b], inc=(k == KC - 1 and j == 3))
    which = nt // 4
    c0 = nt * 4
    bada = self.vec(V_BADA + l * 96 + c0, 4)
    plus = 1.0 if which in (1, 4) else 0.0
    mk.op(dve, lambda e: e.scalar_tensor_tensor(
        out=self.modT[:, l, c0:c0 + 4, :], in0=ps7[:, 0:8].rearrange("p (c j) -> p c j", j=2), scalar=plus,
        in1=bada.unsqueeze(2).broadcast_to([128, 4, 2]), op0=ALU.add, op1=ALU.add),
        reads=[ps7_b, self.vec_b], writes=[self.modb(l, which)])


Prog.ada_tile = _ada_tile


def _ada_bufs(self, es, n=2):
    self._ada_w = [self.sb(es, "adaw%d" % i, [128, KC, 512], BF16) for i in range(n)]
    self._ada_row = [self.sb(es, "modrow%d" % i, [2, 512]) for i in range(n)]


def _ada_bg(self, n=1):
    for _ in range(n):
        if not self.ada_pending:
            return
        l, nt = self.ada_pending.pop(0)
        i = self._ada_i = getattr(self, "_ada_i", 0) + 1
        w, w_b = self._ada_w[i % len(self._ada_w)]
        row, row_b = self._ada_row[i % len(self._ada_row)]
        self.ada_tile(l, nt, w, w_b, row, row_b)


def _ada_flush(self):
    if not self.ada_pending:
        return
    with ExitStack() as es:
        self.ada_bufs(es)
        self.ada_bg(len(self.ada_pending))
        self.mk.barrier()


Prog.ada_bufs = _ada_bufs
Prog.ada_bg = _ada_bg
Prog.ada_flush = _ada_flush


def _phase_ada(self):
    mk = self.mk
    act, sp = mk.act, mk.sp
    self.siluT, self.siluT_b = self.sb(self.es, "siluT", [128, KC, 2], BF16)
    self.ada_pending = [(0, nt) for nt in range(24)] + [(1, nt) for nt in range(24)]
    for l in range(DEPTH):
        self.ret_tables(l)
    with ExitStack() as es:
        cT_sb, cT_b = self.sb(es, "cT_sb", [128, KC, 2])
        self.ada_bufs(es, n=1)
        m0_load, m0_compute = self.phase_mod0(es)
        mk.dma(sp, cT_sb[:], self.cT, writes=[cT_b])
        mk.op(act, lambda e: e.activation(out=self.siluT[:], in_=cT_sb[:], func=AF.Silu),
              reads=[cT_b], writes=[self.siluT_b])
        left = {"n": 8, "k": 0}

        def hook():
            if left["n"] > 0:
                left["n"] -= 1
                self.ada_bg(1)
                if left["n"] == 0:
                    m0_load(0)
                    m0_load(1)
            elif left["k"] < KC:
                k = left["k"]
                left["k"] += 1
                m0_load(k + 2)
                m0_compute(k)

        hook()
        hook()
        self.bg_hook = hook
        self.hy_filters()
        self.bg_hook = None
        while left["n"] > 0 or left["k"] < KC:
            hook()
        mk.barrier()


Prog.phase_ada = _phase_ada


def _modb(self, l, which):
    if not hasattr(self, "_modbufs"):
        self._modbufs = {}
    key = (l, which)
    if key not in self._modbufs:
        self._modbufs[key] = Buf("mod_%d_%d" % key)
    return self._modbufs[key]


Prog.modb = _modb


def _mod(self, l, which, k, cond):
    return self.modT[:, l, which * 16 + k, cond:cond + 1]


Prog.mod = _mod


def _phase_mod0(self, es):
    mk = self.mk
    act, sp = mk.act, mk.sp
    xin = [self.sb(es, "m0x%d" % i, [128, T]) for i in range(3)]
    ho = [self.sb(es, "m0h%d" % i, [128, T], BF16) for i in range(2)]

    def load(k):
        if k < KC:
            x, x_b = xin[k % 3]
            mk.dma(sp, x[:], self.xT0[k * 128:(k + 1) * 128, :], writes=[x_b])

    def compute(k):
        x, x_b = xin[k % 3]
        h, h_b = ho[k % 2]
        for (s, e_, cond) in ((0, 1024, 0), (1024, 1536, 1)):
            mk.op(act, lambda e, s=s, e_=e_, cond=cond: e.activation(
                out=h[:, s:e_], in_=x[:, s:e_], func=AF.Identity,
                scale=self.mod(0, 1, k, cond), bias=self.mod(0, 0, k, cond)),
                reads=[x_b, self.modb(0, 0), self.modb(0, 1)], writes=[h_b])
        mk.dma(sp, self.hT_d[k * 128:(k + 1) * 128, :], h[:], reads=[h_b], writes=[self.bufs["hT_d"]])
    return load, compute


Prog.phase_mod0 = _phase_mod0


def _phase_in(self, l):
    mk = self.mk
    pe, act, dve, pool, sp = mk.pe, mk.act, mk.dve, mk.pool, mk.sp
    B = self.bufs
    ps, psb = self.ps, self.psb
    with ExitStack() as es:
        hT, hT_b = self.sb(es, "hT", [128, KC, T], BF16)
        wb = [self.sb(es, "inw%d" % i, [128, KC, 512], BF16) for i in range(2)]
        ropeT, rope_b = self.sb(es, "ropeT", [128, 2, LAT])
        hob = [self.sb(es, "hob%d" % i, [128, T], BF16) for i in range(2)]
        sq3 = [self.sb(es, "sq%d" % i, [128, 512], BF16) for i in range(3)]
        rstd3 = [self.sb(es, "rstd%d" % i, [128, 512]) for i in range(3)]
        qn3 = [self.sb(es, "qn%d" % i, [128, 512]) for i in range(6)]
        t1 = [self.sb(es, "t1_%d" % i, [128, 512]) for i in range(2)]
        t2 = [self.sb(es, "t2_%d" % i, [128, 512]) for i in range(2)]
        hyraw = [self.sb(es, "hyraw%d" % i, [128, T]) for i in range(2)]
        hyo = [self.sb(es, "hyo%d" % i, [128, T]) for i in range(2)]
        tmb = [self.sb(es, "tmb%d" % i, [128, 512], BF16) for i in range(2)]
        tmf = [self.sb(es, "tmf%d" % i, [128, 512]) for i in range(2)]
        kct = [self.sb(es, "kct%d" % i, [128, 128]) for i in range(2)]
        cnt = {"e": 0, "tm": 0, "hy": 0, "set": 0}

        hT_kb = self.load_kq(hT, KC, self.hT_d, B["hT_d"])
        mk.dma(sp, ropeT[:], self.rope, writes=[rope_b])

        wqb = [[Buf("inw%d_q%d" % (i_, q_)) for q_ in range(4)] for i_ in range(2)]
        for i_ in range(2):
            wb[i_] = (wb[i_][0], wqb[i_])

        def load_w(i):
            w, w_b = wb[i % 2]
            src = self.w_in[l].rearrange("(kc p) n -> p kc n", p=128)[:, :, i * 512:(i + 1) * 512]
            for q in range(4):
                mk.dma(pool, w[:, q * 4:(q + 1) * 4, :], src[:, q * 4:(q + 1) * 4, :], writes=[w_b[q]])

        def fm_chunk(w, w_b, col0):
            banks = (0, 1, 2) if cnt["set"] % 2 == 0 else (3, 4, 5)
            cnt["set"] += 1
            for k in range(KC):
                for tt in range(3):
                    mk.op(pe, lambda e, k=k, tt=tt: e.matmul(
                        ps[banks[tt]][:, :], w[:, k, col0:col0 + 128], hT[:, k, tt * 512:(tt + 1) * 512],
                        start=(k == 0), stop=(k == KC - 1)),
                        reads=[w_b[k // 4], hT_kb[k]], writes=[psb[banks[tt]]], inc=(k == KC - 1))
            return banks

        def tm_tile(w, w_b, col0, ncols, t):
            bank = 6 + (cnt["tm"] % 2)
            cnt["tm"] += 1
            for k in range(KC):
                mk.op(pe, lambda e, k=k: e.matmul(
                    ps[bank][:, 0:ncols], hT[:, k, t * 128:(t + 1) * 128], w[:, k, col0:col0 + ncols],
                    start=(k == 0), stop=(k == KC - 1)),
                    reads=[w_b[k // 4], hT_kb[k]], writes=[psb[bank]], inc=(k == KC - 1))
            return bank

        pend = {"s1": None, "s2": None}

        def qk_stage1(st):
            banks, wvec = st["banks"], st["wvec"]
            st["qn"] = []
            sqs = []
            for tt in range(3):
                bk = banks[tt]
                sq_, sq_b = self.sb(es, "sqp", [128, 512]) if False else sq3[(st["idx"] * 3 + tt) % len(sq3)]
                mk.op(act, lambda e, bk=bk, sq_=sq_: e.activation(out=sq_[:], in_=ps[bk][:, :], func=AF.Square),
                      reads=[psb[bk]], writes=[sq_b])
                sqs.append((sq_, sq_b))
            for tt in range(3):
                bk = banks[tt]
                sq_, sq_b = sqs[tt]
                r_, r_b = rstd3[(st["idx"] * 3 + tt) % len(rstd3)]
                q_, q_b = qn3[(st["idx"] * 3 + tt) % len(qn3)]
                mk.op(pe, lambda e, sq_=sq_: e.matmul(ps[6][:, :], self.oneshd_bf[:], sq_[:], start=True, stop=True),
                      reads=[self.oneshd_bf_b, sq_b], writes=[psb[6]])
                mk.op(act, lambda e, r_=r_: e.activation(out=r_[:], in_=ps[6][:, :], func=AF.Sqrt,
                                                         bias=self.vec(V_EPS), scale=1.0),
                      reads=[psb[6], self.vec_b], writes=[r_b])
                mk.op(dve, lambda e, r_=r_: e.reciprocal(r_[:], r_[:]), reads=[r_b], writes=[r_b])
                mk.op(dve, lambda e, bk=bk, r_=r_, q_=q_: e.scalar_tensor_tensor(
                    out=q_[:], in0=ps[bk][:, :], scalar=wvec, in1=r_[:], op0=ALU.mult, op1=ALU.mult),
                    reads=[psb[bk], r_b, self.vec_b], writes=[q_b])
                st["qn"].append((q_, q_b))

        def qk_stage2(st):
            ho, ho_b = hob[st["idx"] % 2]
            is_k, kvh = st["is_k"], st["kvh"]
            for tt in range(3):
                q_, q_b = st["qn"][tt]
                i = cnt["e"] % 2
                cnt["e"] += 1
                if tt < 2:
                    mk.op(pe, lambda e, q_=q_: e.matmul(ps[7][:, :], self.cmat(CM_PSW), q_[:], start=True, stop=True),
                          reads=[self.cm_b, q_b], writes=[psb[7]])
                    mk.op(dve, lambda e, i=i, tt=tt, q_=q_: e.tensor_tensor(
                        out=t1[i][0][:], in0=q_[:], in1=ropeT[:, 0, tt * 512:(tt + 1) * 512], op=ALU.mult),
                        reads=[q_b, rope_b], writes=[t1[i][1]])
                    mk.op(dve, lambda e, i=i, tt=tt: e.tensor_tensor(
                        out=t2[i][0][:], in0=ps[7][:, :], in1=ropeT[:, 1, tt * 512:(tt + 1) * 512], op=ALU.mult),
                        reads=[psb[7], rope_b], writes=[t2[i][1]])
                    mk.op(dve, lambda e, i=i, tt=tt: e.tensor_tensor(
                        out=ho[:, tt * 512:(tt + 1) * 512], in0=t1[i][0][:], in1=t2[i][0][:], op=ALU.add),
                        reads=[t1[i][1], t2[i][1]], writes=[ho_b])
                else:
                    mk.op(act, lambda e, q_=q_: e.activation(out=ho[:, 1024:1536], in_=q_[:], func=AF.Identity),
                          reads=[q_b], writes=[ho_b])
                    if is_k:
                        for j in range(4):
                            c, c_b = kct[j % 2]
                            mk.op(pe, lambda e, j=j, q_=q_: e.matmul(
                                ps[7][:, 0:128], q_[:, j * 128:(j + 1) * 128], self.cmat(CM_ID),
                                start=True, stop=True),
                                reads=[q_b, self.cm_b], writes=[psb[7]])
                            mk.op(act, lambda e, c=c: e.activation(out=c[:], in_=ps[7][:, 0:128], func=AF.Identity),
                                  reads=[psb[7]], writes=[c_b])
                            mk.dma(sp, self.nck[j // 2, l, kvh, (j % 2) * 128:(j % 2 + 1) * 128, :], c[:],
                                   reads=[c_b], writes=[B["nck"]])
            mk.dma(sp, st["dst"], ho[:], reads=[ho_b], writes=[st["dst_b"]])

        def qk_advance(new_st):
            if pend["s1"] is not None:
                qk_stage1(pend["s1"])
            if pend["s2"] is not None:
                qk_stage2(pend["s2"])
            pend["s2"] = pend["s1"]
            pend["s1"] = new_st

        def qk_drain():
            qk_advance(None)
            qk_advance(None)

        def qk_head(w, w_b, col0, wvec, dst, dst_b, is_k, kvh):
            banks = fm_chunk(w, w_b, col0)
            cnt["qk"] = cnt.get("qk", 0) + 1
            qk_advance({"banks": banks, "wvec": wvec, "dst": dst, "dst_b": dst_b, "is_k": is_k, "kvh": kvh,
                        "idx": cnt["qk"]})

        def copy_head(w, w_b, col0, scale, dst, dst_b):
            banks = fm_chunk(w, w_b, col0)
            ho, ho_b = hob[cnt["e"] % 2]
            cnt["e"] += 1
            for tt in range(3):
                bk = banks[tt]
                mk.op(act, lambda e, bk=bk, tt=tt: e.activation(
                    out=ho[:, tt * 512:(tt + 1) * 512], in_=ps[bk][:, :], func=AF.Identity, scale=scale),
                    reads=[psb[bk]], writes=[ho_b])
            mk.dma(sp, dst, ho[:], reads=[ho_b], writes=[dst_b])

        def hy_chunk(w, w_b, col0, ch):
            banks = fm_chunk(w, w_b, col0)
            i = cnt["hy"] % 2
            cnt["hy"] += 1
            raw, raw_b = hyraw[i]
            o, o_b = hyo[i]
            for tt in range(3):
                bk = banks[tt]
                mk.op(act, lambda e, bk=bk, tt=tt: e.activation(
                    out=raw[:, tt * 512:(tt + 1) * 512], in_=ps[bk][:, :], func=AF.Identity),
                    reads=[psb[bk]], writes=[raw_b])
            self.conv3(raw, raw_b, o, o_b, lambda tap: self.vec(V_HYC + (l * 3 + tap) * 12 + ch))
            mk.dma(sp, self.hyT_d[ch * 128:(ch + 1) * 128, :], o[:], reads=[o_b], writes=[B["hyT_d"]])

        def tm_group(w, w_b, col0, ncols, kind):
            for t in range(NT):
                bank = tm_tile(w, w_b, col0, ncols, t)
                i = cnt["e"] % 2
                cnt["e"] += 1
                rows = slice(t * 128, (t + 1) * 128)
                if kind == "v":
                    ob, ob_b = tmb[i]
                    mk.op(act, lambda e: e.activation(out=ob[:, 0:256], in_=ps[bank][:, 0:256], func=AF.Identity),
                          reads=[psb[bank]], writes=[ob_b])
                    mk.dma(sp, self.v_d[rows, :], ob[:, 0:256], reads=[ob_b], writes=[B["v_d"]])
                    if t >= 8:
                        of, of_b = tmf[i]
                        mk.op(dve, lambda e: e.tensor_copy(of[:, 0:256], ps[bank][:, 0:256]),
                              reads=[psb[bank]], writes=[of_b])
                        seq, half = (t - 8) // 2, (t - 8) % 2
                        mk.dma(sp, self.ncv[seq, l, :, half * 128:(half + 1) * 128, :].rearrange("kv p d -> p kv d"),
                               of[:, 0:256].rearrange("p (kv d) -> p kv d", kv=2),
                               reads=[of_b], writes=[B["ncv"]])
                elif kind == "rk":
                    ob, ob_b = tmb[i]
                    mk.op(act, lambda e: e.activation(out=ob[:], in_=ps[bank][:, :], func=AF.Identity, scale=HD ** -0.5),
                          reads=[psb[bank]], writes=[ob_b])
                    mk.dma(sp, self.rk_d[rows, :], ob[:], reads=[ob_b], writes=[B["rk_d"]])
                elif kind == "rv":
                    ob, ob_b = tmb[i]
                    mk.op(act, lambda e: e.activation(out=ob[:], in_=ps[bank][:, :], func=AF.Identity),
                          reads=[psb[bank]], writes=[ob_b])
                    mk.dma(sp, self.rv_d[rows, :], ob[:], reads=[ob_b], writes=[B["rv_d"]])
                elif kind == "rg":
                    of, of_b = tmf[i]
                    mk.op(act, lambda e: e.activation(out=of[:], in_=ps[bank][:, :], func=AF.Silu),
                          reads=[psb[bank]], writes=[of_b])
                    mk.dma(sp, self.rg_d[rows, :], of[:], reads=[of_b], writes=[B["rg_d"]])

        load_w(0)
        for i in range(10):
            if i + 1 < 10:
                load_w(i + 1)
            w, w_b = wb[i % 2]
            if i < 2:
                for hh in range(4):
                    h = i * 4 + hh
                    qk_head(w, w_b, hh * 128, self.vec(V_QKN + l * 2 + 0), self.qT_d[h], B["qT_d"], False, 0)
            elif i == 2:
                for hh in range(2):
                    qk_head(w, w_b, hh * 128, self.vec(V_QKN + l * 2 + 1), self.kT_d[hh], B["kT_d"], True, hh)
                qk_drain()
                tm_group(w, w_b, 256, 256, "v")
            elif i == 3:
                for hh in range(4):
                    copy_head(w, w_b, hh * 128, 1.0, self.rqT_d[hh], B["rqT_d"])
            elif i == 4:
                for hh in range(4):
                    copy_head(w, w_b, hh * 128, HD ** -0.5, self.rkT_d[hh], B["rkT_d"])
                tm_group(w, w_b, 0, 512, "rk")
            elif i == 5:
                tm_group(w, w_b, 0, 512, "rv")
            elif i == 6:
                tm_group(w, w_b, 0, 512, "rg")
            else:
                for hh in range(4):
                    hy_chunk(w, w_b, hh * 128, (i - 7) * 4 + hh)
        mk.barrier()


Prog.phase_in = _phase_in


def _conv3(self, raw, raw_b, o, o_b, tapvec):
    mk = self.mk
    mk.op(mk.act, lambda e: e.activation(out=o[:, :], in_=raw[:, :], func=AF.Identity, scale=tapvec(1)),
          reads=[raw_b, self.vec_b], writes=[o_b])
    for (s, e_, _c) in SEGS:
        mk.op(mk.dve, lambda e, s=s, e_=e_: e.scalar_tensor_tensor(
            out=o[:, s + 1:e_], in0=raw[:, s:e_ - 1], scalar=tapvec(0), in1=o[:, s + 1:e_],
            op0=ALU.mult, op1=ALU.add), reads=[raw_b, o_b, self.vec_b], writes=[o_b])
        mk.op(mk.dve, lambda e, s=s, e_=e_: e.scalar_tensor_tensor(
            out=o[:, s:e_ - 1], in0=raw[:, s + 1:e_], scalar=tapvec(2), in1=o[:, s:e_ - 1],
            op0=ALU.mult, op1=ALU.add), reads=[raw_b, o_b, self.vec_b], writes=[o_b])


Prog.conv3 = _conv3


def _fm(v):
    return np.ascontiguousarray(v.reshape(-1, 128).T)


def _const_tables():
    f32 = np.float32
    cm = np.zeros((NCM, 128, 128), f32)
    cm[CM_ID] = np.eye(128, dtype=f32)
    cm[CM_ONES_HD] = 1.0 / 128.0
    m = np.arange(128)
    cm[CM_PSW][(m + 64) % 128, m] = 1.0
    cm[CM_ONES_D] = 1.0 / D
    j = np.arange(128)[:, None].astype(f32)
    i = np.arange(128)[None, :].astype(f32)
    cm[CM_DPOS] = np.maximum(i - j, 0)
    cm[CM_MF] = (i >= j)
    cm[CM_DNEG] = np.maximum(j - i, 0)
    cm[CM_MB] = (j >= i)
    cm[CM_IP1] = np.broadcast_to(i + 1.0, (128, 128))
    cm[CM_CMI] = np.broadcast_to(128.0 - i, (128, 128))
    cmats = np.ascontiguousarray(cm.transpose(1, 0, 2).reshape(128, NCM * 128))
    t = np.arange(LAT)
    row = (t // GRID_W).astype(f32)
    col = (t % GRID_W).astype(f32)
    n_freq = HD // 4
    inv_freq = (np.float32(10000.0) ** (-np.arange(n_freq, dtype=f32) / n_freq)).astype(f32)
    ang = np.concatenate([row[:, None] * inv_freq[None], col[:, None] * inv_freq[None]], -1)
    cos, sin = np.cos(ang).astype(f32), np.sin(ang).astype(f32)
    rope = np.zeros((128, 2, LAT), f32)
    rope[:64, 0] = cos.T
    rope[64:, 0] = cos.T
    rope[:64, 1] = -sin.T
    rope[64:, 1] = sin.T
    return cmats, rope


def _prep(inp):
    f32 = np.float32
    g = {k: np.asarray(v) for k, v in inp.items()}
    cmats, rope = _const_tables()
    vec = np.zeros((128, NVEC), f32)
    for l in range(DEPTH):
        vec[:, V_QKN + l * 2 + 0] = g["q_norm"][l]
        vec[:, V_QKN + l * 2 + 1] = g["k_norm"][l]
        for tap in range(3):
            vec[:, V_HYC + (l * 3 + tap) * 12:V_HYC + (l * 3 + tap + 1) * 12] = _fm(g["hy_conv"][l, tap])
            vec[:, V_FCV + (l * 3 + tap) * 88:V_FCV + (l * 3 + tap + 1) * 88] = _fm(g["ffn_conv"][l, tap])
        for j, nm in enumerate(("ln1_w", "ln1_b", "ln2_w", "ln2_b")):
            vec[:, V_LNP + (l * 4 + j) * 16:V_LNP + (l * 4 + j + 1) * 16] = _fm(g[nm][l])
        for o in range(2):
            vec[:, V_HYB + (l * 2 + o) * 4:V_HYB + (l * 2 + o + 1) * 4] = _fm(g["hy_bias"][l, o])
        vec[:, V_RETD + l * 8:V_RETD + (l + 1) * 8] = g["ret_decay"][l].reshape(1, 8)
        vec[:64, V_HF + l * 3 + 0] = g["hf_b1"][l]
        vec[:64, V_HF + l * 3 + 1] = g["hf_freq"][l]
        vec[:64, V_HF + l * 3 + 2] = g["hf_b2"][l]
        vec[:, V_BADA + l * 96:V_BADA + (l + 1) * 96] = _fm(g["b_ada"][l])
    vec[:, V_JJ] = np.arange(128)
    vec[:, V_EPS] = EPS
    vec[:, V_EPS + 1] = -PI
    vec[:, V_EPS + 2] = 1.0
    vec[:, V_JJ + 1] = 127 - np.arange(128)
    shared = {
        "w_ada": g["w_ada"], "w_in": g["w_in"], "w_out": g["w_out"], "w_up": g["w_up"],
        "w_down": g["w_down"], "vecs": vec, "cmats": cmats, "rope": rope,
        "hf_w1": g["hf_w1"], "hf_w2": g["hf_w2"], "hf_w3": g["hf_w3"],
    }
    for L_, tb in _hy_consts().items():
        for nm, arr in tb.items():
            shared["hy_%s_%d" % (nm, L_)] = arr
    maps = []
    for i in range(NCORES):
        xT0 = np.concatenate([g["x_sample"][i].T, g["x_prompt"][2 * i].T, g["x_prompt"][2 * i + 1].T], axis=1)
        cpair = np.stack([g["c"][i], g["c_ctx"]], 0)
        cT = np.ascontiguousarray(cpair.reshape(2, KC, 128).transpose(2, 1, 0))
        m = dict(shared)
        m["xT0"] = np.ascontiguousarray(xT0)
        m["cT"] = cT
        m["cache_kT"] = np.ascontiguousarray(g["cache_k"][i].transpose(0, 1, 3, 2))
        m["cache_v"] = np.ascontiguousarray(g["cache_v"][i])
        m["state_ret"] = np.ascontiguousarray(g["state_ret"][i])
        maps.append(m)
    return maps


_PROG_CACHE = {}


def _get_prog(debug=False, stop_after=None):
    key = (debug, stop_after)
    if key not in _PROG_CACHE:
        p = Prog(debug=debug, stop_after=stop_after)
        p.build()
        _PROG_CACHE[key] = p
    return _PROG_CACHE[key]


def run(inputs, debug=False, stop_after=None, cores=None):
    p = _get_prog(debug, stop_after)
    maps = _prep(inputs)
    maps = [{k: m[k] for k in p.in_names} for m in maps]
    if cores is not None:
        maps = [maps[c] for c in cores]
    res = run_bass_kernel_spmd(p.nc, maps, core_ids=list(range(len(maps))))
    return p, res


def kernel(**inputs):
    p, res = run(inputs)
    R = res.results
    y_s = np.stack([R[i]["yT"][:, 0:LAT].T for i in range(NCORES)], 0)
    y_p = np.stack([R[i]["yT"][:, LAT + s * CTX:LAT + (s + 1) * CTX].T for i in range(NCORES) for s in range(2)], 0)
    nck = np.concatenate([R[i]["nck"] for i in range(NCORES)], 0)
    ncv = np.concatenate([R[i]["ncv"] for i in range(NCORES)], 0)
    nsr = np.concatenate([R[i]["nsr"] for i in range(NCORES)], 0)
    return (np.ascontiguousarray(y_p, dtype=np.float32), np.ascontiguousarray(y_s, dtype=np.float32),
            nck.astype(np.float32), ncv.astype(np.float32), nsr.astype(np.float32))


def _phase_att(self, l):
    mk = self.mk
    pe, act, dve, pool, sp = mk.pe, mk.act, mk.dve, mk.pool, mk.sp
    B = self.bufs
    ps, psb = self.ps, self.psb
    scale = HD ** -0.5
    with ExitStack() as es:
        qT, qT_b = self.sb(es, "a_qT", [128, 8, T], BF16)
        kT, kT_b = self.sb(es, "a_kT", [128, 2, PAST + T], BF16)
        vA, vA_b = self.sb(es, "a_v", [128, 16, 256], BF16)
        E = [self.sb(es, "a_E%d" % i, [128, 512], BF16) for i in range(3)]
        rec = [self.sb(es, "a_rec%d" % i, [128, 512]) for i in range(2)]
        aT = [self.sb(es, "a_o%d" % i, [128, 512], BF16) for i in range(2)]
        if self.ada_pending:
            self.ada_bufs(es)
        mk.dma(sp, kT[:, :, PAST:PAST + T], self.kT_d.rearrange("h p t -> p h t"), reads=[B["kT_d"]], writes=[kT_b])
        mk.dma(pool, kT[:, :, 0:PAST], self.cache_kT[l].rearrange("h p t -> p h t"), writes=[kT_b])
        qT_hb = [Buf("qTh%d" % h) for h in range(8)]
        mk.dma(sp, qT[:, 0, :], self.qT_d[0], reads=[B["qT_d"]], writes=[qT_hb[0]])
        mk.dma(sp, vA[:, 4:16, :], self.v_d.rearrange("(c p) n -> p c n", p=128), reads=[B["v_d"]], writes=[vA_b])
        for h_ in range(1, 8):
            mk.dma(sp, qT[:, h_, :], self.qT_d[h_], reads=[B["qT_d"]], writes=[qT_hb[h_]])
        for kv_ in range(2):
            mk.dma(pool, vA[:, 0:4, kv_ * 128:(kv_ + 1) * 128],
                   self.cache_v[l, kv_].rearrange("(c p) d -> p c d", p=128), writes=[vA_b])
        cnt = {"s": 0, "g": 0}

        def attend(h, q0, q1, chunks):
            kv = h // 4
            n = q1 - q0
            g = cnt["g"] % 2
            cnt["g"] += 1
            ob, db = 3 + g, 5 + g
            nchunks = len(chunks)
            sis = []
            for ci in range(nchunks + 1):
                if ci < nchunks:
                    koff, vc = chunks[ci]
                    si = cnt["s"] % 3
                    cnt["s"] += 1
                    sis.append(si)
                    mk.op(pe, lambda e, si=si, koff=koff: e.matmul(ps[si][:, 0:n], kT[:, kv, koff:koff + 128], qT[:, h, q0:q1],
                                                                 start=True, stop=True),
                          reads=[kT_b, qT_hb[h]], writes=[psb[si]])
                    Et, Et_b = E[si]
                    mk.op(act, lambda e, si=si, Et=Et: e.activation(out=Et[:, 0:n], in_=ps[si][:, 0:n], func=AF.Exp, scale=scale),
                          reads=[psb[si]], writes=[Et_b])
                if ci >= 1:
                    cj = ci - 1
                    koff, vc = chunks[cj]
                    Et, Et_b = E[sis[cj]]
                    last = cj == nchunks - 1
                    mk.op(pe, lambda e, vc=vc, Et=Et, cj=cj, last=last: e.matmul(
                        ps[ob][:, 0:n], vA[:, vc, kv * 128:(kv + 1) * 128], Et[:, 0:n], start=(cj == 0), stop=last),
                        reads=[vA_b, Et_b], writes=[psb[ob]], inc=last)
                    mk.op(pe, lambda e, Et=Et, cj=cj, last=last: e.matmul(
                        ps[db][:, 0:n], self.ones_bf[:], Et[:, 0:n], start=(cj == 0), stop=last),
                        reads=[self.ones_bf_b, Et_b], writes=[psb[db]], inc=True)
            r, r_b = rec[g]
            o, o_b = aT[g]
            mk.op(dve, lambda e: e.reciprocal(r[:, 0:n], ps[db][:, 0:n]), reads=[psb[db]], writes=[r_b])
            mk.op(dve, lambda e: e.tensor_tensor(out=o[:, 0:n], in0=ps[ob][:, 0:n], in1=r[:, 0:n], op=ALU.mult),
                  reads=[psb[ob], r_b], writes=[o_b])
            mk.dma(sp, self.catT_d[h * 128:(h + 1) * 128, q0:q1], o[:, 0:n], reads=[o_b], writes=[B["catT_d"]])
            self.ada_bg(1)

        lat_chunks = [(c * 128, c) for c in range(4)] + [(PAST + c * 128, 4 + c) for c in range(8)]
        for h in range(8):
            for qt in range(2):
                attend(h, qt * 512, (qt + 1) * 512, lat_chunks)
            for s in range(2):
                t0 = 8 + 2 * s
                ch = [(PAST + (t0 + c) * 128, 4 + t0 + c) for c in range(2)]
                attend(h, LAT + s * CTX, LAT + (s + 1) * CTX, ch)
        mk.barrier()


Prog.phase_att = _phase_att


def _ret_tables(self, l):
    mk = self.mk
    act, dve = mk.act, mk.dve
    es = self.es
    M, M_b = self.sb(es, "rt_M", [128, 4, 128])
    dq, _ = self.sb(es, "rt_dq", [128, 4, 2, 128])
    dk, _ = self.sb(es, "rt_dk", [128, 4, 2])
    cd, _ = self.sb(es, "rt_cd", [128, 4, 2])
    cdf, _ = self.sb(es, "rt_cdf", [128, 2, 4, 128])
    dq_b = dk_b = cd_b = M_b
    if True:
        lg, lg_b = self.sb(es, "r_lg", [128, 8])
        tmpm, tmpm_b = self.sb(es, "r_tmpm", [128, 128])
        retd = self.vec(V_RETD + l * 8, 8)
        mk.op(act, lambda e: e.activation(out=lg[:], in_=retd, func=AF.Exp, scale=-1.0),
              reads=[self.vec_b], writes=[lg_b])
        mk.op(act, lambda e: e.activation(out=lg[:], in_=lg[:], func=AF.Ln, bias=self.vec(V_EPS + 2), scale=1.0),
              reads=[lg_b, self.vec_b], writes=[lg_b])
        mk.op(dve, lambda e: e.tensor_scalar_mul(lg[:], lg[:], -1.0), reads=[lg_b], writes=[lg_b])
        for h in range(4):
            lf = lg[:, h:h + 1]
            lb = lg[:, 4 + h:5 + h]
            mk.op(act, lambda e: e.activation(out=tmpm[:], in_=self.cmat(CM_DPOS), func=AF.Exp, scale=lf),
                  reads=[self.cm_b, lg_b], writes=[tmpm_b])
            mk.op(dve, lambda e: e.tensor_tensor(out=M[:, h, :], in0=tmpm[:], in1=self.cmat(CM_MF), op=ALU.mult),
                  reads=[tmpm_b, self.cm_b], writes=[M_b])
            mk.op(act, lambda e: e.activation(out=tmpm[:], in_=self.cmat(CM_DNEG), func=AF.Exp, scale=lb),
                  reads=[self.cm_b, lg_b], writes=[tmpm_b])
            mk.op(dve, lambda e: e.tensor_tensor(out=tmpm[:], in0=tmpm[:], in1=self.cmat(CM_MB), op=ALU.mult),
                  reads=[tmpm_b, self.cm_b], writes=[tmpm_b])
            mk.op(dve, lambda e: e.tensor_tensor(out=M[:, h, :], in0=M[:, h, :], in1=tmpm[:], op=ALU.add),
                  reads=[tmpm_b, M_b], writes=[M_b])
            mk.op(act, lambda e: e.activation(out=dq[:, h, 0, :], in_=self.cmat(CM_IP1), func=AF.Exp, scale=lf),
                  reads=[self.cm_b, lg_b], writes=[dq_b])
            mk.op(act, lambda e: e.activation(out=dq[:, h, 1, :], in_=self.cmat(CM_CMI), func=AF.Exp, scale=lb),
                  reads=[self.cm_b, lg_b], writes=[dq_b])
            mk.op(act, lambda e: e.activation(out=dk[:, h, 0:1], in_=self.vec(V_JJ + 1), func=AF.Exp, scale=lf),
                  reads=[self.vec_b, lg_b], writes=[dk_b])
            mk.op(act, lambda e: e.activation(out=dk[:, h, 1:2], in_=self.vec(V_JJ), func=AF.Exp, scale=lb),
                  reads=[self.vec_b, lg_b], writes=[dk_b])
            mk.op(act, lambda e: e.activation(out=cd[:, h, 0:1], in_=lf, func=AF.Exp, scale=128.0),
                  reads=[lg_b], writes=[cd_b])
            mk.op(act, lambda e: e.activation(out=cd[:, h, 1:2], in_=lb, func=AF.Exp, scale=128.0),
                  reads=[lg_b], writes=[cd_b])
            for d_ in range(2):
                mk.op(act, lambda e, d_=d_: e.activation(out=cdf[:, d_, h, :], in_=self.cmat(CM_MF), func=AF.Identity,
                                                        scale=0.0, bias=cd[:, h, d_:d_ + 1]),
                      reads=[cd_b, self.cm_b], writes=[cd_b])

    if not hasattr(self, "rt"):
        self.rt = {}
    self.rt[l] = {"M": M, "dq": dq, "dk": dk, "cd": cd, "cdf": cdf, "b": M_b}


Prog.ret_tables = _ret_tables


def _phase_ret(self, l):
    mk = self.mk
    pe, act, dve, pool, sp = mk.pe, mk.act, mk.dve, mk.pool, mk.sp
    B = self.bufs
    ps, psb = self.ps, self.psb
    ident = self.cmat(CM_ID)
    with ExitStack() as es:
        rqT, rqT_b = self.sb(es, "r_qT", [128, 4, T], BF16)
        rkT, rkT_b = self.sb(es, "r_kT", [128, 4, T], BF16)
        rk, rk_b = self.sb(es, "r_k", [128, NT, 512], BF16)
        rv, rv_b = self.sb(es, "r_v", [128, NT, 512], BF16)
        rg, rg_b = self.sb(es, "r_g", [128, NT, 512])
        rkF, _ = self.sb(es, "r_kF", [128, NT, 512], BF16)
        rkB, _ = self.sb(es, "r_kB", [128, NT, 512], BF16)
        Sf4, Sf4_b = self.sb(es, "r_Sf4", [128, 4, 128])
        Sb4, Sb4_b = self.sb(es, "r_Sb4", [128, 4, 128])
        Sfb4, Sfb4_b = self.sb(es, "r_Sfb4", [128, 4, 128], BF16)
        cdf = self.rt[l]["cdf"]
        SbAll, SbAll_b = self.sb(es, "r_SbAll", [128, NT, 4, 128], BF16)
        msk4 = [self.sb(es, "r_msk%d" % i, [128, 4, 128], BF16) for i in range(2)]
        qf4 = [self.sb(es, "r_qf%d" % i, [128, 4, 128], BF16) for i in range(2)]
        qb4 = [self.sb(es, "r_qb%d" % i, [128, 4, 128], BF16) for i in range(2)]
        st6, st6_b = self.sb(es, "r_st6", [128, 4, 6])
        mv, mv_b = self.sb(es, "r_mv", [128, 4, 2])
        rs, rs_b = self.sb(es, "r_rs", [128, 4])
        rn = [self.sb(es, "r_rn%d" % i, [128, 512]) for i in range(2)]
        retT, retT_b = self.sb(es, "r_retT", [128, 4, T], BF16)
        if self.ada_pending:
            self.ada_bufs(es)

        mk.dma(sp, rk[:], self.rk_d.rearrange("(c p) n -> p c n", p=128), reads=[B["rk_d"]], writes=[rk_b])
        mk.dma(sp, rv[:], self.rv_d.rearrange("(c p) n -> p c n", p=128), reads=[B["rv_d"]], writes=[rv_b])
        mk.dma(sp, rqT[:], self.rqT_d.rearrange("h p t -> p h t"), reads=[B["rqT_d"]], writes=[rqT_b])
        mk.dma(sp, rkT[:], self.rkT_d.rearrange("h p t -> p h t"), reads=[B["rkT_d"]], writes=[rkT_b])
        mk.dma(sp, rg[:], self.rg_d.rearrange("(c p) n -> p c n", p=128), reads=[B["rg_d"]], writes=[rg_b])

        M, dq, dk, cd = self.rt[l]["M"], self.rt[l]["dq"], self.rt[l]["dk"], self.rt[l]["cd"]
        M_b = dq_b = dk_b = cd_b = self.rt[l]["b"]
        rkF_nb = [Buf("rkF%d" % n) for n in range(NT)]
        rkB_nb = [Buf("rkB%d" % n) for n in range(NT)]

        def scale_keys(n, fwd):
            for h in range(4):
                hc = slice(h * 128, (h + 1) * 128)
                if fwd:
                    mk.op(act, lambda e, n=n, h=h, hc=hc: e.activation(out=rkF[:, n, hc], in_=rk[:, n, hc], func=AF.Identity,
                                                                     scale=dk[:, h, 0:1]),
                          reads=[rk_b, dk_b], writes=[rkF_nb[n]])
                else:
                    mk.op(dve, lambda e, n=n, h=h, hc=hc: e.tensor_scalar_mul(rkB[:, n, hc], rk[:, n, hc], dk[:, h, 1:2]),
                          reads=[rk_b, dk_b], writes=[rkB_nb[n]])

        cnt = {"a": 0, "kv": 0, "t": 0}
        segs = ((0, 8, True, None), (8, 2, False, 0), (10, 2, False, 1))
        for (c0, nch, has_init, seq) in segs:
            for d_, (S4, S4_b) in enumerate(((Sf4, Sf4_b), (Sb4, Sb4_b))):
                if has_init:
                    mk.dma(sp, S4[:], self.state_ret[l, d_].rearrange("h p e -> p h e"), writes=[S4_b])
                else:
                    mk.op(dve, lambda e, S4=S4: e.memset(S4[:], 0.0), writes=[S4_b])
            for n in range(c0 + nch - 1, c0 - 1, -1):
                scale_keys(n, False)
                mk.op(act, lambda e, n=n: e.activation(out=SbAll[:, n, :, :], in_=Sb4[:], func=AF.Identity),
                      reads=[Sb4_b], writes=[SbAll_b])
                bk = 4 + cnt["kv"] % 2
                cnt["kv"] += 1
                for h in range(4):
                    hc = slice(h * 128, (h + 1) * 128)
                    mk.op(pe, lambda e, n=n, hc=hc, bk=bk: e.matmul(ps[bk][:, hc], rkB[:, n, hc], rv[:, n, hc],
                                                                  start=True, stop=True),
                          reads=[rkB_nb[n], rv_b], writes=[psb[bk]], inc=(h == 3))
                mk.op(dve, lambda e: e.tensor_tensor(out=Sb4[:], in0=Sb4[:], in1=cdf[:, 1, :, :], op=ALU.mult),
                      reads=[Sb4_b, cd_b], writes=[Sb4_b])
                mk.op(dve, lambda e, bk=bk: e.tensor_tensor(out=Sb4[:], in0=Sb4[:],
                                                            in1=ps[bk][:, :].rearrange("p (h e) -> p h e", h=4), op=ALU.add),
                      reads=[Sb4_b, psb[bk]], writes=[Sb4_b])
            if seq is not None:
                mk.dma(sp, self.nsr[seq, l, 1].rearrange("h p e -> p h e"), Sb4[:], reads=[Sb4_b], writes=[B["nsr"]])
            mk.op(act, lambda e: e.activation(out=Sfb4[:], in_=Sf4[:], func=AF.Identity), reads=[Sf4_b], writes=[Sfb4_b])

            def P1(n):
                tk = slice(n * 128, (n + 1) * 128)
                ab = n % 2
                for h in range(4):
                    hc = slice(h * 128, (h + 1) * 128)
                    mk.op(pe, lambda e, h=h, hc=hc: e.matmul(ps[ab][:, hc], rkT[:, h, tk], rqT[:, h, tk], start=True, stop=True),
                          reads=[rkT_b, rqT_b], writes=[psb[ab]], inc=(h == 3))
                m_, m_b = msk4[n % 2]
                f_, f_b = qf4[n % 2]
                b_, b_b = qb4[n % 2]
                for h in range(4):
                    hc = slice(h * 128, (h + 1) * 128)
                    mk.op(dve, lambda e, h=h, hc=hc: e.tensor_tensor(out=m_[:, h, :], in0=ps[ab][:, hc], in1=M[:, h, :], op=ALU.mult),
                          reads=[psb[ab], M_b], writes=[m_b])
                    mk.op(dve, lambda e, h=h: e.tensor_tensor(out=f_[:, h, :], in0=rqT[:, h, tk], in1=dq[:, h, 0, :], op=ALU.mult),
                          reads=[rqT_b, dq_b], writes=[f_b])
                    mk.op(dve, lambda e, h=h: e.tensor_tensor(out=b_[:, h, :], in0=rqT[:, h, tk], in1=dq[:, h, 1, :], op=ALU.mult),
                          reads=[rqT_b, dq_b], writes=[b_b])

            def P2(n):
                ob = 2 + (n % 2)
                kb = 4 + (n % 2)
                m_, m_b = msk4[n % 2]
                f_, f_b = qf4[n % 2]
                b_, b_b = qb4[n % 2]
                for h in range(4):
                    hc = slice(h * 128, (h + 1) * 128)
                    mk.op(pe, lambda e, h=h, hc=hc: e.matmul(ps[ob][:, hc], m_[:, h, :], rv[:, n, hc], start=True, stop=False),
                          reads=[m_b, rv_b], writes=[psb[ob]], inc=False)
                    mk.op(pe, lambda e, h=h, hc=hc: e.matmul(ps[ob][:, hc], f_[:, h, :], Sfb4[:, h, :], start=False, stop=False),
                          reads=[f_b, Sfb4_b], writes=[psb[ob]], inc=False)
                    mk.op(pe, lambda e, h=h, hc=hc: e.matmul(ps[ob][:, hc], b_[:, h, :], SbAll[:, n, h, :], start=False, stop=True),
                          reads=[b_b, SbAll_b], writes=[psb[ob]])
                for h in range(4):
                    hc = slice(h * 128, (h + 1) * 128)
                    mk.op(pe, lambda e, hc=hc: e.matmul(ps[kb][:, hc], rkF[:, n, hc], rv[:, n, hc], start=True, stop=True),
                          reads=[rkF_nb[n], rv_b], writes=[psb[kb]], inc=(h == 3))
                mk.op(dve, lambda e: e.tensor_tensor(out=Sf4[:], in0=Sf4[:], in1=cdf[:, 0, :, :], op=ALU.mult),
                      reads=[Sf4_b, cd_b], writes=[Sf4_b])
                mk.op(dve, lambda e: e.tensor_tensor(out=Sf4[:], in0=Sf4[:],
                                                     in1=ps[kb][:, :].rearrange("p (h e) -> p h e", h=4), op=ALU.add),
                      reads=[Sf4_b, psb[kb]], writes=[Sf4_b])
                mk.op(act, lambda e: e.activation(out=Sfb4[:], in_=Sf4[:], func=AF.Identity), reads=[Sf4_b], writes=[Sfb4_b])

            def P3(n):
                tk = slice(n * 128, (n + 1) * 128)
                ob = 2 + (n % 2)
                for h in range(4):
                    hc = slice(h * 128, (h + 1) * 128)
                    mk.op(dve, lambda e, h=h, hc=hc: e.bn_stats(st6[:, h, :], ps[ob][:, hc]), reads=[psb[ob]], writes=[st6_b])
                    mk.op(dve, lambda e, h=h: e.bn_aggr(mv[:, h, :], st6[:, h, :]), reads=[st6_b], writes=[mv_b])
                mk.op(act, lambda e: e.activation(out=rs[:], in_=mv[:, :, 1], func=AF.Sqrt, bias=self.vec(V_EPS), scale=1.0),
                      reads=[mv_b, self.vec_b], writes=[rs_b])
                mk.op(dve, lambda e: e.reciprocal(rs[:], rs[:]), reads=[rs_b], writes=[rs_b])
                r_, r_b = rn[n % 2]
                for h in range(4):
                    hc = slice(h * 128, (h + 1) * 128)
                    mk.op(dve, lambda e, h=h, hc=hc: e.tensor_scalar(r_[:, hc], ps[ob][:, hc], mv[:, h, 0:1], rs[:, h:h + 1],
                                                                  ALU.subtract, ALU.mult),
                          reads=[psb[ob], mv_b, rs_b], writes=[r_b])
                mk.op(dve, lambda e: e.tensor_tensor(out=r_[:], in0=r_[:], in1=rg[:, n, :], op=ALU.mult),
                      reads=[r_b, rg_b], writes=[r_b])
                for h in range(4):
                    hc = slice(h * 128, (h + 1) * 128)
                    tb = 6 + cnt["t"] % 2
                    cnt["t"] += 1
                    mk.op(pe, lambda e, tb=tb, hc=hc: e.matmul(ps[tb][:, 0:128], r_[:, hc], ident, start=True, stop=True),
                          reads=[r_b, self.cm_b], writes=[psb[tb]])
                    mk.op(act, lambda e, tb=tb, h=h: e.activation(out=retT[:, h, tk], in_=ps[tb][:, 0:128], func=AF.Identity),
                          reads=[psb[tb]], writes=[retT_b])

            scale_keys(c0, True)
            P1(c0)
            for n in range(c0, c0 + nch):
                self.ada_bg(1)
                if n + 1 < c0 + nch:
                    scale_keys(n + 1, True)
                    P1(n + 1)
                P2(n)
                P3(n)
            if seq is not None:
                mk.dma(sp, self.nsr[seq, l, 0].rearrange("h p e -> p h e"), Sf4[:], reads=[Sf4_b], writes=[B["nsr"]])
        mk.dma(sp, self.catT_d[1024:1536, :].rearrange("(h p) t -> p h t", p=128), retT[:], reads=[retT_b],
               writes=[B["catT_d"]])
        mk.barrier()


Prog.phase_ret = _phase_ret


def _hy_tables(L):
    f32 = np.float32
    f = np.arange(L, dtype=np.int64)[None, :]
    j = np.arange(2 * L, dtype=np.int64)[:, None]
    m = ((2 * f + 1) * j) % (4 * L)
    th = np.pi * m.astype(np.float64) / (2 * L)
    cf2 = np.cos(th)
    sf2 = np.sin(th)
    ci = (cf2[:L].T / L)
    si = (sf2[:L].T / L)
    bf = ml_dtypes.bfloat16
    pos = np.arange(L, dtype=f32)
    t = (pos / f32(max(L - 1, 1))).astype(f32)
    w = (f32(2.0 * math.pi) * pos / f32(L)).astype(f32)
    fb = np.linspace(1e-4, 15, 16, dtype=f32)
    ang = (w[:, None] * fb[None, :]).astype(f32)
    z = np.concatenate([t[:, None], np.cos(ang), -np.sin(ang)], -1).astype(f32)
    deltas = np.linspace(math.log(1e-2) / 0.3, math.log(1e-2) / 1.5, 512, dtype=f32)
    dec = np.exp(-t[:, None] * np.abs(deltas)[None, :]).astype(f32)
    return {
        "cf2": cf2.astype(bf), "sf2": sf2.astype(bf), "ci": ci.astype(bf), "si": si.astype(bf),
        "zt": np.ascontiguousarray(np.stack([z.T, z[::-1].T], 0)),
        "dec": np.ascontiguousarray(np.stack([dec, dec[::-1]], 0)),
    }


_HYT = {}


def _hy_consts():
    if not _HYT:
        for L in (LAT, CTX):
            _HYT[L] = _hy_tables(L)
    return _HYT


def _declare_hy(self):
    self.hyc_in = {}
    for L in (LAT, CTX):
        d = {}
        d["cf2"] = self.din("hy_cf2_%d" % L, [2 * L, L], BF16)
        d["sf2"] = self.din("hy_sf2_%d" % L, [2 * L, L], BF16)
        d["ci"] = self.din("hy_ci_%d" % L, [L, L], BF16)
        d["si"] = self.din("hy_si_%d" % L, [L, L], BF16)
        d["zt"] = self.din("hy_zt_%d" % L, [2, 33, L])
        d["dec"] = self.din("hy_dec_%d" % L, [2, L, 512])
        self.hyc_in[L] = d
    self.hf_w1 = self.din("hf_w1", [DEPTH, 33, 64])
    self.hf_w2 = self.din("hf_w2", [DEPTH, 64, 64])
    self.hf_w3 = self.din("hf_w3", [DEPTH, 64, 2048])
    self.kspec_d = self.dscr("kspec_d", [DEPTH, 2, 2, 2, LAT, 512])
    self.bufs["kspec_d"] = Buf("kspec_d")


Prog.declare_hy = _declare_hy


def _hy_filters(self):
    mk = self.mk
    pe, act, dve, pool, sp = mk.pe, mk.act, mk.dve, mk.pool, mk.sp
    B = self.bufs
    ps, psb = self.ps, self.psb
    ident = self.cmat(CM_ID)
    TWO_PI = 2.0 * PI
    for li, L in enumerate((LAT, CTX)):
        C = self.hyc_in[L]
        nj = 2 * L // 128
        nf = L // 128
        ntile = max(L // 512, 1)
        N = min(L, 512)
        with ExitStack() as es:
            cf2, cf2_b = self.sb(es, "h_cf2", [128, nj, L], BF16)
            sf2, sf2_b = self.sb(es, "h_sf2", [128, nj, L], BF16)
            w1, w1_b = self.sb(es, "h_w1", [33, 64])
            w2, w2_b = self.sb(es, "h_w2", [64, 64])
            w3, w3_b = self.sb(es, "h_w3", [64, 2048], BF16)
            zt, zt_b = self.sb(es, "h_zt", [33, 2, L])
            fb, fb_b = self.sb(es, "h_fb", [64, 2])
            vv = [self.sb(es, "h_vv%d" % i, [64, 512]) for i in range(2)]
            mm_ = [self.sb(es, "h_mm%d" % i, [64, 512]) for i in range(2)]
            hd1 = [self.sb(es, "h_hd1%d" % i, [64, 512]) for i in range(2)]
            hd2, hd2_b = self.sb(es, "h_hd2", [64, 2, L], BF16)
            dec = [self.sb(es, "h_dec%d" % i, [128, 512]) for i in range(2)]
            gsb, gsb_b = self.sb(es, "h_g", [128, 2, nj, 512], BF16)
            ko = [self.sb(es, "h_ko%d" % i, [128, 512]) for i in range(2)]
            mk.dma(sp, cf2[:], C["cf2"].rearrange("(c p) f -> p c f", p=128), writes=[cf2_b])
            mk.dma(sp, sf2[:], C["sf2"].rearrange("(c p) f -> p c f", p=128), writes=[sf2_b])
            for l in range(DEPTH):
                mk.dma(sp, w1[:], self.hf_w1[l], writes=[w1_b])
                mk.dma(sp, w2[:], self.hf_w2[l], writes=[w2_b])
                mk.dma(pool, w3[:], self.hf_w3[l], writes=[w3_b])
                mk.dma(sp, zt[:], C["zt"].rearrange("d k t -> k d t"), writes=[zt_b])
                freq = self.vec_sb[0:64, V_HF + l * 3 + 1:V_HF + l * 3 + 2]
                for j_, boff in enumerate((0, 2)):
                    bvec = self.vec_sb[0:64, V_HF + l * 3 + boff:V_HF + l * 3 + boff + 1]
                    mk.op(dve, lambda e, j_=j_, bvec=bvec: e.tensor_scalar(fb[:, j_:j_ + 1], bvec, freq, None, ALU.mult),
                          reads=[self.vec_b], writes=[fb_b])
                negpi = self.vec_sb[0:64, V_EPS + 1:V_EPS + 2]
                it = 0

                def sin_layer(src_bank, n, fbcol, out_ap, out_b):
                    nonlocal it
                    i = it % 2
                    it += 1
                    v_, v_b = vv[i]
                    m_, m_b = mm_[i]
                    mk.op(dve, lambda e: e.tensor_scalar(v_[:, 0:n], ps[src_bank][0:64, 0:n], freq, fb[:, fbcol:fbcol + 1],
                                                         ALU.mult, ALU.add),
                          reads=[psb[src_bank], self.vec_b, fb_b], writes=[v_b])
                    mk.op(dve, lambda e: e.tensor_scalar(m_[:, 0:n], v_[:, 0:n], PI, -TWO_PI, ALU.is_gt, ALU.mult),
                          reads=[v_b], writes=[m_b])
                    mk.op(dve, lambda e: e.tensor_tensor(out=m_[:, 0:n], in0=m_[:, 0:n], in1=v_[:, 0:n], op=ALU.add),
                          reads=[v_b, m_b], writes=[m_b])
                    mk.op(dve, lambda e: e.tensor_scalar(v_[:, 0:n], v_[:, 0:n], -PI, TWO_PI, ALU.is_lt, ALU.mult),
                          reads=[v_b], writes=[v_b])
                    mk.op(dve, lambda e: e.tensor_tensor(out=m_[:, 0:n], in0=m_[:, 0:n], in1=v_[:, 0:n], op=ALU.add),
                          reads=[v_b, m_b], writes=[m_b])
                    mk.op(act, lambda e: e.activation(out=out_ap, in_=m_[:, 0:n], func=AF.Sin),
                          reads=[m_b], writes=[out_b])

                for d_ in range(2):
                    for tt in range(ntile):
                        ts_ = slice(tt * N, (tt + 1) * N)
                        mk.op(pe, lambda e: e.matmul(ps[0][0:64, 0:N], w1[:, :], zt[:, d_, ts_], start=True, stop=True),
                              reads=[w1_b, zt_b], writes=[psb[0]])
                        h1, h1_b = hd1[(d_ * ntile + tt) % 2]
                        sin_layer(0, N, 0, h1[:, 0:N], h1_b)
                        mk.op(pe, lambda e: e.matmul(ps[1][0:64, 0:N], w2[:, :], h1[:, 0:N], start=True, stop=True),
                              reads=[w2_b, h1_b], writes=[psb[1]])
                        sin_layer(1, N, 1, hd2[:, d_, ts_], hd2_b)
                it2 = 0
                for d_ in range(2):
                    for jc in range(nf):
                        dc, dc_b = dec[it2 % 2]
                        mk.dma(sp, dc[:], C["dec"][d_, jc * 128:(jc + 1) * 128, :], writes=[dc_b])
                        for o in range(2):
                            bk = 2 + it2 % 2
                            it2 += 1
                            col0 = o * 1024 + d_ * 512
                            mk.op(pe, lambda e, bk=bk, col0=col0: e.matmul(ps[bk][:, :], hd2[:, d_, jc * 128:(jc + 1) * 128],
                                                                         w3[:, col0:col0 + 512], start=True, stop=True),
                                  reads=[hd2_b, w3_b], writes=[psb[bk]])
                            mk.op(dve, lambda e, bk=bk, o=o: e.scalar_tensor_tensor(
                                out=gsb[:, o, d_ * nf + jc, :], in0=ps[bk][:, :], scalar=(1.0 if d_ == 0 else -1.0),
                                in1=dc[:], op0=ALU.mult, op1=ALU.mult),
                                reads=[psb[bk], dc_b], writes=[gsb_b])
                        it2 += 1
                it3 = 0
                for o in range(2):
                    for fc in range(nf):
                        for cs, (mat, mat_b) in enumerate(((cf2, cf2_b), (sf2, sf2_b))):
                            bk = 4 + it3 % 2
                            k_, k_b = ko[it3 % 2]
                            it3 += 1
                            for jc in range(nj):
                                mk.op(pe, lambda e, jc=jc, mat=mat, bk=bk: e.matmul(
                                    ps[bk][:, :], mat[:, jc, fc * 128:(fc + 1) * 128], gsb[:, o, jc, :],
                                    start=(jc == 0), stop=(jc == nj - 1)),
                                    reads=[mat_b, gsb_b], writes=[psb[bk]], inc=(jc == nj - 1))
                            mk.op(act, lambda e, bk=bk, k_=k_: e.activation(out=k_[:], in_=ps[bk][:, :], func=AF.Identity),
                                  reads=[psb[bk]], writes=[k_b])
                            mk.dma(sp, self.kspec_d[l, li, o, cs, fc * 128:(fc + 1) * 128, :], k_[:], reads=[k_b],
                                   writes=[B["kspec_d"]])
                            if self.bg_hook is not None:
                                self.bg_hook()
            mk.barrier()


Prog.hy_filters = _hy_filters


def _phase_hy(self, l):
    mk = self.mk
    pe, act, dve, pool, sp = mk.pe, mk.act, mk.dve, mk.pool, mk.sp
    B = self.bufs
    ps, psb = self.ps, self.psb
    ident = self.cmat(CM_ID)
    for (L, seqs) in ((LAT, ((0, 1024),)), (CTX, ((1024, 1280), (1280, 1536)))):
        li = 0 if L == LAT else 1
        C = self.hyc_in[L]
        nf = L // 128
        ntile = max(L // 512, 1)
        N = min(L, 512)
        with ExitStack() as es:
            cf, cf_b = self.sb(es, "c_cf", [128, nf, L], BF16)
            sf, sf_b = self.sb(es, "c_sf", [128, nf, L], BF16)
            ci, ci_b = self.sb(es, "c_ci", [128, nf, L], BF16)
            si, si_b = self.sb(es, "c_si", [128, nf, L], BF16)
            zT, _ = self.sb(es, "c_zT", [128, 4, L])
            zT_bb = [[Buf("zT%d_%d" % (cc_, tt_)) for tt_ in range(ntile)] for cc_ in range(4)]
            xT = [self.sb(es, "c_xT%d" % i, [128, 4, L]) for i in range(2)]
            ztok, ztok_b = self.sb(es, "c_ztok", [128, nf, 512], BF16)
            yre, yre_b = self.sb(es, "c_yre", [128, nf, 512], BF16)
            ysn, ysn_b = self.sb(es, "c_ysn", [128, nf, 512], BF16)
            kc = [self.sb(es, "c_kc%d" % i, [128, 512]) for i in range(2)]
            ks = [self.sb(es, "c_ks%d" % i, [128, 512]) for i in range(2)]
            tq = [self.sb(es, "c_tq%d" % i, [128, 512]) for i in range(4)]
            tz = [self.sb(es, "c_tz%d" % i, [128, 512]) for i in range(2)]
            outT, outT_b = self.sb(es, "c_outT", [128, 4, L], BF16)
            def load_z(a, b_):
                mk.dma(sp, zT[:], self.hyT_d[0:512, a:b_].rearrange("(c p) t -> p c t", p=128), reads=[B["hyT_d"]],
                       writes=[zb for row in zT_bb for zb in row])

            load_z(*seqs[0])
            mk.dma(sp, cf[:], C["cf2"][0:L, :].rearrange("(c p) f -> p c f", p=128), writes=[cf_b])
            mk.dma(sp, sf[:], C["sf2"][0:L, :].rearrange("(c p) f -> p c f", p=128), writes=[sf_b])
            mk.dma(sp, ci[:], C["ci"].rearrange("(c p) n -> p c n", p=128), writes=[ci_b])
            mk.dma(sp, si[:], C["si"].rearrange("(c p) n -> p c n", p=128), writes=[si_b])
            for si_, (a, b_) in enumerate(seqs):
                if si_ > 0:
                    load_z(a, b_)
                for o in range(2):
                    mk.dma(sp, xT[o][0][:], self.hyT_d[512 * (o + 1):512 * (o + 2), a:b_].rearrange("(c p) t -> p c t", p=128),
                           reads=[B["hyT_d"]], writes=[xT[o][1]])
                itk = 0
                for o in range(2):
                    for tc in range(nf):
                        bk = 6 + tc % 2
                        for cc in range(4):
                            mk.op(pe, lambda e, cc=cc, bk=bk: e.matmul(ps[bk][:, cc * 128:(cc + 1) * 128],
                                                                     zT[:, cc, tc * 128:(tc + 1) * 128], ident,
                                                                     start=True, stop=True),
                                  reads=[zT_bb[cc][(tc * 128) // N], self.cm_b], writes=[psb[bk]], inc=(cc == 3))
                        mk.op(act, lambda e, bk=bk: e.activation(out=ztok[:, tc, :], in_=ps[bk][:, :], func=AF.Identity),
                              reads=[psb[bk]], writes=[ztok_b])
                    for fc in range(nf):
                        i = itk % 2
                        itk += 1
                        kc_, kc_b = kc[i]
                        ks_, ks_b = ks[i]
                        mk.dma(sp, kc_[:], self.kspec_d[l, li, o, 0, fc * 128:(fc + 1) * 128, :], reads=[B["kspec_d"]], writes=[kc_b])
                        mk.dma(sp, ks_[:], self.kspec_d[l, li, o, 1, fc * 128:(fc + 1) * 128, :], reads=[B["kspec_d"]], writes=[ks_b])
                        bc, bs = (0, 1) if i == 0 else (2, 3)
                        for (mat, mat_b, bk) in ((cf, cf_b, bc), (sf, sf_b, bs)):
                            for tc in range(nf):
                                mk.op(pe, lambda e, mat=mat, bk=bk, tc=tc: e.matmul(
                                    ps[bk][:, :], mat[:, tc, fc * 128:(fc + 1) * 128], ztok[:, tc, :],
                                    start=(tc == 0), stop=(tc == nf - 1)),
                                    reads=[mat_b, ztok_b], writes=[psb[bk]], inc=(tc == nf - 1))
                        q0, q1, q2, q3 = tq
                        mk.op(dve, lambda e: e.tensor_tensor(out=q0[0][:], in0=ps[bc][:, :], in1=kc_[:], op=ALU.mult),
                              reads=[psb[bc], kc_b], writes=[q0[1]])
                        mk.op(dve, lambda e: e.tensor_tensor(out=q1[0][:], in0=ps[bs][:, :], in1=ks_[:], op=ALU.mult),
                              reads=[psb[bs], ks_b], writes=[q1[1]])
                        mk.op(pool, lambda e: e.tensor_tensor(out=yre[:, fc, :], in0=q0[0][:], in1=q1[0][:], op=ALU.subtract),
                              reads=[q0[1], q1[1]], writes=[yre_b])
                        mk.op(dve, lambda e: e.tensor_tensor(out=q2[0][:], in0=ps[bc][:, :], in1=ks_[:], op=ALU.mult),
                              reads=[psb[bc], ks_b], writes=[q2[1]])
                        mk.op(dve, lambda e: e.tensor_tensor(out=q3[0][:], in0=ps[bs][:, :], in1=kc_[:], op=ALU.mult),
                              reads=[psb[bs], kc_b], writes=[q3[1]])
                        mk.op(pool, lambda e: e.tensor_tensor(out=ysn[:, fc, :], in0=q2[0][:], in1=q3[0][:], op=ALU.add),
                              reads=[q2[1], q3[1]], writes=[ysn_b])
                    ity = 0
                    for cc in range(4):
                        bias = self.vec(V_HYB + (l * 2 + o) * 4 + cc)
                        for tt in range(ntile):
                            ts_ = slice(tt * N, (tt + 1) * N)
                            bk = 4 + ity % 2
                            t_, t_b = tz[ity % 2]
                            ity += 1
                            for fc in range(nf):
                                mk.op(pe, lambda e, fc=fc, bk=bk: e.matmul(ps[bk][:, 0:N], yre[:, fc, cc * 128:(cc + 1) * 128],
                                                                         ci[:, fc, ts_], start=(fc == 0), stop=False),
                                      reads=[yre_b, ci_b], writes=[psb[bk]], inc=False)
                                mk.op(pe, lambda e, fc=fc, bk=bk: e.matmul(ps[bk][:, 0:N], ysn[:, fc, cc * 128:(cc + 1) * 128],
                                                                         si[:, fc, ts_], start=False, stop=(fc == nf - 1)),
                                      reads=[ysn_b, si_b], writes=[psb[bk]], inc=(fc == nf - 1))
                            mk.op(dve, lambda e, bk=bk, t_=t_: e.scalar_tensor_tensor(
                                out=t_[:, 0:N], in0=zT[:, cc, ts_], scalar=bias, in1=ps[bk][:, 0:N], op0=ALU.mult, op1=ALU.add),
                                reads=[zT_bb[cc][tt], self.vec_b, psb[bk]], writes=[t_b])
                            if o == 0:
                                mk.op(pool, lambda e, t_=t_: e.tensor_tensor(out=zT[:, cc, ts_], in0=t_[:, 0:N],
                                                                            in1=xT[0][0][:, cc, ts_], op=ALU.mult),
                                      reads=[t_b, xT[0][1]], writes=[zT_bb[cc][tt]])
                            else:
                                mk.op(pool, lambda e, t_=t_: e.tensor_tensor(out=outT[:, cc, ts_], in0=t_[:, 0:N],
                                                                            in1=xT[1][0][:, cc, ts_], op=ALU.mult),
                                      reads=[t_b, xT[1][1]], writes=[outT_b])
                mk.dma(sp, self.catT_d[1536:2048, a:b_].rearrange("(c p) t -> p c t", p=128), outT[:], reads=[outT_b],
                       writes=[B["catT_d"]])
            mk.barrier()


Prog.phase_hy = _phase_hy


def _gemm_resid(self, l, wdram, nkc, actT, actT_b, gate_idx, xsrc, xsrc_b, es, wcols, wb=None, preloaded=False):
    mk = self.mk
    pe, act, dve, pool, sp = mk.pe, mk.act, mk.dve, mk.pool, mk.sp
    B = self.bufs
    ps, psb = self.ps, self.psb
    ntile = D // wcols
    per = wcols // 128
    if wb is None:
        wb = [self.sb(es, "gw%d" % i, [128, nkc, wcols], BF16) for i in range(2)]
    xb = [self.sb(es, "gx%d" % i, [128, T]) for i in range(2 if nkc <= 16 else 1)]
    rb = [self.sb(es, "gr%d" % i, [128, T]) for i in range(2 if nkc <= 16 else 1)]
    tb = [self.sb(es, "gt%d" % i, [128, 512]) for i in range(2)]

    def load_w(i):
        w, w_b = wb[i % 2]
        src = wdram.rearrange("(kc p) n -> p kc n", p=128)[:, :, i * wcols:(i + 1) * wcols]
        mk.dma(pool, w[:], src, writes=[w_b])

    if not preloaded:
        load_w(0)
    it = 0
    for i in range(ntile):
        if i + 1 < ntile:
            load_w(i + 1)
        w, w_b = wb[i % 2]
        for j in range(per):
            oc = i * per + j
            banks = (0, 1, 2) if oc % 2 == 0 else (3, 4, 5)
            x, x_b = xb[oc % len(xb)]
            r, r_b = rb[oc % len(rb)]
            mk.dma(sp, x[:], xsrc[oc * 128:(oc + 1) * 128, :], reads=[xsrc_b], writes=[x_b])
            for k in range(nkc):
                for tt in range(3):
                    mk.op(pe, lambda e, k=k, tt=tt: e.matmul(
                        ps[banks[tt]][:, :], w[:, k, j * 128:(j + 1) * 128], actT[:, k, tt * 512:(tt + 1) * 512],
                        start=(k == 0), stop=(k == nkc - 1)),
                        reads=[w_b, actT_b[k]], writes=[psb[banks[tt]]], inc=(k == nkc - 1))
            for tt in range(3):
                t_, t_b = tb[it % 2]
                it += 1
                cond = 0 if tt < 2 else 1
                ts_ = slice(tt * 512, (tt + 1) * 512)
                mk.op(act, lambda e, tt=tt, t_=t_, cond=cond: e.activation(
                    out=t_[:], in_=ps[banks[tt]][:, :], func=AF.Identity, scale=self.mod(l, gate_idx, oc, cond)),
                    reads=[psb[banks[tt]], self.modb(l, gate_idx)], writes=[t_b])
                mk.op(dve, lambda e, t_=t_, ts_=ts_: e.scalar_tensor_tensor(
                    out=r[:, ts_], in0=x[:, ts_], scalar=ALPHA, in1=t_[:], op0=ALU.mult, op1=ALU.add),
                    reads=[x_b, t_b], writes=[r_b])
            mk.dma(sp, self.rT_d[oc * 128:(oc + 1) * 128, :], r[:], reads=[r_b], writes=[B["rT_d"]])


Prog.gemm_resid = _gemm_resid


def _phase_out(self, l):
    mk = self.mk
    B = self.bufs
    self.ada_flush()
    with ExitStack() as es:
        catT, catT_b = self.sb(es, "o_catT", [128, KC, T], BF16)
        catT_b = self.load_kq(catT, KC, self.catT_d, B["catT_d"])
        xsrc, xsrc_b = (self.xT0, Buf("xT0")) if l == 0 else (self.xT_d, B["xT_d"])
        self.gemm_resid(l, self.w_out[l], KC, catT, catT_b, 2, xsrc, xsrc_b, es, 512)
        mk.barrier()


Prog.phase_out = _phase_out


def _phase_down(self, l, dwb=None):
    mk = self.mk
    B = self.bufs
    with ExitStack() as es:
        aT, aT_b = self.sb(es, "d_actT", [128, FC, T], BF16)
        aT_b = self.load_kq(aT, FC, self.actT_d, B["actT_d"], nq=11)
        self.gemm_resid(l, self.w_down[l], FC, aT, aT_b, 5, self.xT_d, B["xT_d"], es, 128,
                        wb=dwb, preloaded=dwb is not None)
        mk.barrier()


Prog.phase_down = _phase_down


def _phase_ln(self, l, which):
    mk = self.mk
    pe, act, dve, pool, sp = mk.pe, mk.act, mk.dve, mk.pool, mk.sp
    B = self.bufs
    ps, psb = self.ps, self.psb
    final = (which == 1 and l == DEPTH - 1)
    ones_d = self.onesd_bf
    with ExitStack() as es:
        rA, _ = self.sb(es, "n_r", [128, KC, T])
        rA_b = [Buf("n_r%d" % k) for k in range(KC)]
        r16 = [self.sb(es, "n_r16%d" % i, [128, T], BF16) for i in range(2)]
        sq = [self.sb(es, "n_sq%d" % i, [128, T], BF16) for i in range(2)]
        mean, mean_b = self.sb(es, "n_mean", [128, T])
        rstd, rstd_b = self.sb(es, "n_rstd", [128, T])
        AB, AB_b = self.sb(es, "n_AB", [128, 2, KC, 2])
        xo = [self.sb(es, "n_xo%d" % i, [128, T]) for i in range(2)]
        ho = [self.sb(es, "n_ho%d" % i, [128, T], BF16) for i in range(2)]
        for k in range(KC):
            mk.dma(sp, rA[:, k, :], self.rT_d[k * 128:(k + 1) * 128, :], reads=[B["rT_d"]], writes=[rA_b[k]])
        lw = lambda k: self.vec(V_LNP + (l * 4 + which * 2 + 0) * 16 + k)
        lb = lambda k: self.vec(V_LNP + (l * 4 + which * 2 + 1) * 16 + k)
        if not final:
            ml, sh_i, sc_i = (l, 3, 4) if which == 0 else (l + 1, 0, 1)
            lwv = self.vec(V_LNP + (l * 4 + which * 2 + 0) * 16, 16)
            lbv = self.vec(V_LNP + (l * 4 + which * 2 + 1) * 16, 16)
            for cond in range(2):
                sc = self.modT[:, ml, sc_i * 16:(sc_i + 1) * 16, cond]
                sh = self.modT[:, ml, sh_i * 16:(sh_i + 1) * 16, cond]
                mk.op(dve, lambda e, sc=sc, cond=cond: e.tensor_tensor(out=AB[:, 0, :, cond], in0=lwv, in1=sc, op=ALU.mult),
                      reads=[self.vec_b, self.modb(ml, sc_i)], writes=[AB_b])
                mk.op(dve, lambda e, sc=sc, cond=cond: e.tensor_tensor(out=AB[:, 1, :, cond], in0=lbv, in1=sc, op=ALU.mult),
                      reads=[self.vec_b, self.modb(ml, sc_i)], writes=[AB_b])
                mk.op(dve, lambda e, sh=sh, cond=cond: e.tensor_tensor(out=AB[:, 1, :, cond], in0=AB[:, 1, :, cond], in1=sh, op=ALU.add),
                      reads=[AB_b, self.modb(ml, sh_i)], writes=[AB_b])
        for k in range(KC):
            s_, s_b = sq[k % 2]
            c_, c_b = r16[k % 2]
            mk.op(act, lambda e, k=k, s_=s_: e.activation(out=s_[:], in_=rA[:, k, :], func=AF.Square),
                  reads=[rA_b[k]], writes=[s_b])
            mk.op(dve, lambda e, k=k, c_=c_: e.tensor_copy(c_[:], rA[:, k, :]), reads=[rA_b[k]], writes=[c_b])
            for tt in range(3):
                ts_ = slice(tt * 512, (tt + 1) * 512)
                mk.op(pe, lambda e, k=k, tt=tt, ts_=ts_, c_=c_: e.matmul(ps[tt][:, :], ones_d[:], c_[:, ts_],
                                                                       start=(k == 0), stop=(k == KC - 1)),
                      reads=[self.onesd_bf_b, c_b], writes=[psb[tt]], inc=(tt == 2 or k == KC - 1))
            for tt in range(3):
                ts_ = slice(tt * 512, (tt + 1) * 512)
                mk.op(pe, lambda e, k=k, tt=tt, ts_=ts_, s_=s_: e.matmul(ps[3 + tt][:, :], ones_d[:], s_[:, ts_],
                                                                       start=(k == 0), stop=(k == KC - 1)),
                      reads=[self.onesd_bf_b, s_b], writes=[psb[3 + tt]], inc=(tt == 2 or k == KC - 1))
        for tt in range(3):
            ts_ = slice(tt * 512, (tt + 1) * 512)
            mk.op(act, lambda e, tt=tt, ts_=ts_: e.activation(out=mean[:, ts_], in_=ps[tt][:, :], func=AF.Identity),
                  reads=[psb[tt]], writes=[mean_b])
            mk.op(dve, lambda e, ts_=ts_: e.tensor_tensor(out=rstd[:, ts_], in0=mean[:, ts_], in1=mean[:, ts_], op=ALU.mult),
                  reads=[mean_b], writes=[rstd_b])
            mk.op(dve, lambda e, tt=tt, ts_=ts_: e.tensor_tensor(out=rstd[:, ts_], in0=ps[3 + tt][:, :], in1=rstd[:, ts_],
                                                                op=ALU.subtract),
                  reads=[psb[3 + tt], rstd_b], writes=[rstd_b])
        mk.op(act, lambda e: e.activation(out=rstd[:], in_=rstd[:], func=AF.Sqrt, bias=self.vec(V_EPS), scale=1.0),
              reads=[rstd_b, self.vec_b], writes=[rstd_b])
        mk.op(dve, lambda e: e.reciprocal(rstd[:], rstd[:]), reads=[rstd_b], writes=[rstd_b])
        dst, dst_b = (self.yT, B["yT"]) if final else (self.xT_d, B["xT_d"])
        for k in range(KC):
            mk.op(dve, lambda e, k=k: e.tensor_tensor(out=rA[:, k, :], in0=rA[:, k, :], in1=mean[:], op=ALU.subtract),
                  reads=[rA_b[k], mean_b], writes=[rA_b[k]])
            mk.op(dve, lambda e, k=k: e.tensor_tensor(out=rA[:, k, :], in0=rA[:, k, :], in1=rstd[:], op=ALU.mult),
                  reads=[rA_b[k], rstd_b], writes=[rA_b[k]])
            x_, x_b = xo[k % 2]
            mk.op(act, lambda e, k=k, x_=x_: e.activation(out=x_[:], in_=rA[:, k, :], func=AF.Identity,
                                                         scale=lw(k), bias=lb(k)),
                  reads=[rA_b[k], self.vec_b], writes=[x_b])
            mk.dma(sp, dst[k * 128:(k + 1) * 128, :], x_[:], reads=[x_b], writes=[dst_b])
            if not final:
                h_, h_b = ho[k % 2]
                for (s, e_, cond) in ((0, 1024, 0), (1024, 1536, 1)):
                    mk.op(act, lambda e, k=k, s=s, e_=e_, cond=cond, h_=h_: e.activation(
                        out=h_[:, s:e_], in_=rA[:, k, s:e_], func=AF.Identity,
                        scale=AB[:, 0, k, cond:cond + 1], bias=AB[:, 1, k, cond:cond + 1]),
                        reads=[rA_b[k], AB_b], writes=[h_b])
                mk.dma(sp, self.hT_d[k * 128:(k + 1) * 128, :], h_[:], reads=[h_b], writes=[B["hT_d"]])
        mk.barrier()


Prog.phase_ln = _phase_ln


def _phase_up(self, l, after=None):
    mk = self.mk
    pe, act, dve, pool, sp = mk.pe, mk.act, mk.dve, mk.pool, mk.sp
    B = self.bufs
    ps, psb = self.ps, self.psb
    WC = 256
    with ExitStack() as es:
        hT, hT_b = self.sb(es, "u_hT", [128, KC, T], BF16)
        wg = [self.sb(es, "u_wg%d" % i, [128, KC, WC], BF16) for i in range(2)]
        wu = [self.sb(es, "u_wu%d" % i, [128, KC, WC], BF16) for i in range(2)]
        raw = [self.sb(es, "u_raw%d" % i, [128, T]) for i in range(2)]
        go = [self.sb(es, "u_go%d" % i, [128, T]) for i in range(2)]
        uo = [self.sb(es, "u_uo%d" % i, [128, T]) for i in range(2)]
        ab = [self.sb(es, "u_ab%d" % i, [128, T], BF16) for i in range(2)]
        hT_kb = self.load_kq(hT, KC, self.hT_d, B["hT_d"])
        wsrc = self.w_up[l].rearrange("(kc p) n -> p kc n", p=128)
        ntile = DFF // WC

        def load_w(i):
            mk.dma(pool, wg[i % 2][0][:], wsrc[:, :, i * WC:(i + 1) * WC], writes=[wg[i % 2][1]])
            mk.dma(pool, wu[i % 2][0][:], wsrc[:, :, DFF + i * WC:DFF + (i + 1) * WC], writes=[wu[i % 2][1]])

        cnt = {"set": 0, "raw": 0}

        def chunk(w, w_b, col0, tapchunk, o, o_b):
            banks = (0, 1, 2) if cnt["set"] % 2 == 0 else (3, 4, 5)
            cnt["set"] += 1
            for k in range(KC):
                for tt in range(3):
                    mk.op(pe, lambda e, k=k, tt=tt: e.matmul(
                        ps[banks[tt]][:, :], w[:, k, col0:col0 + 128], hT[:, k, tt * 512:(tt + 1) * 512],
                        start=(k == 0), stop=(k == KC - 1)),
                        reads=[w_b, hT_kb[k]], writes=[psb[banks[tt]]], inc=(k == KC - 1))
            r_, r_b = raw[cnt["raw"] % 2]
            cnt["raw"] += 1
            for tt in range(3):
                mk.op(act, lambda e, tt=tt: e.activation(out=r_[:, tt * 512:(tt + 1) * 512], in_=ps[banks[tt]][:, :],
                                                         func=AF.Identity),
                      reads=[psb[banks[tt]]], writes=[r_b])
            self.conv3(r_, r_b, o, o_b, lambda tap: self.vec(V_FCV + (l * 3 + tap) * 88 + tapchunk))

        load_w(0)
        for i in range(ntile):
            if i + 1 < ntile:
                load_w(i + 1)
            for j in range(WC // 128):
                fc = i * (WC // 128) + j
                g_, g_b = go[fc % 2]
                u_, u_b = uo[fc % 2]
                a_, a_b = ab[fc % 2]
                chunk(wg[i % 2][0], wg[i % 2][1], j * 128, fc, g_, g_b)
                chunk(wu[i % 2][0], wu[i % 2][1], j * 128, FC + fc, u_, u_b)
                mk.op(act, lambda e, g_=g_: e.activation(out=g_[:], in_=g_[:], func=AF.Silu), reads=[g_b], writes=[g_b])
                mk.op(dve, lambda e, g_=g_, u_=u_, a_=a_: e.tensor_tensor(out=a_[:], in0=g_[:], in1=u_[:], op=ALU.mult),
                      reads=[g_b, u_b], writes=[a_b])
                mk.dma(sp, self.actT_d[fc * 128:(fc + 1) * 128, :], a_[:], reads=[a_b], writes=[B["actT_d"]])
        if after is not None:
            after()
        mk.barrier()


Prog.phase_up = _phase_up


def _phase_ffn(self, l):
    mk = self.mk
    with ExitStack() as esw:
        dwb = [self.sb(esw, "dw%d" % i, [128, FC, 128], BF16) for i in range(2)]

        def prefetch():
            src = self.w_down[l].rearrange("(kc p) n -> p kc n", p=128)[:, :, 0:128]
            mk.dma(mk.pool, dwb[0][0][:], src, writes=[dwb[0][1]])

        self.phase_up(l, after=prefetch)
        self.phase_down(l, dwb)


Prog.phase_ffn = _phase_ffn
```
